# Optimizing a Trainium2 kernel written in Bass

```python
import math
import jax, jax.numpy as jnp
from jax import lax
import numpy as np

D_MODEL = 4096
BATCH = 2
SEQ = 8192
DEPTH = 2

HG_DK = 128
HG_DV = 128
HG_WIDTH = D_MODEL // 4
HG_HEADS = HG_WIDTH // HG_DV
HG_CHUNK = 64
DA_DH = 128
DA_HEADS = D_MODEL // (4 * DA_DH)
DA_QK_WIDTH = DA_HEADS * 2 * DA_DH
DA_V_WIDTH = DA_HEADS * 2 * DA_DH
DA_QBLOCK = 128
ROPE_THETA = 500000.0
ROT_DIM = DA_DH // 4
SG_CHUNK = 128
SG_WIDTH = D_MODEL // 4
SG_GDIM = 128
SG_GROUPS = SG_WIDTH // SG_GDIM
N_BRANCH = 3
D_FF = 4 * D_MODEL
ALPHA = (2.0 * DEPTH) ** 0.25
BETA = (8.0 * DEPTH) ** -0.25
EPS = 1e-5
IN_SIZES = (HG_WIDTH, HG_WIDTH, HG_WIDTH, HG_WIDTH, HG_WIDTH,
            DA_QK_WIDTH, DA_QK_WIDTH, DA_V_WIDTH, SG_WIDTH, SG_WIDTH)
D_IN = sum(IN_SIZES)

kernel_name = "hybrid_gated_bidir_encoder"


def layer_norm(t, g, b):
    t32 = t.astype(jnp.float32)
    mu = jnp.mean(t32, axis=-1, keepdims=True)
    var = jnp.mean(jnp.square(t32 - mu), axis=-1, keepdims=True)
    return ((t32 - mu) * lax.rsqrt(var + EPS) * g + b).astype(t.dtype)


def rms_norm(t, g):
    t32 = t.astype(jnp.float32)
    ms = jnp.mean(jnp.square(t32), axis=-1, keepdims=True)
    return (t32 * lax.rsqrt(ms + EPS) * g).astype(t.dtype)


def rotary_tables(seq):
    pos = jnp.arange(seq, dtype=jnp.float32)
    freqs = ROPE_THETA ** (-jnp.arange(0, ROT_DIM, 2, dtype=jnp.float32) / ROT_DIM)
    ang = pos[:, None] * freqs[None, :]
    return jnp.cos(ang), jnp.sin(ang)


def partial_rotary(t, cos, sin):
    half = ROT_DIM // 2
    c = cos[None, :, None, None, :].astype(t.dtype)
    s = sin[None, :, None, None, :].astype(t.dtype)
    t1 = t[..., :half]
    t2 = t[..., half:ROT_DIM]
    return jnp.concatenate([t1 * c - t2 * s, t2 * c + t1 * s, t[..., ROT_DIM:]], axis=-1)


def hgrn_lower_bounds(lb_raw):
    p = jax.nn.softmax(lb_raw.astype(jnp.float32), axis=0)
    cum = jnp.cumsum(p, axis=0)
    return cum - cum[:1]


def hgrn_gates(f_pre, lb):
    x32 = f_pre.astype(jnp.float32)
    log_f = jnp.logaddexp(jnp.log(lb), jnp.log1p(-lb) + jax.nn.log_sigmoid(x32))
    k = (1.0 - lb) * jax.nn.sigmoid(-x32)
    return k, log_f


def gla_chunkwise(q, k, v, log_f):
    B, S, H, DK = q.shape
    DV = v.shape[-1]
    nc = S // HG_CHUNK

    def to_chunks(t):
        return t.reshape(B, nc, HG_CHUNK, H, t.shape[-1]).transpose(1, 0, 3, 2, 4)

    lower = jnp.tril(jnp.ones((HG_CHUNK, HG_CHUNK), dtype=bool))

    def step(state, blk):
        qc, kc, vc, gc = blk
        b = jnp.cumsum(gc, axis=2)
        rel = jnp.where(lower[None, None, :, :, None],
                        b[:, :, :, None, :] - b[:, :, None, :, :], -jnp.inf)
        scores = jnp.einsum('bhtd,bhsd,bhtsd->bhts', qc, kc, jnp.exp(rel))
        o = (jnp.einsum('bhts,bhsv->bhtv', scores, vc)
             + jnp.einsum('bhtd,bhdv->bhtv', qc * jnp.exp(b), state))
        b_last = b[:, :, -1:, :]
        state = (jnp.exp(b_last[:, :, 0, :])[..., None] * state
                 + jnp.einsum('bhsd,bhsv->bhdv', kc * jnp.exp(b_last - b), vc))
        return state, o

    init = jnp.zeros((B, H, DK, DV), jnp.float32)
    _, o = lax.scan(step, init, (to_chunks(q), to_chunks(k), to_chunks(v), to_chunks(log_f)))
    return o.transpose(1, 0, 3, 2, 4).reshape(B, S, H, DV)


def hgrn2_mixer(a_q, a_ff, a_fb, a_i, a_g, lb, norm_g):
    B, S, _ = a_q.shape
    heads = lambda t: t.astype(jnp.float32).reshape(B, S, HG_HEADS, -1)
    q = heads(a_q)
    v = heads(a_i)
    k_f, lf_f = hgrn_gates(a_ff, lb[0])
    k_b, lf_b = hgrn_gates(a_fb, lb[1])
    o_fwd = gla_chunkwise(q, heads(k_f), v, heads(lf_f))
    flip = lambda t: jnp.flip(t, axis=1)
    o_bwd = flip(gla_chunkwise(flip(q), flip(heads(k_b)), flip(v), flip(heads(lf_b))))
    o = rms_norm(o_fwd + o_bwd, norm_g.reshape(HG_HEADS, HG_DV))
    o = o.reshape(B, S, HG_WIDTH).astype(a_q.dtype) * jax.nn.silu(a_g)
    return o


def diff_attention(b_q, b_k, b_v, cos, sin, lam_params, norm_g, layer):
    B, S, _ = b_q.shape
    q = partial_rotary(b_q.reshape(B, S, DA_HEADS, 2, DA_DH), cos, sin) * (DA_DH ** -0.5)
    k = partial_rotary(b_k.reshape(B, S, DA_HEADS, 2, DA_DH), cos, sin)
    v = b_v.reshape(B, S, DA_HEADS, 2 * DA_DH)
    lam_init = 0.8 - 0.6 * math.exp(-0.3 * layer)
    lp = lam_params.astype(jnp.float32)
    lam = jnp.exp(jnp.sum(lp[0] * lp[1])) - jnp.exp(jnp.sum(lp[2] * lp[3])) + lam_init
    q_blocks = q.reshape(B, S // DA_QBLOCK, DA_QBLOCK, DA_HEADS, 2, DA_DH).transpose(1, 0, 2, 3, 4, 5)

    def attend(qb):
        s = jnp.einsum('bqhcd,bkhcd->bhcqk', qb, k).astype(jnp.float32)
        p = jax.nn.softmax(s, axis=-1)
        a = p[:, :, 0] - lam * p[:, :, 1]
        return jnp.einsum('bhqk,bkhv->bqhv', a.astype(v.dtype), v)

    o = lax.map(attend, q_blocks)
    o = o.transpose(1, 0, 2, 3, 4).reshape(B, S, DA_HEADS, 2 * DA_DH)
    o = rms_norm(o, norm_g) * (1.0 - lam_init)
    return o.reshape(B, S, DA_V_WIDTH)


def spatial_gating(c_u, c_v, norm_g, norm_b, w_s, b_s):
    B, S, _ = c_u.shape
    u = jax.nn.gelu(c_u)
    v = layer_norm(jax.nn.gelu(c_v), norm_g, norm_b)
    vb = v.reshape(B, S // SG_CHUNK, SG_CHUNK, SG_GROUPS, SG_GDIM)
    mixed = jnp.einsum('gts,bnsgc->bntgc', w_s, vb) + b_s.T[:, :, None]
    return u * mixed.reshape(B, S, SG_WIDTH)


def hybrid_mixer(h, layer, cos, sin, lb, w_in, hg_norm_g, da_lambda, da_norm_g,
                 sg_norm_g, sg_norm_b, sg_w_s, sg_b_s, w_branch_a, w_branch_b,
                 w_branch_c, w_gate, b_gate, w_out):
    B, S, _ = h.shape
    split_idx = []
    acc = 0
    for n in IN_SIZES[:-1]:
        acc += n
        split_idx.append(acc)
    proj = h @ w_in
    a_q, a_ff, a_fb, a_i, a_g, b_q, b_k, b_v, c_u, c_v = jnp.split(proj, split_idx, axis=-1)
    y_a = hgrn2_mixer(a_q, a_ff, a_fb, a_i, a_g, lb, hg_norm_g)
    y_b = diff_attention(b_q, b_k, b_v, cos, sin, da_lambda, da_norm_g, layer)
    y_c = spatial_gating(c_u, c_v, sg_norm_g, sg_norm_b, sg_w_s, sg_b_s)
    gates = jax.nn.sigmoid(h @ w_gate + b_gate).reshape(B, S, N_BRANCH, D_MODEL)
    merged = (gates[:, :, 0] * (y_a @ w_branch_a)
              + gates[:, :, 1] * (y_b @ w_branch_b)
              + gates[:, :, 2] * (y_c @ w_branch_c))
    return merged @ w_out


def setup_inputs(seed: int = 0) -> dict:
    key = jax.random.key(seed)
    ks = jax.random.split(key, 24)
    f32 = jnp.float32

    def normal(k, shape, scale):
        return jax.random.normal(k, shape, f32) * scale

    col_scales = (1.0, 1.0, 1.0, BETA, 1.0, 1.0, 1.0, BETA, 1.0, 1.0)
    col_scale = jnp.concatenate([jnp.full((n,), s, f32) for n, s in zip(IN_SIZES, col_scales)])
    return {
        "x": normal(ks[0], (BATCH, SEQ, D_MODEL), 1.0),
        "w_in": normal(ks[1], (DEPTH, D_MODEL, D_IN), D_MODEL ** -0.5) * col_scale,
        "hg_lb_raw": normal(ks[2], (DEPTH, 2, HG_WIDTH), 0.1),
        "hg_norm_g": 1.0 + normal(ks[3], (DEPTH, HG_WIDTH), 0.05),
        "da_lambda": normal(ks[4], (DEPTH, 4, DA_DH), 0.1),
        "da_norm_g": 1.0 + normal(ks[5], (DEPTH, 2 * DA_DH), 0.05),
        "sg_norm_g": 1.0 + normal(ks[6], (DEPTH, SG_WIDTH), 0.05),
        "sg_norm_b": normal(ks[7], (DEPTH, SG_WIDTH), 0.02),
        "sg_w_s": normal(ks[8], (DEPTH, SG_GROUPS, SG_CHUNK, SG_CHUNK), 0.5 * SG_CHUNK ** -0.5),
        "sg_b_s": 1.0 + normal(ks[9], (DEPTH, SG_GROUPS, SG_CHUNK), 0.05),
        "w_branch_a": normal(ks[10], (DEPTH, HG_WIDTH, D_MODEL), BETA * HG_WIDTH ** -0.5),
        "w_branch_b": normal(ks[11], (DEPTH, DA_V_WIDTH, D_MODEL), BETA * DA_V_WIDTH ** -0.5),
        "w_branch_c": normal(ks[12], (DEPTH, SG_WIDTH, D_MODEL), BETA * SG_WIDTH ** -0.5),
        "w_gate": normal(ks[13], (DEPTH, D_MODEL, N_BRANCH * D_MODEL), D_MODEL ** -0.5),
        "b_gate": normal(ks[14], (DEPTH, N_BRANCH * D_MODEL), 0.02),
        "w_out": normal(ks[15], (DEPTH, D_MODEL, D_MODEL), BETA * D_MODEL ** -0.5),
        "ln1_g": 1.0 + normal(ks[16], (DEPTH, D_MODEL), 0.05),
        "ln1_b": normal(ks[17], (DEPTH, D_MODEL), 0.02),
        "w_up": normal(ks[18], (DEPTH, D_MODEL, D_FF), BETA * D_MODEL ** -0.5),
        "w_down": normal(ks[19], (DEPTH, D_FF, D_MODEL), BETA * D_FF ** -0.5),
        "ln2_g": 1.0 + normal(ks[20], (DEPTH, D_MODEL), 0.05),
        "ln2_b": normal(ks[21], (DEPTH, D_MODEL), 0.02),
    }


def reference(x, w_in, hg_lb_raw, hg_norm_g, da_lambda, da_norm_g, sg_norm_g, sg_norm_b,
              sg_w_s, sg_b_s, w_branch_a, w_branch_b, w_branch_c, w_gate, b_gate, w_out,
              ln1_g, ln1_b, w_up, w_down, ln2_g, ln2_b):
    cos, sin = rotary_tables(x.shape[1])
    lb_all = hgrn_lower_bounds(hg_lb_raw)
    for layer in range(DEPTH):
        mix = hybrid_mixer(x, layer, cos, sin, lb_all[layer], w_in[layer], hg_norm_g[layer],
                           da_lambda[layer], da_norm_g[layer], sg_norm_g[layer], sg_norm_b[layer],
                           sg_w_s[layer], sg_b_s[layer], w_branch_a[layer], w_branch_b[layer],
                           w_branch_c[layer], w_gate[layer], b_gate[layer], w_out[layer])
        x = layer_norm(ALPHA * x + mix, ln1_g[layer], ln1_b[layer])
        hid = jnp.square(jax.nn.relu(x @ w_up[layer]))
        x = layer_norm(ALPHA * x + hid @ w_down[layer], ln2_g[layer], ln2_b[layer])
    return x
```

```python
import math
import numpy as np
import concourse.bass as bass
import concourse.mybir as mybir
from concourse.bass_utils import run_bass_kernel_spmd

F32 = mybir.dt.float32
BF16 = mybir.dt.bfloat16
AF = mybir.ActivationFunctionType
ALU = mybir.AluOpType

EPS = 1e-5
ROPE_THETA = 500000.0


def make_cfg(D=4096, S=8192, DEPTH=2, B=2):
    c = dict(D=D, S=S, DEPTH=DEPTH, B=B)
    c["KC"] = D // 128
    c["HGW"] = D // 4
    c["H"] = c["HGW"] // 128
    c["DFF"] = 4 * D
    c["DIN"] = 5 * c["HGW"] + 3 * (2 * c["H"] * 128) + 2 * c["HGW"]
    c["ALPHA"] = (2.0 * DEPTH) ** 0.25
    return c


class Buf:
    __slots__ = ("w", "r", "name")

    def __init__(self, name=""):
        self.w = {}
        self.r = {}
        self.name = name


class Sch:
    SEM_CAP = 30000

    def __init__(self, nc):
        self.nc = nc
        self.eng = {"pe": nc.tensor, "act": nc.scalar, "dve": nc.vector, "pool": nc.gpsimd, "sp": nc.sync}
        self.cur = {}
        self.seen = {e: {} for e in self.eng}
        self.nsem = 0
        self.all_tokens = {}
        self.snap = {}
        self.dma_ring = []
        self.dma_next = 0
        self.NDMA = 12
        for e in ("pe", "act", "dve", "pool"):
            self.cur[e] = [self._newsem(), 0]
        for i in range(self.NDMA):
            self.dma_ring.append([self._newsem(), 0])

    def _newsem(self):
        self.nsem += 1
        return self.nc.alloc_semaphore(f"sm{self.nsem}")

    def _wait(self, e, tok):
        sem, val = tok
        k = id(sem)
        se = self.seen[e]
        if se.get(k, 0) >= val:
            return
        self.eng[e].wait_ge(sem, val)
        se[k] = val
        sn = self.snap.get((k, val))
        if sn:
            for k2, v2 in sn.items():
                if se.get(k2, 0) < v2:
                    se[k2] = v2

    def _deps(self, e, rd, wr):
        toks = {}
        for b in rd:
            for k, t in b.w.items():
                if toks.get(k, (None, 0))[1] < t[1]:
                    toks[k] = t
        for b in wr:
            for d in (b.w, b.r):
                for k, t in d.items():
                    if toks.get(k, (None, 0))[1] < t[1]:
                        toks[k] = t
        for t in toks.values():
            self._wait(e, t)

    def _mark(self, tok, rd, wr, e=None):
        k = id(tok[0])
        if e is not None:
            self.snap[(k, tok[1])] = dict(self.seen[e])
        self.all_tokens[k] = tok
        for b in wr:
            b.w[k] = tok
            b.r = {}
        for b in rd:
            b.r[k] = tok

    def op(self, e, fn, rd=(), wr=()):
        self._deps(e, rd, wr)
        cur = self.cur[e]
        if cur[1] >= self.SEM_CAP:
            cur[0] = self._newsem()
            cur[1] = 0
        cur[1] += 1
        tok = (cur[0], cur[1])
        fn(self.eng[e]).then_inc(cur[0], 1)
        self._mark(tok, rd, wr, e)
        return tok

    def pe_group(self, fns, rd=(), wr=()):
        self._deps("pe", rd, wr)
        cur = self.cur["pe"]
        if cur[1] >= self.SEM_CAP:
            cur[0] = self._newsem()
            cur[1] = 0
        ins = None
        for fn in fns:
            ins = fn(self.nc.tensor)
        cur[1] += 1
        tok = (cur[0], cur[1])
        ins.then_inc(cur[0], 1)
        self._mark(tok, rd, wr, "pe")
        return tok

    def dma(self, out, in_, rd=(), wr=(), e="sp"):
        self._deps(e, rd, wr)
        slot = self.dma_ring[self.dma_next]
        self.dma_next = (self.dma_next + 1) % self.NDMA
        if slot[1] + 16 > self.SEM_CAP:
            slot[0] = self._newsem()
            slot[1] = 0
        if slot[1] > 0:
            self._wait(e, (slot[0], slot[1]))
        slot[1] += 16
        tok = (slot[0], slot[1])
        self.eng[e].dma_start(out=out, in_=in_).then_inc(slot[0], 16)
        self._mark(tok, rd, wr, e)
        return tok

    def barrier(self):
        toks = list(self.all_tokens.values())
        for e in self.eng:
            for t in toks:
                self._wait(e, t)


def build_program(cfg, debug_outputs=()):
    D, S, DEPTH, KC, H, DFF, DIN = (cfg[k] for k in ("D", "S", "DEPTH", "KC", "H", "DFF", "DIN"))
    T = S
    ALPHA = cfg["ALPHA"]
    HGW = cfg["HGW"]
    NPROJ = DIN // 128
    NGATE = 3 * KC
    FC = DFF // 128
    TT = 512 if T >= 512 else T
    NT = T // TT
    TM = 512 if T >= 512 else T
    NTM = T // TM
    NCH = T // 64
    NB128 = T // 128

    nc = bass.Bass("TRN2", target_bir_lowering=False)
    S_ = Sch(nc)

    def din(name, shape, dt=F32):
        return nc.dram_tensor(name, list(shape), dt, kind="ExternalInput").ap()

    def dscr(name, shape, dt=F32):
        return nc.dram_tensor(name, list(shape), dt).ap()

    xT_in = din("xT", [KC, 128, T])
    w_in_f = din("w_in_t", [DEPTH, NPROJ, 128, KC * 128])
    w_gate_f = din("w_gate_t", [DEPTH, NGATE, 128, KC * 128])
    w_ba_f = din("w_ba_t", [DEPTH, KC, 128, H * 128])
    w_bb_f = din("w_bb_t", [DEPTH, KC, 128, 2 * H * 128])
    w_bc_f = din("w_bc_t", [DEPTH, KC, 128, H * 128])
    w_out_f = din("w_out_t", [DEPTH, KC, 128, KC * 128])
    w_up_f = din("w_up_t", [DEPTH, FC, 128, KC * 128])
    w_down_f = din("w_down_t", [DEPTH, KC, 128, FC * 128])
    b_gate_in = din("b_gate_p", [DEPTH, 128, NGATE])
    ln1g_in = din("ln1_g_p", [DEPTH, 128, KC]); ln1b_in = din("ln1_b_p", [DEPTH, 128, KC])
    ln2g_in = din("ln2_g_p", [DEPTH, 128, KC]); ln2b_in = din("ln2_b_p", [DEPTH, 128, KC])
    lbraw_in = din("lbraw_p", [128, DEPTH * 2 * H])
    hgg_in = din("hgg_p", [DEPTH, 128, H])
    dal_in = din("dal_p", [DEPTH, 1, 512])
    dag_in = din("dag_p", [DEPTH, 128, 2])
    sgg_in = din("sgg_b", [DEPTH, 128, HGW]); sgb_in = din("sgb_b", [DEPTH, 128, HGW])
    sgw_in = din("sgw_t", [DEPTH, 128, H * 128])
    sgbs_in = din("sgbs_b", [DEPTH, 128, H * 128])
    cos_in = din("cosF", [32, T]); sin_in = din("sinF", [32, T])
    rot_in = din("rotm", [32, 32])
    ident_in = din("ident", [128, 128])
    trif_in = din("trif", [64, 64]); trib_in = din("trib", [64, 64])

    yT_out = nc.dram_tensor("yT", [KC, 128, T], F32, kind="ExternalOutput").ap()

    Wb = {}
    for nm, src in (("in", w_in_f), ("gate", w_gate_f), ("ba", w_ba_f), ("bb", w_bb_f), ("bc", w_bc_f),
                    ("out", w_out_f), ("up", w_up_f), ("down", w_down_f)):
        Wb[nm] = [dscr(f"wb_{nm}{l_}", src.shape[1:], BF16) for l_ in range(DEPTH)]
    XT = dscr("XT", [KC, 128, T])
    HT = dscr("HT", [KC, 128, T], BF16)
    PJA = dscr("PJA", [5 * H, 128, T]); PJB = dscr("PJB", [6 * H, 128, T]); PJC = dscr("PJC", [2 * H, 128, T])

    class _Proj:
        def __getitem__(self, idx):
            c = idx[0]
            c0 = c.start if isinstance(c, slice) else c
            if c0 < 5 * H:
                t_, off = PJA, 0
            elif c0 < 11 * H:
                t_, off = PJB, 5 * H
            else:
                t_, off = PJC, 11 * H
            if isinstance(c, slice):
                return t_[(slice(c.start - off, c.stop - off),) + tuple(idx[1:])]
            return t_[(c - off,) + tuple(idx[1:])]
    PROJ = _Proj()
    GATE = dscr("GATE", [NGATE, 128, T], BF16)
    YT = dscr("YT", [KC, 128, T], BF16)
    AQM = dscr("AQM", [2, H, 128, T], BF16); AKM = dscr("AKM", [2, H, 128, T], BF16)
    AQH = dscr("AQH", [2, H, 128, T], BF16); AKH = dscr("AKH", [2, H, 128, T], BF16)
    AV = dscr("AV", [H, 128, T], BF16)
    ADEC = dscr("ADEC", [2, H, 128, NCH])
    OA = dscr("OA", [2, H, T, 128])
    BQK = dscr("BQK", [4, 128, T], BF16)

    o_aq, o_aff, o_afb, o_ai, o_ag = 0, H, 2 * H, 3 * H, 4 * H
    o_bq, o_bk, o_bv = 5 * H, 7 * H, 9 * H
    o_cu, o_cv = 11 * H, 12 * H

    from contextlib import ExitStack
    ES = ExitStack()

    uid = [0]

    def uname(name):
        uid[0] += 1
        return f"sb{uid[0]}_{name}"

    def sb(name, shape, dt=F32):
        return ES.enter_context(nc.sbuf_tensor(uname(name), list(shape), dt))

    ident_f = sb("ident_f", [128, 128]); ident_b = sb("ident_b", [128, 128], BF16)
    trif = sb("trif", [64, 64]); trib = sb("trib", [64, 64])
    rotm = sb("rotm", [32, 32])
    ones_f = sb("ones_f", [128, 128])
    ones_b = sb("ones_b", [128, 128], BF16)
    lbt = sb("lbt", [128, DEPTH * 2 * H]); omlt = sb("omlt", [128, DEPTH * 2 * H])
    B_const = Buf("const")

    psum = [ES.enter_context(nc.psum_tensor(f"ps{i}", [128, 512], F32)) for i in range(8)]
    PB = [Buf(f"ps{i}") for i in range(8)]

    S_.dma(ident_f[:], ident_in, wr=[B_const])
    S_.dma(trif[:], trif_in, wr=[B_const])
    S_.dma(trib[:], trib_in, wr=[B_const])
    S_.dma(rotm[:], rot_in, wr=[B_const])
    S_.dma(lbt[:], lbraw_in, wr=[B_const])
    S_.barrier()
    S_.op("dve", lambda e: e.tensor_copy(out=ident_b[:], in_=ident_f[:]), rd=[B_const], wr=[B_const])
    S_.op("dve", lambda e: e.memset(ones_f[:], 1.0), wr=[B_const])
    S_.op("dve", lambda e: e.memset(ones_b[:], 1.0), wr=[B_const])
    NL = 2 * H
    S_.op("act", lambda e: e.activation(out=lbt[:], in_=lbt[:], func=AF.Exp), rd=[B_const], wr=[B_const])
    zsum = sb("zsum", [128, NL]); zrec = sb("zrec", [128, NL])
    S_.op("dve", lambda e: e.tensor_copy(out=zsum[:], in_=lbt[:, 0:NL]), rd=[B_const], wr=[B_const])
    for l in range(1, DEPTH):
        S_.op("dve", lambda e, l=l: e.tensor_tensor(out=zsum[:], in0=zsum[:], in1=lbt[:, l * NL:(l + 1) * NL], op=ALU.add),
              rd=[B_const], wr=[B_const])
    S_.op("dve", lambda e: e.reciprocal(out=zrec[:], in_=zsum[:]), rd=[B_const], wr=[B_const])
    S_.op("dve", lambda e: e.memset(lbt[:, 0:NL], 0.0), rd=[B_const], wr=[B_const])
    for l in range(2, DEPTH):
        S_.op("dve", lambda e, l=l: e.tensor_tensor(out=lbt[:, l * NL:(l + 1) * NL], in0=lbt[:, l * NL:(l + 1) * NL],
                                                    in1=lbt[:, (l - 1) * NL:l * NL], op=ALU.add), rd=[B_const], wr=[B_const])
    for l in range(DEPTH):
        S_.op("dve", lambda e, l=l: e.tensor_tensor(out=lbt[:, l * NL:(l + 1) * NL], in0=lbt[:, l * NL:(l + 1) * NL],
                                                    in1=zrec[:], op=ALU.mult), rd=[B_const], wr=[B_const])
    S_.op("dve", lambda e: e.tensor_scalar(out=omlt[:], in0=lbt[:], scalar1=-1.0, scalar2=1.0, op0=ALU.mult, op1=ALU.add),
          rd=[B_const], wr=[B_const])
    S_.barrier()

    def stage_sb():
        es = ExitStack()
        def f(name, shape, dt=F32):
            return es.enter_context(nc.sbuf_tensor(uname(name), list(shape), dt))
        return es, f

    ENG3 = ("dve", "pool", "act")

    def rsqrt(ap, b):
        S_.op("act", lambda e: e.activation(out=ap, in_=ap, func=AF.Sqrt), rd=[b], wr=[b])
        S_.op("dve", lambda e: e.reciprocal(out=ap, in_=ap), rd=[b], wr=[b])

    def copy_op(en, out, in_, rd, wr):
        if en == "act":
            return S_.op("act", lambda e: e.activation(out=out, in_=in_, func=AF.Copy), rd=rd, wr=wr)
        return S_.op(en, lambda e: e.tensor_copy(out=out, in_=in_), rd=rd, wr=wr)

    def stage_cast(src, dst):
        es, f = stage_sb()
        s2 = src
        d2 = dst
        n, _, fsz = s2.shape
        CH = min(4096, fsz)
        NBUF = 3
        tin = [f(f"ci{i}", [128, CH]) for i in range(NBUF)]
        tout = [f(f"co{i}", [128, CH], BF16) for i in range(NBUF)]
        bi = [Buf() for _ in range(NBUF)]; bo = [Buf() for _ in range(NBUF)]
        k = 0
        for i in range(n):
            for c in range(fsz // CH):
                j = k % NBUF
                S_.dma(tin[j][:], s2[i, :, c * CH:(c + 1) * CH], wr=[bi[j]])
                copy_op(ENG3[k % 3], tout[j][:], tin[j][:], [bi[j]], [bo[j]])
                S_.dma(d2[i, :, c * CH:(c + 1) * CH], tout[j][:], rd=[bo[j]])
                k += 1
        S_.barrier()
        es.close()

    def stage_x0():
        es, f = stage_sb()
        CH = min(2048, T)
        tin = [f(f"xi{i}", [128, CH]) for i in range(2)]
        tout = [f(f"xo{i}", [128, CH], BF16) for i in range(2)]
        bi = [Buf(), Buf()]; bo = [Buf(), Buf()]
        k = 0
        for kc in range(KC):
            for c in range(T // CH):
                j = k % 2
                S_.dma(tin[j][:], xT_in[kc, :, c * CH:(c + 1) * CH], wr=[bi[j]])
                copy_op(ENG3[k % 2], tout[j][:], tin[j][:], [bi[j]], [bo[j]])
                S_.dma(XT[kc, :, c * CH:(c + 1) * CH], tin[j][:], rd=[bi[j]])
                S_.dma(HT[kc, :, c * CH:(c + 1) * CH], tout[j][:], rd=[bo[j]])
                k += 1
        S_.barrier()
        es.close()

    class WRing:
        def __init__(self, f, kcw, nbuf=4, tag="w"):
            self.t = [f(f"{tag}{i}", [128, kcw * 128], BF16) for i in range(nbuf)]
            self.b = [Buf() for _ in range(nbuf)]
            self.k = 0
            self.n = nbuf

        def load(self, src):
            j = self.k % self.n
            self.k += 1
            S_.dma(self.t[j][:, 0:src.shape[-1]], src, wr=[self.b[j]])
            return self.t[j], self.b[j]

    psk = [0]

    ps_ring = [8]

    def next_ps(n=1):
        j = psk[0] % ps_ring[0]
        psk[0] += 1
        return psum[j], PB[j]

    def gemm_tile(wring, wsrcs, act_fn, act_bufs, kcs, N, epi):
        for m, src in enumerate(wsrcs):
            wt, wb = wring.load(src)
            ps, pb = next_ps()
            fns = []
            for k in range(kcs):
                fns.append(lambda e, k=k, wt=wt, ps=ps: e.matmul(ps[:, 0:N], wt[:, k * 128:(k + 1) * 128], act_fn(k),
                                                                 start=(k == 0), stop=(k == kcs - 1)))
            S_.pe_group(fns, rd=[wb] + list(act_bufs), wr=[pb])
            epi(m, ps, pb)

    def stage_proj(l):
        es, f = stage_sb()
        ht = [f(f"ht{i}", [128, KC, TT], BF16) for i in range(2)]
        hb = [Buf(), Buf()]
        bg = f("bg", [128, NGATE])
        bgb = Buf()
        S_.dma(bg[:], b_gate_in[l], wr=[bgb])
        wr_ = WRing(f, KC, 4)
        ot = [f(f"ot{i}", [128, TT]) for i in range(4)]
        otb = [f(f"otb{i}", [128, TT], BF16) for i in range(4)]
        ob = [Buf() for _ in range(4)]
        cnt = [0]
        for tt in range(NT):
            j = tt % 2
            tsl = slice(tt * TT, (tt + 1) * TT)
            S_.dma(ht[j][:], HT[:, :, tsl].rearrange("k p t -> p k t"), wr=[hb[j]])

            def epi_proj(m, ps, pb, tsl=tsl):
                i = cnt[0] % 4
                cnt[0] += 1
                copy_op(("dve", "act")[cnt[0] % 2], ot[i][:], ps[:, 0:TT], [pb], [ob[i]])
                S_.dma(PROJ[m, :, tsl], ot[i][:], rd=[ob[i]])

            def epi_gate(m, ps, pb, tsl=tsl):
                i = cnt[0] % 4
                cnt[0] += 1
                S_.op("act", lambda e: e.activation(out=otb[i][:], in_=ps[:, 0:TT], func=AF.Sigmoid, bias=bg[:, m:m + 1], scale=1.0),
                      rd=[pb, bgb], wr=[ob[i]])
                S_.dma(GATE[m, :, tsl], otb[i][:], rd=[ob[i]])

            act = lambda k, j=j: ht[j][:, k, :]
            gemm_tile(wr_, [Wb["in"][l][m] for m in range(NPROJ)], act, [hb[j]], KC, TT, epi_proj)
            gemm_tile(wr_, [Wb["gate"][l][m] for m in range(NGATE)], act, [hb[j]], KC, TT, epi_gate)
        S_.barrier()
        es.close()

    def stage_a_prep(l):
        es, f = stage_sb()
        TS = min(2048, T)
        NCS = TS // 64
        q = f("aq", [128, TS]); a = f("aa", [128, TS]); kk = f("akk", [128, TS])
        pe_ = f("ape", [128, TS + 64]); onesr = f("aones", [128, TS])
        arg = [f(f"aarg{i}", [128, TS]) for i in range(2)]
        ex = [f(f"aex{i}", [128, TS]) for i in range(2)]
        outb = [f(f"aob{i}", [128, TS], BF16) for i in range(2)]
        dec = f("adec", [128, NCS]); dec2 = f("adec2", [128, NCS])
        bq, ba, bk, bpe, bones, bdec = Buf(), Buf(), Buf(), Buf(), Buf(), Buf()
        barg = [Buf(), Buf()]; bex = [Buf(), Buf()]; bob = [Buf(), Buf()]
        S_.op("pool", lambda e: e.memset(onesr[:], 1.0), wr=[bones])
        S_.op("pool", lambda e: e.memset(pe_[:], 0.0), wr=[bpe])
        v3 = lambda ap: ap.rearrange("p (c t) -> p c t", t=64)
        cnt = [0]
        for h in range(H):
            for sg in range(T // TS):
                tsl = slice(sg * TS, (sg + 1) * TS)
                S_.dma(q[:], PROJ[o_aq + h, :, tsl], wr=[bq])
                S_.dma(a[:], PROJ[o_ai + h, :, tsl], wr=[ba])
                i0 = cnt[0] % 2; cnt[0] += 1
                S_.op("pool", lambda e, i0=i0: e.tensor_copy(out=outb[i0][:], in_=a[:]), rd=[ba], wr=[bob[i0]])
                S_.dma(AV[h, :, tsl], outb[i0][:], rd=[bob[i0]])
                for dr in range(2):
                    col = l * 2 * H + dr * H + h
                    S_.dma(a[:], PROJ[(o_aff if dr == 0 else o_afb) + h, :, tsl], wr=[ba])
                    S_.op("act", lambda e: e.activation(out=a[:], in_=a[:], func=AF.Sigmoid), rd=[ba], wr=[ba])
                    S_.op("dve", lambda e, col=col: e.tensor_scalar(out=a[:], in0=a[:], scalar1=omlt[:, col:col + 1], scalar2=lbt[:, col:col + 1],
                                                                    op0=ALU.mult, op1=ALU.add), rd=[ba, B_const], wr=[ba])
                    S_.op("pool", lambda e: e.tensor_scalar(out=kk[:], in0=a[:], scalar1=-1.0, scalar2=1.0, op0=ALU.mult, op1=ALU.add),
                          rd=[ba], wr=[bk])
                    S_.op("act", lambda e: e.activation(out=a[:], in_=a[:], func=AF.Ln), rd=[ba], wr=[ba])
                    S_.op("dve", lambda e: e.tensor_tensor_scan(out=pe_[:, 1:TS + 1], data0=onesr[:], data1=a[:], initial=0.0,
                                                                op0=ALU.mult, op1=ALU.add), rd=[ba, bones], wr=[bpe])
                    PF = v3(pe_[:, 1:TS + 1]) if dr == 0 else v3(pe_[:, 0:TS])
                    base = v3(pe_[:, 0:TS])
                    Rmid = base[:, :, 32:33]; Rcs = base[:, :, 0:1]; Rce = v3(pe_[:, 64:TS + 64])[:, :, 0:1]
                    bc = lambda r: r.broadcast_to([128, NCS, 64])
                    if dr == 0:
                        specs = [(PF, bc(Rmid), q, AQM), (bc(Rmid), PF, kk, AKM), (PF, bc(Rcs), q, AQH), (bc(Rce), PF, kk, AKH)]
                    else:
                        specs = [(bc(Rmid), PF, q, AQM), (PF, bc(Rmid), kk, AKM), (bc(Rce), PF, q, AQH), (PF, bc(Rcs), kk, AKH)]
                    for (x0, x1, mul, dst) in specs:
                        i = cnt[0] % 2; cnt[0] += 1
                        S_.op("dve", lambda e, i=i, x0=x0, x1=x1: e.tensor_tensor(out=v3(arg[i][:]), in0=x0, in1=x1, op=ALU.subtract),
                              rd=[bpe], wr=[barg[i]])
                        S_.op("act", lambda e, i=i: e.activation(out=ex[i][:], in_=arg[i][:], func=AF.Exp), rd=[barg[i]], wr=[bex[i]])
                        mb = bq if mul is q else bk
                        S_.op("pool", lambda e, i=i, mul=mul: e.tensor_tensor(out=outb[i][:], in0=ex[i][:], in1=mul[:], op=ALU.mult),
                              rd=[bex[i], mb], wr=[bob[i]])
                        S_.dma(dst[dr, h, :, tsl], outb[i][:], rd=[bob[i]])
                    S_.op("dve", lambda e: e.tensor_tensor(out=dec[:].unsqueeze(2), in0=Rce, in1=Rcs, op=ALU.subtract), rd=[bpe], wr=[bdec])
                    S_.op("act", lambda e: e.activation(out=dec2[:], in_=dec[:], func=AF.Exp), rd=[bdec], wr=[bdec])
                    S_.dma(ADEC[dr, h, :, sg * NCS:(sg + 1) * NCS], dec2[:], rd=[bdec])
        S_.barrier()
        es.close()

    def stage_a_rec(l):
        es, f = stage_sb()
        GT = min(512, T)
        GC = GT // 64
        NG = T // GT
        names = ("qm", "km", "qh", "kh", "v")
        tl = [[[f(f"r{n}{d}{i}", [128, GT], BF16) for n in names] for i in range(2)] for d in range(2)]
        tb = [[Buf() for i in range(2)] for d in range(2)]
        dec = [f(f"rdec{d}", [128, NCH]) for d in range(2)]; bdec = [Buf(), Buf()]
        st = [f(f"rst{d}", [128, 128]) for d in range(2)]; stb = [f(f"rstb{d}", [128, 128], BF16) for d in range(2)]
        bst = [Buf(), Buf()]; bstb = [Buf(), Buf()]
        khv = [[f(f"rkhv{d}{i}", [64, 256], BF16) for i in range(2)] for d in range(2)]
        bkhv = [[Buf(), Buf()] for d in range(2)]
        pT = [[f(f"rpT{d}{i}", [64, 64], BF16) for i in range(2)] for d in range(2)]
        bpT = [[Buf(), Buf()] for d in range(2)]
        ost = [[f(f"rost{d}{i}", [64, GC, 128]) for i in range(2)] for d in range(2)]
        bost = [[Buf(), Buf()] for d in range(2)]
        srcs = (AQM, AKM, AQH, AKH)
        for h in range(H):
            for d in range(2):
                S_.dma(dec[d][:], ADEC[d, h], wr=[bdec[d]])
                S_.op("dve", lambda e, d=d: e.memset(st[d][:], 0.0), wr=[bst[d]])
                S_.op("pool", lambda e, d=d: e.memset(stb[d][:], 0.0), wr=[bstb[d]])
            for gi in range(NG):
                for d in range(2):
                    g = gi if d == 0 else NG - 1 - gi
                    j = gi % 2
                    tsl = slice(g * GT, (g + 1) * GT)
                    for n in range(4):
                        S_.dma(tl[d][j][n][:], srcs[n][d, h, :, tsl], wr=[tb[d][j]])
                    S_.dma(tl[d][j][4][:], AV[h, :, tsl], wr=[tb[d][j]])
                for ci in range(GC):
                    for d in range(2):
                        g = gi if d == 0 else NG - 1 - gi
                        j = gi % 2
                        cl = ci if d == 0 else GC - 1 - ci
                        c = g * GC + cl
                        lo = cl * 64
                        qm, km, qh, kh, vv = tl[d][j]
                        i2 = (gi * GC + ci) % 2
                        ps1, pb1 = next_ps()
                        S_.pe_group([lambda e: e.matmul(ps1[0:64, 0:128], kh[:, lo:lo + 64], ident_b[:], start=True, stop=True),
                                     lambda e: e.matmul(ps1[0:64, 128:256], vv[:, lo:lo + 64], ident_b[:], start=True, stop=True)],
                                    rd=[tb[d][j], B_const], wr=[pb1])
                        S_.op("act", lambda e: e.activation(out=khv[d][i2][:], in_=ps1[0:64, 0:256], func=AF.Copy), rd=[pb1], wr=[bkhv[d][i2]])
                        ps2, pb2 = next_ps()
                        S_.pe_group([lambda e: e.matmul(ps2[0:64, 0:64], km[:, lo:lo + 64], qm[:, lo:lo + 64], start=True, stop=True)],
                                    rd=[tb[d][j]], wr=[pb2])
                        msk = trif if d == 0 else trib
                        S_.op("dve", lambda e: e.tensor_tensor(out=pT[d][i2][:], in0=ps2[0:64, 0:64], in1=msk[:], op=ALU.mult),
                              rd=[pb2, B_const], wr=[bpT[d][i2]])
                        ps3, pb3 = next_ps()
                        S_.pe_group([lambda e: e.matmul(ps3[0:64, 0:128], pT[d][i2][:], khv[d][i2][:, 128:256], start=True, stop=False),
                                     lambda e: e.matmul(ps3[0:64, 0:128], qh[:, lo:lo + 64], stb[d][:], start=False, stop=True)],
                                    rd=[bpT[d][i2], bkhv[d][i2], tb[d][j], bstb[d]], wr=[pb3])
                        S_.op("act", lambda e: e.activation(out=ost[d][j][:, cl, :], in_=ps3[0:64, 0:128], func=AF.Copy), rd=[pb3], wr=[bost[d][j]])
                        ps4, pb4 = next_ps()
                        S_.pe_group([lambda e: e.matmul(ps4[:, 0:128], khv[d][i2][:, 0:128], khv[d][i2][:, 128:256], start=True, stop=True)],
                                    rd=[bkhv[d][i2]], wr=[pb4])
                        S_.op("dve", lambda e: e.scalar_tensor_tensor(out=st[d][:], in0=st[d][:], scalar=dec[d][:, c:c + 1], in1=ps4[:, 0:128],
                                                                       op0=ALU.mult, op1=ALU.add), rd=[pb4, bdec[d], bst[d]], wr=[bst[d]])
                        S_.op("pool", lambda e: e.tensor_copy(out=stb[d][:], in_=st[d][:]), rd=[bst[d]], wr=[bstb[d]])
                for d in range(2):
                    g = gi if d == 0 else NG - 1 - gi
                    j = gi % 2
                    S_.dma(OA[d, h, g * GT:(g + 1) * GT, :].rearrange("(c t) v -> t c v", t=64), ost[d][j][:], rd=[bost[d][j]])
        S_.barrier()
        es.close()

    def stage_a_norm(l):
        es, f = stage_sb()
        NB = TT // 128
        of = [f(f"nof{i}", [128, NB, 128]) for i in range(2)]; ob_ = [f(f"nob{i}", [128, NB, 128]) for i in range(2)]
        sq = f("nsq", [128, NB, 128]); ss = f("nss", [128, NB]); on = [f(f"non{i}", [128, NB, 128]) for i in range(2)]
        sg_ = [f(f"nsg{i}", [128, TT]) for i in range(2)]; yo = [f(f"nyo{i}", [128, TT], BF16) for i in range(2)]
        hg = f("nhg", [128, H])
        bof = [Buf(), Buf()]; bsq, bss = Buf(), Buf(); bon = [Buf(), Buf()]; bsg = [Buf(), Buf()]; byo = [Buf(), Buf()]; bhg = Buf()
        S_.dma(hg[:], hgg_in[l], wr=[bhg])
        k = 0
        for h in range(H):
            for tt in range(NT):
                j = k % 2; k += 1
                tsl = slice(tt * TT, (tt + 1) * TT)
                S_.dma(of[j][:], OA[0, h, tsl, :].rearrange("(b p) v -> p b v", p=128), wr=[bof[j]])
                S_.dma(ob_[j][:], OA[1, h, tsl, :].rearrange("(b p) v -> p b v", p=128), wr=[bof[j]])
                S_.dma(sg_[j][:], PROJ[o_ag + h, :, tsl], wr=[bsg[j]])
                S_.op("act", lambda e: e.activation(out=sg_[j][:], in_=sg_[j][:], func=AF.Silu), rd=[bsg[j]], wr=[bsg[j]])
                S_.op("pool", lambda e: e.tensor_tensor(out=of[j][:], in0=of[j][:], in1=ob_[j][:], op=ALU.add), rd=[bof[j]], wr=[bof[j]])
                S_.op("pool", lambda e: e.tensor_tensor(out=sq[:], in0=of[j][:], in1=of[j][:], op=ALU.mult), rd=[bof[j]], wr=[bsq])
                S_.op("dve", lambda e: e.tensor_reduce(out=ss[:], in_=sq[:], axis=mybir.AxisListType.X, op=ALU.add), rd=[bsq], wr=[bss])
                S_.op("dve", lambda e: e.tensor_scalar(out=ss[:], in0=ss[:], scalar1=1.0 / 128, scalar2=EPS, op0=ALU.mult, op1=ALU.add), rd=[bss], wr=[bss])
                rsqrt(ss[:], bss)
                S_.op("dve", lambda e: e.tensor_tensor(out=on[j][:], in0=of[j][:], in1=ss[:].unsqueeze(2).broadcast_to([128, NB, 128]), op=ALU.mult),
                      rd=[bof[j], bss], wr=[bon[j]])
                ps, pb = next_ps()
                S_.pe_group([(lambda e, b=b: e.matmul(ps[:, b * 128:(b + 1) * 128], on[j][:, b, :], ident_f[:], start=True, stop=True)) for b in range(NB)],
                            rd=[bon[j], B_const], wr=[pb])
                S_.op("dve", lambda e: e.scalar_tensor_tensor(out=yo[j][:], in0=ps[:, 0:TT], scalar=hg[:, h:h + 1], in1=sg_[j][:], op0=ALU.mult, op1=ALU.mult),
                      rd=[pb, bhg, bsg[j]], wr=[byo[j]])
                S_.dma(YT[h, :, tsl], yo[j][:], rd=[byo[j]])
        S_.barrier()
        es.close()

    def next_ps_lo():
        j = psk[0] % 4
        psk[0] += 1
        return psum[j], PB[j]

    def stage_b(l):
        es, f = stage_sb()
        lam_init = 0.8 - 0.6 * math.exp(-0.3 * l)
        scale = 128.0 ** -0.5
        NB = TT // 128
        dal = f("bdal", [1, 512]); pr = f("bpr", [1, 256]); s2 = f("bs2", [1, 2]); lam1 = f("blam1", [1, 1])
        lamb = f("blamb", [128, 1]); dag = f("bdag", [128, 2])
        bl = Buf()
        S_.dma(dal[:], dal_in[l], wr=[bl])
        S_.dma(dag[:], dag_in[l], wr=[bl])
        d4 = dal[:].rearrange("p (a d) -> p a d", d=128)
        S_.op("dve", lambda e: e.tensor_tensor(out=pr[:].rearrange("p (a d) -> p a d", d=128), in0=d4[:, 0::2, :], in1=d4[:, 1::2, :], op=ALU.mult), rd=[bl], wr=[bl])
        S_.op("dve", lambda e: e.tensor_reduce(out=s2[:], in_=pr[:].rearrange("p (a d) -> p a d", d=128), axis=mybir.AxisListType.X, op=ALU.add), rd=[bl], wr=[bl])
        S_.op("act", lambda e: e.activation(out=s2[:], in_=s2[:], func=AF.Exp), rd=[bl], wr=[bl])
        S_.op("dve", lambda e: e.tensor_tensor(out=lam1[:], in0=s2[:, 0:1], in1=s2[:, 1:2], op=ALU.subtract), rd=[bl], wr=[bl])
        S_.op("dve", lambda e: e.tensor_scalar(out=lam1[:], in0=lam1[:], scalar1=lam_init, scalar2=None, op0=ALU.add), rd=[bl], wr=[bl])
        ps, pb = next_ps()
        S_.pe_group([lambda e: e.matmul(ps[:, 0:1], ones_f[0:1, :], lam1[:], start=True, stop=True)], rd=[bl, B_const], wr=[pb])
        S_.op("dve", lambda e: e.tensor_copy(out=lamb[:], in_=ps[:, 0:1]), rd=[pb], wr=[bl])
        S_.op("dve", lambda e: e.tensor_scalar(out=dag[:], in0=dag[:], scalar1=(1.0 - lam_init), scalar2=None, op0=ALU.mult), rd=[bl], wr=[bl])

        xin = [f(f"bx{i}", [128, TT]) for i in range(2)]; bx = [Buf(), Buf()]
        cs_ = [f(f"bcs{i}", [32, 2, TT]) for i in range(2)]; bcs = [Buf(), Buf()]
        t1 = f("bt1", [32, TT]); t2 = f("bt2", [32, TT]); bt = Buf()
        xb = [f(f"bxb{i}", [128, TT], BF16) for i in range(2)]; bxb = [Buf(), Buf()]
        v0 = [f(f"bv0{i}", [128, TT]) for i in range(2)]; v1_ = [f(f"bv1{i}", [128, TT]) for i in range(2)]; bv = [Buf(), Buf()]
        V1 = f("bV1", [128, NB128, 257], BF16); bV1 = Buf()
        KT = [f(f"bKT{c}", [128, T], BF16) for c in range(2)]; bKT = Buf()
        QT = [f(f"bQT{i}", [128, TT], BF16) for i in range(2)]; bQT = [Buf(), Buf()]
        pT = [f(f"bpT{i}", [128, TT], BF16) for i in range(3)]; bpT = [Buf() for _ in range(3)]
        Oc = [f(f"bOc{c}", [128, NB, 257]) for c in range(2)]; bOc = [Buf(), Buf()]
        rz = f("brz", [128, NB, 2]); on = f("bon", [128, NB, 256]); sq = f("bsq", [128, NB, 256]); ss = f("bss", [128, NB])
        bo = Buf()
        yo = [f(f"byo{i}", [128, TT], BF16) for i in range(2)]; byo = [Buf(), Buf()]
        PBacc = Buf()
        S_.op("pool", lambda e: e.memset(V1[:], 1.0), wr=[bV1])
        k = 0
        for h in range(H):
            for idx in range(4):
                chunk = (o_bq if idx < 2 else o_bk) + 2 * h + (idx % 2)
                for tt in range(NT):
                    j = k % 2; k += 1
                    tsl = slice(tt * TT, (tt + 1) * TT)
                    S_.dma(xin[j][:], PROJ[chunk, :, tsl], wr=[bx[j]])
                    S_.dma(cs_[j][:, 0, :], cos_in[:, tsl], wr=[bcs[j]])
                    S_.dma(cs_[j][:, 1, :], sin_in[:, tsl], wr=[bcs[j]])
                    ps, pb = next_ps()
                    S_.pe_group([lambda e: e.matmul(ps[0:32, 0:TT], rotm[:], xin[j][0:32, :], start=True, stop=True)], rd=[bx[j], B_const], wr=[pb])
                    S_.op("dve", lambda e: e.tensor_tensor(out=t1[:], in0=xin[j][0:32, :], in1=cs_[j][:, 0, :], op=ALU.mult), rd=[bx[j], bcs[j]], wr=[bt])
                    S_.op("dve", lambda e: e.tensor_tensor(out=t2[:], in0=ps[0:32, 0:TT], in1=cs_[j][:, 1, :], op=ALU.mult), rd=[pb, bcs[j]], wr=[bt])
                    S_.op("pool", lambda e: e.tensor_tensor(out=xin[j][0:32, :], in0=t1[:], in1=t2[:], op=ALU.add), rd=[bt], wr=[bx[j]])
                    S_.op("act", lambda e: e.activation(out=xb[j][:], in_=xin[j][:], func=AF.Copy), rd=[bx[j]], wr=[bxb[j]])
                    S_.dma(BQK[idx, :, tsl], xb[j][:], rd=[bxb[j]])
            for tt in range(NT):
                j = k % 2; k += 1
                tsl = slice(tt * TT, (tt + 1) * TT)
                S_.dma(v0[j][:], PROJ[o_bv + 2 * h, :, tsl], wr=[bv[j]])
                S_.dma(v1_[j][:], PROJ[o_bv + 2 * h + 1, :, tsl], wr=[bv[j]])
                for b in range(NB):
                    ps, pb = next_ps()
                    S_.pe_group([lambda e: e.matmul(ps[:, 0:128], v0[j][:, b * 128:(b + 1) * 128], ident_f[:], start=True, stop=True),
                                 lambda e: e.matmul(ps[:, 128:256], v1_[j][:, b * 128:(b + 1) * 128], ident_f[:], start=True, stop=True)],
                                rd=[bv[j], B_const], wr=[pb])
                    S_.op(("act", "dve")[b % 2] if False else "dve", lambda e: e.tensor_copy(out=V1[:, tt * NB + b, 0:256], in_=ps[:, 0:256]), rd=[pb], wr=[bV1])
            S_.barrier()
            S_.dma(KT[0][:], BQK[2], wr=[bKT])
            S_.dma(KT[1][:], BQK[3], wr=[bKT])
            for qb in range(NT):
                tsl = slice(qb * TT, (qb + 1) * TT)
                for c in range(2):
                    jq = k % 2; k += 1
                    S_.dma(QT[jq][:], BQK[c, :, tsl], wr=[bQT[jq]])
                    for kb in range(NB128):
                        ps, pb = next_ps_lo()
                        S_.pe_group([lambda e: e.matmul(ps[:, 0:TT], KT[c][:, kb * 128:(kb + 1) * 128], QT[jq][:], start=True, stop=True)],
                                    rd=[bKT, bQT[jq]], wr=[pb])
                        ip = kb % 3
                        S_.op("act", lambda e: e.activation(out=pT[ip][:], in_=ps[:, 0:TT], func=AF.Exp, scale=scale), rd=[pb], wr=[bpT[ip]])
                        S_.pe_group([(lambda e, qs=qs: e.matmul(psum[4 + qs][:, 0:257], pT[ip][:, qs * 128:(qs + 1) * 128], V1[:, kb, :],
                                                                start=(kb == 0), stop=(kb == NB128 - 1))) for qs in range(NB)],
                                    rd=[bpT[ip], bV1], wr=[PBacc, PB[4], PB[5], PB[6], PB[7]])
                    for qs in range(NB):
                        S_.op(("dve", "act")[qs % 2], (lambda e, qs=qs: e.tensor_copy(out=Oc[c][:, qs, :], in_=psum[4 + qs][:, 0:257])) if qs % 2 == 0 else
                              (lambda e, qs=qs: e.activation(out=Oc[c][:, qs, :], in_=psum[4 + qs][:, 0:257], func=AF.Copy)), rd=[PBacc], wr=[bOc[c]])
                S_.op("dve", lambda e: e.reciprocal(out=rz[:, :, 0:1], in_=Oc[0][:, :, 256:257]), rd=[bOc[0]], wr=[bo])
                S_.op("dve", lambda e: e.reciprocal(out=rz[:, :, 1:2], in_=Oc[1][:, :, 256:257]), rd=[bOc[1]], wr=[bo])
                S_.op("dve", lambda e: e.tensor_scalar(out=rz[:, :, 1:2], in0=rz[:, :, 1:2], scalar1=lamb[:, 0:1], scalar2=None, op0=ALU.mult), rd=[bo, bl], wr=[bo])
                S_.op("dve", lambda e: e.tensor_tensor(out=on[:], in0=Oc[0][:, :, 0:256], in1=rz[:, :, 0:1].broadcast_to([128, NB, 256]), op=ALU.mult), rd=[bOc[0], bo], wr=[bo])
                S_.op("pool", lambda e: e.tensor_tensor(out=sq[:], in0=Oc[1][:, :, 0:256], in1=rz[:, :, 1:2].broadcast_to([128, NB, 256]), op=ALU.mult), rd=[bOc[1], bo], wr=[bo])
                S_.op("dve", lambda e: e.tensor_tensor(out=on[:], in0=on[:], in1=sq[:], op=ALU.subtract), rd=[bo], wr=[bo])
                S_.op("pool", lambda e: e.tensor_tensor(out=sq[:], in0=on[:], in1=on[:], op=ALU.mult), rd=[bo], wr=[bo])
                S_.op("dve", lambda e: e.tensor_reduce(out=ss[:], in_=sq[:], axis=mybir.AxisListType.X, op=ALU.add), rd=[bo], wr=[bo])
                S_.op("dve", lambda e: e.tensor_scalar(out=ss[:], in0=ss[:], scalar1=1.0 / 256, scalar2=EPS, op0=ALU.mult, op1=ALU.add), rd=[bo], wr=[bo])
                rsqrt(ss[:], bo)
                S_.op("dve", lambda e: e.tensor_tensor(out=on[:], in0=on[:], in1=ss[:].unsqueeze(2).broadcast_to([128, NB, 256]), op=ALU.mult), rd=[bo], wr=[bo])
                for half in range(2):
                    ps, pb = next_ps_lo()
                    S_.pe_group([(lambda e, qs=qs: e.matmul(ps[:, qs * 128:(qs + 1) * 128], on[:, qs, half * 128:(half + 1) * 128], ident_f[:], start=True, stop=True))
                                 for qs in range(NB)], rd=[bo, B_const], wr=[pb])
                    jy = k % 2; k += 1
                    S_.op("dve", lambda e: e.tensor_scalar(out=yo[jy][:], in0=ps[:, 0:TT], scalar1=dag[:, half:half + 1], scalar2=None, op0=ALU.mult), rd=[pb, bl], wr=[byo[jy]])
                    S_.dma(YT[H + 2 * h + half, :, tsl], yo[jy][:], rd=[byo[jy]])
            S_.barrier()
        es.close()

    def stage_c(l):
        es, f = stage_sb()
        G = H
        sgw_f = f("csgwf", [128, G * 128]); sgw = f("csgw", [128, G * 128], BF16); sgbs = f("csgbs", [128, G * 128])
        sgg = f("csgg", [128, HGW]); sgb = f("csgb", [128, HGW])
        bc_ = Buf()
        S_.dma(sgw_f[:], sgw_in[l], wr=[bc_]); S_.dma(sgbs[:], sgbs_in[l], wr=[bc_])
        S_.dma(sgg[:], sgg_in[l], wr=[bc_]); S_.dma(sgb[:], sgb_in[l], wr=[bc_])
        S_.op("dve", lambda e: e.tensor_copy(out=sgw[:], in_=sgw_f[:]), rd=[bc_], wr=[bc_])
        cu = [f(f"ccu{i}", [128, G, 128]) for i in range(2)]; cv = [f(f"ccv{i}", [128, G, 128]) for i in range(2)]
        bcu = [Buf(), Buf()]; bcv = [Buf(), Buf()]
        gv = f("cgv", [128, HGW]); xc = f("cxc", [128, HGW]); st = f("cst", [128, 2]); vt = f("cvt", [128, HGW], BF16)
        bg_ = Buf(); bvt = Buf()
        tmp = f("ctmp", [128, G, 128]); yo = [f(f"cyo{i}", [128, G, 128], BF16) for i in range(2)]; btmp = Buf(); byo = [Buf(), Buf()]
        for blk in range(NB128):
            j = blk % 2
            tsl = slice(blk * 128, (blk + 1) * 128)
            S_.dma(cu[j][:], PROJ[o_cu:o_cu + G, :, tsl].rearrange("g p t -> p g t"), wr=[bcu[j]])
            S_.dma(cv[j][:], PROJ[o_cv:o_cv + G, :, tsl].rearrange("g p t -> p g t"), wr=[bcv[j]])
            S_.op("act", lambda e: e.activation(out=cu[j][:], in_=cu[j][:], func=AF.Gelu_apprx_tanh), rd=[bcu[j]], wr=[bcu[j]])
            nbk = (G * 128 + 511) // 512
            pss = [next_ps() for _ in range(nbk)]
            for bk in range(nbk):
                gs = range(bk * 4, min(G, bk * 4 + 4))
                S_.pe_group([(lambda e, g=g: e.matmul(pss[bk][0][:, (g % 4) * 128:(g % 4 + 1) * 128], cv[j][:, g, :], ident_f[:], start=True, stop=True)) for g in gs],
                            rd=[bcv[j], B_const], wr=[pss[bk][1]])
                w = len(gs) * 128
                S_.op("act", lambda e: e.activation(out=gv[:, bk * 512:bk * 512 + w], in_=pss[bk][0][:, 0:w], func=AF.Gelu_apprx_tanh), rd=[pss[bk][1]], wr=[bg_])
            S_.op("dve", lambda e: e.tensor_reduce(out=st[:, 0:1], in_=gv[:], axis=mybir.AxisListType.X, op=ALU.add), rd=[bg_], wr=[bg_])
            S_.op("dve", lambda e: e.tensor_scalar(out=st[:, 0:1], in0=st[:, 0:1], scalar1=1.0 / HGW, scalar2=None, op0=ALU.mult), rd=[bg_], wr=[bg_])
            S_.op("dve", lambda e: e.tensor_scalar(out=xc[:], in0=gv[:], scalar1=st[:, 0:1], scalar2=None, op0=ALU.subtract), rd=[bg_], wr=[bg_])
            S_.op("pool", lambda e: e.tensor_tensor(out=gv[:], in0=xc[:], in1=xc[:], op=ALU.mult), rd=[bg_], wr=[bg_])
            S_.op("dve", lambda e: e.tensor_reduce(out=st[:, 1:2], in_=gv[:], axis=mybir.AxisListType.X, op=ALU.add), rd=[bg_], wr=[bg_])
            S_.op("dve", lambda e: e.tensor_scalar(out=st[:, 1:2], in0=st[:, 1:2], scalar1=1.0 / HGW, scalar2=EPS, op0=ALU.mult, op1=ALU.add), rd=[bg_], wr=[bg_])
            rsqrt(st[:, 1:2], bg_)
            S_.op("dve", lambda e: e.scalar_tensor_tensor(out=xc[:], in0=xc[:], scalar=st[:, 1:2], in1=sgg[:], op0=ALU.mult, op1=ALU.mult), rd=[bg_, bc_], wr=[bg_])
            S_.op("pool", lambda e: e.tensor_tensor(out=vt[:], in0=xc[:], in1=sgb[:], op=ALU.add), rd=[bg_, bc_], wr=[bvt])
            pss = [next_ps() for _ in range(nbk)]
            for bk in range(nbk):
                gs = range(bk * 4, min(G, bk * 4 + 4))
                S_.pe_group([(lambda e, g=g: e.matmul(pss[bk][0][:, (g % 4) * 128:(g % 4 + 1) * 128], vt[:, g * 128:(g + 1) * 128], sgw[:, g * 128:(g + 1) * 128], start=True, stop=True)) for g in gs],
                            rd=[bvt, bc_], wr=[pss[bk][1]])
                w = len(gs) * 128
                g0 = bk * 4
                S_.op("dve", lambda e: e.tensor_tensor(out=tmp[:, g0:g0 + len(gs), :], in0=pss[bk][0][:, 0:w].rearrange("p (g t) -> p g t", t=128),
                                                       in1=sgbs[:, g0 * 128:g0 * 128 + w].rearrange("p (g t) -> p g t", t=128), op=ALU.add), rd=[pss[bk][1], bc_], wr=[btmp])
            S_.op("pool", lambda e: e.tensor_tensor(out=yo[j][:], in0=tmp[:], in1=cu[j][:], op=ALU.mult), rd=[btmp, bcu[j]], wr=[byo[j]])
            S_.dma(YT[3 * H:4 * H, :, tsl].rearrange("g p t -> p g t"), yo[j][:], rd=[byo[j]])
        S_.barrier()
        es.close()

    RS = dscr("RS", [KC, 128, T])
    RSB = [Buf(f"rs{m}") for m in range(KC)]

    class LNStream:
        def __init__(self, f, g_in, b_in):
            self.xin = [f(f"lx{i}", [128, TM]) for i in range(3)]; self.bx = [Buf() for _ in range(3)]
            self.sq = [f(f"lq{i}", [128, TM]) for i in range(2)]; self.bq = [Buf(), Buf()]
            self.ob = [f(f"lo{i}", [128, TM], BF16) for i in range(2)]; self.bo = [Buf(), Buf()]
            self.mean = f("lmean", [128, TM]); self.rstd = f("lrstd", [128, TM]); self.msq = f("lmsq", [128, TM]); self.bstat = Buf()
            self.g = f("lg", [128, KC]); self.b = f("lb", [128, KC]); self.bgb = Buf()
            S_.dma(self.g[:], g_in, wr=[self.bgb]); S_.dma(self.b[:], b_in, wr=[self.bgb])
            self.k = 0

        def add(self, m, ps, pb, tsl, first, last, from_rs):
            i = self.k % 3; self.k += 1
            if from_rs:
                S_.dma(self.xin[i][:], RS[m, :, tsl], rd=[RSB[m]], wr=[self.bx[i]])
                S_.op("dve", lambda e: e.tensor_tensor(out=self.xin[i][:], in0=self.xin[i][:], in1=ps[:, 0:TM], op=ALU.add), rd=[pb, self.bx[i]], wr=[self.bx[i]])
            else:
                S_.dma(self.xin[i][:], XT[m, :, tsl], wr=[self.bx[i]])
                S_.op("dve", lambda e: e.scalar_tensor_tensor(out=self.xin[i][:], in0=self.xin[i][:], scalar=ALPHA, in1=ps[:, 0:TM], op0=ALU.mult, op1=ALU.add),
                      rd=[pb, self.bx[i]], wr=[self.bx[i]])
            if last:
                q = m % 2
                S_.op("pool", lambda e: e.tensor_tensor(out=self.sq[q][:], in0=self.xin[i][:], in1=self.xin[i][:], op=ALU.mult), rd=[self.bx[i]], wr=[self.bq[q]])
                S_.pe_group([lambda e: e.matmul(psum[6][:, 0:TM], ones_f[:], self.xin[i][:], start=(m == 0), stop=(m == KC - 1))], rd=[self.bx[i], B_const], wr=[PB[6]])
                S_.pe_group([lambda e: e.matmul(psum[7][:, 0:TM], ones_f[:], self.sq[q][:], start=(m == 0), stop=(m == KC - 1))], rd=[self.bq[q], B_const], wr=[PB[7]])
            S_.dma(RS[m, :, tsl], self.xin[i][:], rd=[self.bx[i]], wr=[RSB[m]])

        def finish(self, tsl, final):
            mean, rstd, msq, bstat = self.mean, self.rstd, self.msq, self.bstat
            S_.op("dve", lambda e: e.tensor_scalar(out=mean[:], in0=psum[6][:, 0:TM], scalar1=1.0 / D, scalar2=None, op0=ALU.mult), rd=[PB[6]], wr=[bstat])
            S_.op("dve", lambda e: e.tensor_scalar(out=rstd[:], in0=psum[7][:, 0:TM], scalar1=1.0 / D, scalar2=EPS, op0=ALU.mult, op1=ALU.add), rd=[PB[7]], wr=[bstat])
            S_.op("pool", lambda e: e.tensor_tensor(out=msq[:], in0=mean[:], in1=mean[:], op=ALU.mult), rd=[bstat], wr=[bstat])
            S_.op("dve", lambda e: e.tensor_tensor(out=rstd[:], in0=rstd[:], in1=msq[:], op=ALU.subtract), rd=[bstat], wr=[bstat])
            rsqrt(rstd[:], bstat)
            for m in range(KC):
                i = self.k % 3; self.k += 1
                S_.dma(self.xin[i][:], RS[m, :, tsl], rd=[RSB[m]], wr=[self.bx[i]])
                S_.op("dve", lambda e: e.tensor_tensor(out=self.xin[i][:], in0=self.xin[i][:], in1=mean[:], op=ALU.subtract), rd=[self.bx[i], bstat], wr=[self.bx[i]])
                S_.op("pool", lambda e: e.tensor_tensor(out=self.xin[i][:], in0=self.xin[i][:], in1=rstd[:], op=ALU.mult), rd=[self.bx[i], bstat], wr=[self.bx[i]])
                S_.op("dve", lambda e: e.tensor_scalar(out=self.xin[i][:], in0=self.xin[i][:], scalar1=self.g[:, m:m + 1], scalar2=self.b[:, m:m + 1], op0=ALU.mult, op1=ALU.add),
                      rd=[self.bx[i], self.bgb], wr=[self.bx[i]])
                if final:
                    S_.dma(yT_out[m, :, tsl], self.xin[i][:], rd=[self.bx[i]])
                else:
                    S_.dma(XT[m, :, tsl], self.xin[i][:], rd=[self.bx[i]])
                    q = m % 2
                    S_.op("act", lambda e: e.activation(out=self.ob[q][:], in_=self.xin[i][:], func=AF.Copy), rd=[self.bx[i]], wr=[self.bo[q]])
                    S_.dma(HT[m, :, tsl], self.ob[q][:], rd=[self.bo[q]])

    def stage_merge(l):
        es, f = stage_sb()
        ps_ring[0] = 6
        y = f("my", [128, KC, TM], BF16); by = Buf()
        mg = f("mmg", [128, KC, TM], BF16); bmg = Buf()
        gt = [f(f"mgt{i}", [128, 3, TM], BF16) for i in range(2)]; bgt = [Buf(), Buf()]
        t3 = [f(f"mt{i}", [128, TM]) for i in range(3)]; bt3 = [Buf() for _ in range(3)]
        ln = LNStream(f, ln1g_in[l], ln1b_in[l])
        wr_ = WRing(f, max(KC, 2 * H), 4)
        for tt in range(NTM):
            tsl = slice(tt * TM, (tt + 1) * TM)
            S_.dma(y[:], YT[:, :, tsl].rearrange("k p t -> p k t"), wr=[by])
            for i in range(KC):
                jg = i % 2
                S_.dma(gt[jg][:], GATE[:, :, tsl].rearrange("(c k) p t -> k p c t", k=KC)[i], wr=[bgt[jg]])
                for bi, (nm, koff, kn) in enumerate((("ba", 0, H), ("bb", H, 2 * H), ("bc", 3 * H, H))):
                    wt, wb = wr_.load(Wb[nm][l][i])
                    ps, pb = next_ps()
                    S_.pe_group([(lambda e, k=k: e.matmul(ps[:, 0:TM], wt[:, k * 128:(k + 1) * 128], y[:, koff + k, :], start=(k == 0), stop=(k == kn - 1))) for k in range(kn)],
                                rd=[wb, by], wr=[pb])
                    S_.op("dve", lambda e: e.tensor_tensor(out=t3[bi][:], in0=ps[:, 0:TM], in1=gt[jg][:, bi, :], op=ALU.mult), rd=[pb, bgt[jg]], wr=[bt3[bi]])
                S_.op("pool", lambda e: e.tensor_tensor(out=t3[0][:], in0=t3[0][:], in1=t3[1][:], op=ALU.add), rd=[bt3[0], bt3[1]], wr=[bt3[0]])
                S_.op("pool", lambda e: e.tensor_tensor(out=mg[:, i, :], in0=t3[0][:], in1=t3[2][:], op=ALU.add), rd=[bt3[0], bt3[2]], wr=[bmg])

            def epi(m, ps, pb):
                ln.add(m, ps, pb, tsl, True, True, False)
            gemm_tile(wr_, [Wb["out"][l][m] for m in range(KC)], lambda k: mg[:, k, :], [bmg], KC, TM, epi)
            ln.finish(tsl, False)
        S_.barrier()
        ps_ring[0] = 8
        es.close()

    def stage_ffn(l, final):
        es, f = stage_sb()
        ps_ring[0] = 6
        NH = 2 if FC >= 64 else 1
        FH = FC // NH
        x1 = f("fx1", [128, KC, TM], BF16); bx1 = Buf()
        hid = f("fhid", [128, FH, TM], BF16); bh = Buf()
        rl = [f(f"frl{i}", [128, TM]) for i in range(2)]; brl = [Buf(), Buf()]
        ln = LNStream(f, ln2g_in[l], ln2b_in[l])
        KU = min(FH, 32)
        wr_ = WRing(f, max(KC, KU), 4)
        cnt = [0]
        for tt in range(NTM):
            tsl = slice(tt * TM, (tt + 1) * TM)
            S_.dma(x1[:], HT[:, :, tsl].rearrange("k p t -> p k t"), wr=[bx1])
            for hf in range(NH):
                def epi_up(m, ps, pb):
                    i = cnt[0] % 2; cnt[0] += 1
                    S_.op("act", lambda e: e.activation(out=rl[i][:], in_=ps[:, 0:TM], func=AF.Relu), rd=[pb], wr=[brl[i]])
                    S_.op("pool", lambda e: e.tensor_tensor(out=hid[:, m, :], in0=rl[i][:], in1=rl[i][:], op=ALU.mult), rd=[brl[i]], wr=[bh])
                gemm_tile(wr_, [Wb["up"][l][hf * FH + m] for m in range(FH)], lambda k: x1[:, k, :], [bx1], KC, TM, epi_up)
                for m in range(KC):
                    ps, pb = next_ps()
                    nsu = FH // KU
                    for su in range(nsu):
                        c0 = (hf * FH + su * KU) * 128
                        wt, wb = wr_.load(Wb["down"][l][m][:, c0:c0 + KU * 128])
                        S_.pe_group([(lambda e, k=k: e.matmul(ps[:, 0:TM], wt[:, k * 128:(k + 1) * 128], hid[:, su * KU + k, :],
                                                              start=(su == 0 and k == 0), stop=(su == nsu - 1 and k == KU - 1))) for k in range(KU)],
                                    rd=[wb, bh], wr=[pb])
                    ln.add(m, ps, pb, tsl, hf == 0, hf == NH - 1, hf > 0)
            ln.finish(tsl, final)
        S_.barrier()
        ps_ring[0] = 8
        es.close()

    stop_after = cfg.get("stop_after")
    stage_x0()
    for nm, src in (("in", w_in_f), ("gate", w_gate_f), ("ba", w_ba_f), ("bb", w_bb_f), ("bc", w_bc_f),
                    ("out", w_out_f), ("up", w_up_f), ("down", w_down_f)):
        for l_ in range(DEPTH):
            stage_cast(src[l_], Wb[nm][l_])
    dbg = {}
    for l in range(DEPTH):
        stage_proj(l)
        stage_a_prep(l); stage_a_rec(l); stage_a_norm(l)
        stage_b(l)
        stage_c(l)
        stage_merge(l)
        stage_ffn(l, final=(l == DEPTH - 1))
    for nm in cfg.get("dump", ()):
        src = {"PROJ": PJA, "YT": YT, "XT": XT, "GATE": GATE, "OA": OA, "HT": HT}[nm]
        o = nc.dram_tensor("dump_" + nm, list(src.shape), src.dtype, kind="ExternalOutput").ap()
        S_.dma(o, src)
    S_.barrier()
    ES.close()
    return nc


def tile_w(W):
    K, M = W.shape
    return np.ascontiguousarray(W.reshape(K // 128, 128, M // 128, 128).transpose(2, 1, 0, 3)).reshape(M // 128, 128, (K // 128) * 128)


def host_inputs(cfg, inp, b):
    D, S, DEPTH, KC, H = cfg["D"], cfg["S"], cfg["DEPTH"], cfg["KC"], cfg["H"]
    HGW = cfg["HGW"]
    f32 = np.float32
    m = {}
    m["xT"] = np.ascontiguousarray(inp["x"][b].T).reshape(KC, 128, S)
    for nm, key in (("w_in_t", "w_in"), ("w_gate_t", "w_gate"), ("w_ba_t", "w_branch_a"), ("w_bb_t", "w_branch_b"),
                    ("w_bc_t", "w_branch_c"), ("w_out_t", "w_out"), ("w_up_t", "w_up"), ("w_down_t", "w_down")):
        m[nm] = np.stack([tile_w(np.asarray(inp[key][l])) for l in range(DEPTH)])
    pk = lambda v, n: np.ascontiguousarray(np.asarray(v).reshape(DEPTH, n, 128).transpose(0, 2, 1))
    m["b_gate_p"] = pk(inp["b_gate"], 3 * KC)
    m["ln1_g_p"] = pk(inp["ln1_g"], KC); m["ln1_b_p"] = pk(inp["ln1_b"], KC)
    m["ln2_g_p"] = pk(inp["ln2_g"], KC); m["ln2_b_p"] = pk(inp["ln2_b"], KC)
    m["lbraw_p"] = np.ascontiguousarray(np.asarray(inp["hg_lb_raw"]).reshape(DEPTH, 2, H, 128).transpose(3, 0, 1, 2)).reshape(128, DEPTH * 2 * H)
    m["hgg_p"] = pk(inp["hg_norm_g"], H)
    m["dal_p"] = np.ascontiguousarray(np.asarray(inp["da_lambda"]).reshape(DEPTH, 1, 512))
    m["dag_p"] = pk(inp["da_norm_g"], 2)
    m["sgg_b"] = np.ascontiguousarray(np.broadcast_to(np.asarray(inp["sg_norm_g"])[:, None, :], (DEPTH, 128, HGW)))
    m["sgb_b"] = np.ascontiguousarray(np.broadcast_to(np.asarray(inp["sg_norm_b"])[:, None, :], (DEPTH, 128, HGW)))
    m["sgw_t"] = np.ascontiguousarray(np.asarray(inp["sg_w_s"]).transpose(0, 3, 1, 2)).reshape(DEPTH, 128, H * 128)
    m["sgbs_b"] = np.ascontiguousarray(np.broadcast_to(np.asarray(inp["sg_b_s"]).reshape(DEPTH, 1, H * 128), (DEPTH, 128, H * 128)))
    pos = np.arange(S, dtype=f32)
    freqs = (f32(ROPE_THETA) ** (-np.arange(0, 32, 2, dtype=f32) / f32(32))).astype(f32)
    ang = (pos[:, None] * freqs[None, :]).astype(f32)
    cos = np.cos(ang).astype(f32).T; sin = np.sin(ang).astype(f32).T
    m["cosF"] = np.ascontiguousarray(np.concatenate([cos, cos], 0)); m["sinF"] = np.ascontiguousarray(np.concatenate([sin, sin], 0))
    R = np.zeros((32, 32), f32)
    for i in range(16):
        R[16 + i, i] = -1.0; R[i, 16 + i] = 1.0
    m["rotm"] = R
    m["ident"] = np.eye(128, dtype=f32)
    st = np.arange(64)
    m["trif"] = (st[:, None] <= st[None, :]).astype(f32); m["trib"] = (st[:, None] >= st[None, :]).astype(f32)
    return {k: np.ascontiguousarray(v, dtype=f32) for k, v in m.items()}


def kernel(**inputs):
    cfg = make_cfg()
    nc = build_program(cfg)
    B = cfg["B"]
    base = host_inputs(cfg, inputs, 0)
    in_maps = [base]
    for b in range(1, B):
        mb = dict(base)
        mb["xT"] = np.ascontiguousarray(np.asarray(inputs["x"][b]).T, dtype=np.float32).reshape(cfg["KC"], 128, cfg["S"])
        in_maps.append(mb)
    res = run_bass_kernel_spmd(nc, in_maps, core_ids=list(range(B)))
    out = np.stack([np.ascontiguousarray(res.results[b]["yT"].reshape(cfg["D"], cfg["S"]).T) for b in range(B)])
    return out.astype(np.float32)
```

```python
import math
import numpy as np
import concourse.bass as bass
import concourse.mybir as mybir
from concourse.bass_utils import run_bass_kernel_spmd

F32 = mybir.dt.float32
BF16 = mybir.dt.bfloat16
AF = mybir.ActivationFunctionType
ALU = mybir.AluOpType

EPS = 1e-5
ROPE_THETA = 500000.0


def make_cfg(D=4096, S=8192, DEPTH=2, B=2):
    c = dict(D=D, S=S, DEPTH=DEPTH, B=B)
    c["KC"] = D // 128
    c["HGW"] = D // 4
    c["H"] = c["HGW"] // 128
    c["DFF"] = 4 * D
    c["DIN"] = 5 * c["HGW"] + 3 * (2 * c["H"] * 128) + 2 * c["HGW"]
    c["ALPHA"] = (2.0 * DEPTH) ** 0.25
    return c


class Buf:
    __slots__ = ("w", "r", "name")

    def __init__(self, name=""):
        self.w = {}
        self.r = {}
        self.name = name


class Sch:
    SEM_CAP = 30000

    def __init__(self, nc):
        self.nc = nc
        self.eng = {"pe": nc.tensor, "act": nc.scalar, "dve": nc.vector, "pool": nc.gpsimd, "sp": nc.sync}
        self.cur = {}
        self.seen = {e: {} for e in self.eng}
        self.nsem = 0
        self.all_tokens = {}
        self.snap = {}
        self.dma_ring = []
        self.dma_next = 0
        self.NDMA = 12
        for e in ("pe", "act", "dve", "pool"):
            self.cur[e] = [self._newsem(), 0]
        for i in range(self.NDMA):
            self.dma_ring.append([self._newsem(), 0])

    def _newsem(self):
        self.nsem += 1
        return self.nc.alloc_semaphore(f"sm{self.nsem}")

    def _wait(self, e, tok):
        sem, val = tok
        k = id(sem)
        se = self.seen[e]
        if se.get(k, 0) >= val:
            return
        self.eng[e].wait_ge(sem, val)
        se[k] = val
        sn = self.snap.get((k, val))
        if sn:
            for k2, v2 in sn.items():
                if se.get(k2, 0) < v2:
                    se[k2] = v2

    def _deps(self, e, rd, wr):
        toks = {}
        for b in rd:
            for k, t in b.w.items():
                if toks.get(k, (None, 0))[1] < t[1]:
                    toks[k] = t
        for b in wr:
            for d in (b.w, b.r):
                for k, t in d.items():
                    if toks.get(k, (None, 0))[1] < t[1]:
                        toks[k] = t
        for t in toks.values():
            self._wait(e, t)

    def _mark(self, tok, rd, wr, e=None):
        k = id(tok[0])
        if e is not None:
            self.snap[(k, tok[1])] = dict(self.seen[e])
        self.all_tokens[k] = tok
        for b in wr:
            b.w[k] = tok
            b.r = {}
        for b in rd:
            b.r[k] = tok

    def op(self, e, fn, rd=(), wr=()):
        self._deps(e, rd, wr)
        cur = self.cur[e]
        if cur[1] >= self.SEM_CAP:
            cur[0] = self._newsem()
            cur[1] = 0
        cur[1] += 1
        tok = (cur[0], cur[1])
        fn(self.eng[e]).then_inc(cur[0], 1)
        self._mark(tok, rd, wr, e)
        return tok

    def pe_group(self, fns, rd=(), wr=()):
        self._deps("pe", rd, wr)
        cur = self.cur["pe"]
        if cur[1] >= self.SEM_CAP:
            cur[0] = self._newsem()
            cur[1] = 0
        ins = None
        for fn in fns:
            ins = fn(self.nc.tensor)
        cur[1] += 1
        tok = (cur[0], cur[1])
        ins.then_inc(cur[0], 1)
        self._mark(tok, rd, wr, "pe")
        return tok

    def dma(self, out, in_, rd=(), wr=(), e="sp"):
        self._deps(e, rd, wr)
        slot = self.dma_ring[self.dma_next]
        self.dma_next = (self.dma_next + 1) % self.NDMA
        if slot[1] + 16 > self.SEM_CAP:
            slot[0] = self._newsem()
            slot[1] = 0
        if slot[1] > 0:
            self._wait(e, (slot[0], slot[1]))
        slot[1] += 16
        tok = (slot[0], slot[1])
        self.eng[e].dma_start(out=out, in_=in_).then_inc(slot[0], 16)
        self._mark(tok, rd, wr, e)
        return tok

    def dma_st(self, out, in_, rd=(), wr=()):
        return self.dma(out, in_, rd=rd, wr=wr, e="act")

    def barrier(self):
        toks = list(self.all_tokens.values())
        for e in self.eng:
            for t in toks:
                self._wait(e, t)


def build_program(cfg, debug_outputs=()):
    D, S, DEPTH, KC, H, DFF, DIN = (cfg[k] for k in ("D", "S", "DEPTH", "KC", "H", "DFF", "DIN"))
    T = S
    ALPHA = cfg["ALPHA"]
    HGW = cfg["HGW"]
    NPROJ = DIN // 128
    NGATE = 3 * KC
    FC = DFF // 128
    TT = 512 if T >= 512 else T
    NT = T // TT
    TM = 512 if T >= 512 else T
    NTM = T // TM
    NCH = T // 64
    NB128 = T // 128

    nc = bass.Bass("TRN2", target_bir_lowering=False)
    S_ = Sch(nc)

    def din(name, shape, dt=F32):
        return nc.dram_tensor(name, list(shape), dt, kind="ExternalInput").ap()

    def dscr(name, shape, dt=F32):
        return nc.dram_tensor(name, list(shape), dt).ap()

    xT_in = din("xT", [KC, 128, T])
    w_in_f = din("w_in_t", [DEPTH, NPROJ, 128, KC * 128])
    w_gate_f = din("w_gate_t", [DEPTH, NGATE, 128, KC * 128])
    w_ba_f = din("w_ba_t", [DEPTH, KC, 128, H * 128])
    w_bb_f = din("w_bb_t", [DEPTH, KC, 128, 2 * H * 128])
    w_bc_f = din("w_bc_t", [DEPTH, KC, 128, H * 128])
    w_out_f = din("w_out_t", [DEPTH, KC, 128, KC * 128])
    w_up_f = din("w_up_t", [DEPTH, FC, 128, KC * 128])
    w_down_f = din("w_down_t", [DEPTH, KC, 128, FC * 128])
    b_gate_in = din("b_gate_p", [DEPTH, 128, NGATE])
    ln1g_in = din("ln1_g_p", [DEPTH, 128, KC]); ln1b_in = din("ln1_b_p", [DEPTH, 128, KC])
    ln2g_in = din("ln2_g_p", [DEPTH, 128, KC]); ln2b_in = din("ln2_b_p", [DEPTH, 128, KC])
    lbraw_in = din("lbraw_p", [128, DEPTH * 2 * H])
    hgg_in = din("hgg_p", [DEPTH, 128, H])
    dal_in = din("dal_p", [DEPTH, 1, 512])
    dag_in = din("dag_p", [DEPTH, 128, 2])
    sgg_in = din("sgg_b", [DEPTH, 128, HGW]); sgb_in = din("sgb_b", [DEPTH, 128, HGW])
    sgw_in = din("sgw_t", [DEPTH, 128, H * 128])
    sgbs_in = din("sgbs_b", [DEPTH, 128, H * 128])
    cos_in = din("cosF", [32, T]); sin_in = din("sinF", [32, T])
    rot_in = din("rotm", [32, 32])
    ident_in = din("ident", [128, 128])
    trif_in = din("trif", [64, 64]); trib_in = din("trib", [64, 64])

    yT_out = nc.dram_tensor("yT", [KC, 128, T], F32, kind="ExternalOutput").ap()

    Wb = {}
    for nm, src in (("in", w_in_f), ("gate", w_gate_f), ("ba", w_ba_f), ("bb", w_bb_f), ("bc", w_bc_f),
                    ("out", w_out_f), ("up", w_up_f), ("down", w_down_f)):
        Wb[nm] = [dscr(f"wb_{nm}{l_}", src.shape[1:], BF16) for l_ in range(DEPTH)]
    XT = dscr("XT", [KC, 128, T])
    HT = dscr("HT", [KC, 128, T], BF16)
    PJA = dscr("PJA", [5 * H, 128, T]); PJB = dscr("PJB", [6 * H, 128, T]); PJC = dscr("PJC", [2 * H, 128, T])

    class _Proj:
        def __getitem__(self, idx):
            c = idx[0]
            c0 = c.start if isinstance(c, slice) else c
            if c0 < 5 * H:
                t_, off = PJA, 0
            elif c0 < 11 * H:
                t_, off = PJB, 5 * H
            else:
                t_, off = PJC, 11 * H
            if isinstance(c, slice):
                return t_[(slice(c.start - off, c.stop - off),) + tuple(idx[1:])]
            return t_[(c - off,) + tuple(idx[1:])]
    PROJ = _Proj()
    GATE = dscr("GATE", [NGATE, 128, T], BF16)
    YT = dscr("YT", [KC, 128, T], BF16)
    AQM = dscr("AQM", [2, H, 128, T], BF16); AKM = dscr("AKM", [2, H, 128, T], BF16)
    AQH = dscr("AQH", [2, H, 128, T], BF16); AKH = dscr("AKH", [2, H, 128, T], BF16)
    AV = dscr("AV", [H, 128, T], BF16)
    ADEC = dscr("ADEC", [2, H, 128, NCH])
    OA = dscr("OA", [2, H, T, 128])
    BQK = dscr("BQK", [4, 128, T], BF16)

    o_aq, o_aff, o_afb, o_ai, o_ag = 0, H, 2 * H, 3 * H, 4 * H
    o_bq, o_bk, o_bv = 5 * H, 7 * H, 9 * H
    o_cu, o_cv = 11 * H, 12 * H

    from contextlib import ExitStack
    ES = ExitStack()

    uid = [0]

    def uname(name):
        uid[0] += 1
        return f"sb{uid[0]}_{name}"

    def sb(name, shape, dt=F32):
        return ES.enter_context(nc.sbuf_tensor(uname(name), list(shape), dt))

    ident_f = sb("ident_f", [128, 128]); ident_b = sb("ident_b", [128, 128], BF16)
    trif = sb("trif", [64, 64]); trib = sb("trib", [64, 64])
    rotm = sb("rotm", [32, 32])
    ones_f = sb("ones_f", [128, 128])
    ones_b = sb("ones_b", [128, 128], BF16)
    lbt = sb("lbt", [128, DEPTH * 2 * H]); omlt = sb("omlt", [128, DEPTH * 2 * H])
    B_const = Buf("const")

    psum = [ES.enter_context(nc.psum_tensor(f"ps{i}", [128, 512], F32)) for i in range(8)]
    PB = [Buf(f"ps{i}") for i in range(8)]

    S_.dma(ident_f[:], ident_in, wr=[B_const])
    S_.dma(trif[:], trif_in, wr=[B_const])
    S_.dma(trib[:], trib_in, wr=[B_const])
    S_.dma(rotm[:], rot_in, wr=[B_const])
    S_.dma(lbt[:], lbraw_in, wr=[B_const])
    S_.barrier()
    S_.op("dve", lambda e: e.tensor_copy(out=ident_b[:], in_=ident_f[:]), rd=[B_const], wr=[B_const])
    S_.op("dve", lambda e: e.memset(ones_f[:], 1.0), wr=[B_const])
    S_.op("dve", lambda e: e.memset(ones_b[:], 1.0), wr=[B_const])
    NL = 2 * H
    S_.op("act", lambda e: e.activation(out=lbt[:], in_=lbt[:], func=AF.Exp), rd=[B_const], wr=[B_const])
    zsum = sb("zsum", [128, NL]); zrec = sb("zrec", [128, NL])
    S_.op("dve", lambda e: e.tensor_copy(out=zsum[:], in_=lbt[:, 0:NL]), rd=[B_const], wr=[B_const])
    for l in range(1, DEPTH):
        S_.op("dve", lambda e, l=l: e.tensor_tensor(out=zsum[:], in0=zsum[:], in1=lbt[:, l * NL:(l + 1) * NL], op=ALU.add),
              rd=[B_const], wr=[B_const])
    S_.op("dve", lambda e: e.reciprocal(out=zrec[:], in_=zsum[:]), rd=[B_const], wr=[B_const])
    S_.op("dve", lambda e: e.memset(lbt[:, 0:NL], 0.0), rd=[B_const], wr=[B_const])
    for l in range(2, DEPTH):
        S_.op("dve", lambda e, l=l: e.tensor_tensor(out=lbt[:, l * NL:(l + 1) * NL], in0=lbt[:, l * NL:(l + 1) * NL],
                                                    in1=lbt[:, (l - 1) * NL:l * NL], op=ALU.add), rd=[B_const], wr=[B_const])
    for l in range(DEPTH):
        S_.op("dve", lambda e, l=l: e.tensor_tensor(out=lbt[:, l * NL:(l + 1) * NL], in0=lbt[:, l * NL:(l + 1) * NL],
                                                    in1=zrec[:], op=ALU.mult), rd=[B_const], wr=[B_const])
    S_.op("dve", lambda e: e.tensor_scalar(out=omlt[:], in0=lbt[:], scalar1=-1.0, scalar2=1.0, op0=ALU.mult, op1=ALU.add),
          rd=[B_const], wr=[B_const])
    S_.barrier()

    def stage_sb():
        es = ExitStack()
        def f(name, shape, dt=F32):
            return es.enter_context(nc.sbuf_tensor(uname(name), list(shape), dt))
        return es, f

    ENG3 = ("dve", "pool", "act")

    def rsqrt(ap, b):
        S_.op("act", lambda e: e.activation(out=ap, in_=ap, func=AF.Sqrt), rd=[b], wr=[b])
        S_.op("dve", lambda e: e.reciprocal(out=ap, in_=ap), rd=[b], wr=[b])

    def copy_op(en, out, in_, rd, wr):
        if en == "act":
            return S_.op("act", lambda e: e.activation(out=out, in_=in_, func=AF.Copy), rd=rd, wr=wr)
        return S_.op(en, lambda e: e.tensor_copy(out=out, in_=in_), rd=rd, wr=wr)

    def stage_cast(src, dst):
        es, f = stage_sb()
        s2 = src
        d2 = dst
        n, _, fsz = s2.shape
        CH = min(4096, fsz)
        NBUF = 3
        tin = [f(f"ci{i}", [128, CH]) for i in range(NBUF)]
        tout = [f(f"co{i}", [128, CH], BF16) for i in range(NBUF)]
        bi = [Buf() for _ in range(NBUF)]; bo = [Buf() for _ in range(NBUF)]
        k = 0
        for i in range(n):
            for c in range(fsz // CH):
                j = k % NBUF
                S_.dma(tin[j][:], s2[i, :, c * CH:(c + 1) * CH], wr=[bi[j]])
                copy_op(ENG3[k % 3], tout[j][:], tin[j][:], [bi[j]], [bo[j]])
                S_.dma_st(d2[i, :, c * CH:(c + 1) * CH], tout[j][:], rd=[bo[j]])
                k += 1
        S_.barrier()
        es.close()

    def stage_x0():
        es, f = stage_sb()
        CH = min(2048, T)
        tin = [f(f"xi{i}", [128, CH]) for i in range(2)]
        tout = [f(f"xo{i}", [128, CH], BF16) for i in range(2)]
        bi = [Buf(), Buf()]; bo = [Buf(), Buf()]
        k = 0
        for kc in range(KC):
            for c in range(T // CH):
                j = k % 2
                S_.dma(tin[j][:], xT_in[kc, :, c * CH:(c + 1) * CH], wr=[bi[j]])
                copy_op(ENG3[k % 2], tout[j][:], tin[j][:], [bi[j]], [bo[j]])
                S_.dma_st(XT[kc, :, c * CH:(c + 1) * CH], tin[j][:], rd=[bi[j]])
                S_.dma_st(HT[kc, :, c * CH:(c + 1) * CH], tout[j][:], rd=[bo[j]])
                k += 1
        S_.barrier()
        es.close()

    class WRing:
        def __init__(self, f, kcw, nbuf=4, tag="w"):
            self.t = [f(f"{tag}{i}", [128, kcw * 128], BF16) for i in range(nbuf)]
            self.b = [Buf() for _ in range(nbuf)]
            self.k = 0
            self.n = nbuf

        def load(self, src):
            j = self.k % self.n
            self.k += 1
            S_.dma(self.t[j][:, 0:src.shape[-1]], src, wr=[self.b[j]])
            return self.t[j], self.b[j]

    psk = [0]

    ps_ring = [8]

    def next_ps(n=1):
        j = psk[0] % ps_ring[0]
        psk[0] += 1
        return psum[j], PB[j]

    def gemm_tile(wring, wsrcs, act_fn, act_bufs, kcs, N, epi):
        for m, src in enumerate(wsrcs):
            wt, wb = wring.load(src)
            ps, pb = next_ps()
            fns = []
            for k in range(kcs):
                fns.append(lambda e, k=k, wt=wt, ps=ps: e.matmul(ps[:, 0:N], wt[:, k * 128:(k + 1) * 128], act_fn(k),
                                                                 start=(k == 0), stop=(k == kcs - 1)))
            S_.pe_group(fns, rd=[wb] + list(act_bufs), wr=[pb])
            epi(m, ps, pb)

    def stage_proj(l):
        es, f = stage_sb()
        ht = [f(f"ht{i}", [128, KC, TT], BF16) for i in range(2)]
        hb = [Buf(), Buf()]
        bg = f("bg", [128, NGATE])
        bgb = Buf()
        S_.dma(bg[:], b_gate_in[l], wr=[bgb])
        wr_ = WRing(f, KC, 4)
        ot = [f(f"ot{i}", [128, TT]) for i in range(4)]
        otb = [f(f"otb{i}", [128, TT], BF16) for i in range(4)]
        ob = [Buf() for _ in range(4)]
        cnt = [0]
        for tt in range(NT):
            j = tt % 2
            tsl = slice(tt * TT, (tt + 1) * TT)
            S_.dma(ht[j][:], HT[:, :, tsl].rearrange("k p t -> p k t"), wr=[hb[j]])

            def epi_proj(m, ps, pb, tsl=tsl):
                i = cnt[0] % 4
                cnt[0] += 1
                copy_op(("dve", "act")[cnt[0] % 2], ot[i][:], ps[:, 0:TT], [pb], [ob[i]])
                S_.dma_st(PROJ[m, :, tsl], ot[i][:], rd=[ob[i]])

            def epi_gate(m, ps, pb, tsl=tsl):
                i = cnt[0] % 4
                cnt[0] += 1
                S_.op("act", lambda e: e.activation(out=otb[i][:], in_=ps[:, 0:TT], func=AF.Sigmoid, bias=bg[:, m:m + 1], scale=1.0),
                      rd=[pb, bgb], wr=[ob[i]])
                S_.dma_st(GATE[m, :, tsl], otb[i][:], rd=[ob[i]])

            act = lambda k, j=j: ht[j][:, k, :]
            gemm_tile(wr_, [Wb["in"][l][m] for m in range(NPROJ)], act, [hb[j]], KC, TT, epi_proj)
            gemm_tile(wr_, [Wb["gate"][l][m] for m in range(NGATE)], act, [hb[j]], KC, TT, epi_gate)
        S_.barrier()
        es.close()

    def stage_a_prep(l):
        es, f = stage_sb()
        TS = min(2048, T)
        NCS = TS // 64
        q = f("aq", [128, TS]); a = f("aa", [128, TS]); kk = f("akk", [128, TS])
        pe_ = f("ape", [128, TS + 64]); onesr = f("aones", [128, TS])
        arg = [f(f"aarg{i}", [128, TS]) for i in range(2)]
        ex = [f(f"aex{i}", [128, TS]) for i in range(2)]
        outb = [f(f"aob{i}", [128, TS], BF16) for i in range(2)]
        dec = f("adec", [128, NCS]); dec2 = f("adec2", [128, NCS])
        bq, ba, bk, bpe, bones, bdec = Buf(), Buf(), Buf(), Buf(), Buf(), Buf()
        barg = [Buf(), Buf()]; bex = [Buf(), Buf()]; bob = [Buf(), Buf()]
        S_.op("pool", lambda e: e.memset(onesr[:], 1.0), wr=[bones])
        S_.op("pool", lambda e: e.memset(pe_[:], 0.0), wr=[bpe])
        v3 = lambda ap: ap.rearrange("p (c t) -> p c t", t=64)
        cnt = [0]
        for h in range(H):
            for sg in range(T // TS):
                tsl = slice(sg * TS, (sg + 1) * TS)
                S_.dma(q[:], PROJ[o_aq + h, :, tsl], wr=[bq])
                S_.dma(a[:], PROJ[o_ai + h, :, tsl], wr=[ba])
                i0 = cnt[0] % 2; cnt[0] += 1
                S_.op("pool", lambda e, i0=i0: e.tensor_copy(out=outb[i0][:], in_=a[:]), rd=[ba], wr=[bob[i0]])
                S_.dma_st(AV[h, :, tsl], outb[i0][:], rd=[bob[i0]])
                for dr in range(2):
                    col = l * 2 * H + dr * H + h
                    S_.dma(a[:], PROJ[(o_aff if dr == 0 else o_afb) + h, :, tsl], wr=[ba])
                    S_.op("act", lambda e: e.activation(out=a[:], in_=a[:], func=AF.Sigmoid), rd=[ba], wr=[ba])
                    S_.op("dve", lambda e, col=col: e.tensor_scalar(out=a[:], in0=a[:], scalar1=omlt[:, col:col + 1], scalar2=lbt[:, col:col + 1],
                                                                    op0=ALU.mult, op1=ALU.add), rd=[ba, B_const], wr=[ba])
                    S_.op("pool", lambda e: e.tensor_scalar(out=kk[:], in0=a[:], scalar1=-1.0, scalar2=1.0, op0=ALU.mult, op1=ALU.add),
                          rd=[ba], wr=[bk])
                    S_.op("act", lambda e: e.activation(out=a[:], in_=a[:], func=AF.Ln), rd=[ba], wr=[ba])
                    S_.op("dve", lambda e: e.tensor_tensor_scan(out=pe_[:, 1:TS + 1], data0=onesr[:], data1=a[:], initial=0.0,
                                                                op0=ALU.mult, op1=ALU.add), rd=[ba, bones], wr=[bpe])
                    PF = v3(pe_[:, 1:TS + 1]) if dr == 0 else v3(pe_[:, 0:TS])
                    base = v3(pe_[:, 0:TS])
                    Rmid = base[:, :, 32:33]; Rcs = base[:, :, 0:1]; Rce = v3(pe_[:, 64:TS + 64])[:, :, 0:1]
                    bc = lambda r: r.broadcast_to([128, NCS, 64])
                    if dr == 0:
                        specs = [(PF, bc(Rmid), q, AQM), (bc(Rmid), PF, kk, AKM), (PF, bc(Rcs), q, AQH), (bc(Rce), PF, kk, AKH)]
                    else:
                        specs = [(bc(Rmid), PF, q, AQM), (PF, bc(Rmid), kk, AKM), (bc(Rce), PF, q, AQH), (PF, bc(Rcs), kk, AKH)]
                    for (x0, x1, mul, dst) in specs:
                        i = cnt[0] % 2; cnt[0] += 1
                        S_.op("dve", lambda e, i=i, x0=x0, x1=x1: e.tensor_tensor(out=v3(arg[i][:]), in0=x0, in1=x1, op=ALU.subtract),
                              rd=[bpe], wr=[barg[i]])
                        S_.op("act", lambda e, i=i: e.activation(out=ex[i][:], in_=arg[i][:], func=AF.Exp), rd=[barg[i]], wr=[bex[i]])
                        mb = bq if mul is q else bk
                        S_.op("pool", lambda e, i=i, mul=mul: e.tensor_tensor(out=outb[i][:], in0=ex[i][:], in1=mul[:], op=ALU.mult),
                              rd=[bex[i], mb], wr=[bob[i]])
                        S_.dma_st(dst[dr, h, :, tsl], outb[i][:], rd=[bob[i]])
                    S_.op("dve", lambda e: e.tensor_tensor(out=dec[:].unsqueeze(2), in0=Rce, in1=Rcs, op=ALU.subtract), rd=[bpe], wr=[bdec])
                    S_.op("act", lambda e: e.activation(out=dec2[:], in_=dec[:], func=AF.Exp), rd=[bdec], wr=[bdec])
                    S_.dma_st(ADEC[dr, h, :, sg * NCS:(sg + 1) * NCS], dec2[:], rd=[bdec])
        S_.barrier()
        es.close()

    def stage_a_rec(l):
        es, f = stage_sb()
        GT = min(512, T)
        GC = GT // 64
        NG = T // GT
        names = ("qm", "km", "qh", "kh", "v")
        tl = [[[f(f"r{n}{d}{i}", [128, GT], BF16) for n in names] for i in range(2)] for d in range(2)]
        tb = [[Buf() for i in range(2)] for d in range(2)]
        dec = [f(f"rdec{d}", [128, NCH]) for d in range(2)]; bdec = [Buf(), Buf()]
        st = [f(f"rst{d}", [128, 128]) for d in range(2)]; stb = [f(f"rstb{d}", [128, 128], BF16) for d in range(2)]
        bst = [Buf(), Buf()]; bstb = [Buf(), Buf()]
        khv = [[f(f"rkhv{d}{i}", [64, 256], BF16) for i in range(2)] for d in range(2)]
        bkhv = [[Buf(), Buf()] for d in range(2)]
        pT = [[f(f"rpT{d}{i}", [64, 64], BF16) for i in range(2)] for d in range(2)]
        bpT = [[Buf(), Buf()] for d in range(2)]
        ost = [[f(f"rost{d}{i}", [64, GC, 128]) for i in range(2)] for d in range(2)]
        bost = [[Buf(), Buf()] for d in range(2)]
        srcs = (AQM, AKM, AQH, AKH)
        for h in range(H):
            for d in range(2):
                S_.dma(dec[d][:], ADEC[d, h], wr=[bdec[d]])
                S_.op("dve", lambda e, d=d: e.memset(st[d][:], 0.0), wr=[bst[d]])
                S_.op("pool", lambda e, d=d: e.memset(stb[d][:], 0.0), wr=[bstb[d]])
            for gi in range(NG):
                for d in range(2):
                    g = gi if d == 0 else NG - 1 - gi
                    j = gi % 2
                    tsl = slice(g * GT, (g + 1) * GT)
                    for n in range(4):
                        S_.dma(tl[d][j][n][:], srcs[n][d, h, :, tsl], wr=[tb[d][j]])
                    S_.dma(tl[d][j][4][:], AV[h, :, tsl], wr=[tb[d][j]])
                for ci in range(GC):
                    for d in range(2):
                        g = gi if d == 0 else NG - 1 - gi
                        j = gi % 2
                        cl = ci if d == 0 else GC - 1 - ci
                        c = g * GC + cl
                        lo = cl * 64
                        qm, km, qh, kh, vv = tl[d][j]
                        i2 = (gi * GC + ci) % 2
                        ps1, pb1 = next_ps()
                        S_.pe_group([lambda e: e.matmul(ps1[0:64, 0:128], kh[:, lo:lo + 64], ident_b[:], start=True, stop=True),
                                     lambda e: e.matmul(ps1[0:64, 128:256], vv[:, lo:lo + 64], ident_b[:], start=True, stop=True)],
                                    rd=[tb[d][j], B_const], wr=[pb1])
                        S_.op("act", lambda e: e.activation(out=khv[d][i2][:], in_=ps1[0:64, 0:256], func=AF.Copy), rd=[pb1], wr=[bkhv[d][i2]])
                        ps2, pb2 = next_ps()
                        S_.pe_group([lambda e: e.matmul(ps2[0:64, 0:64], km[:, lo:lo + 64], qm[:, lo:lo + 64], start=True, stop=True)],
                                    rd=[tb[d][j]], wr=[pb2])
                        msk = trif if d == 0 else trib
                        S_.op("dve", lambda e: e.tensor_tensor(out=pT[d][i2][:], in0=ps2[0:64, 0:64], in1=msk[:], op=ALU.mult),
                              rd=[pb2, B_const], wr=[bpT[d][i2]])
                        ps3, pb3 = next_ps()
                        S_.pe_group([lambda e: e.matmul(ps3[0:64, 0:128], pT[d][i2][:], khv[d][i2][:, 128:256], start=True, stop=False),
                                     lambda e: e.matmul(ps3[0:64, 0:128], qh[:, lo:lo + 64], stb[d][:], start=False, stop=True)],
                                    rd=[bpT[d][i2], bkhv[d][i2], tb[d][j], bstb[d]], wr=[pb3])
                        S_.op("act", lambda e: e.activation(out=ost[d][j][:, cl, :], in_=ps3[0:64, 0:128], func=AF.Copy), rd=[pb3], wr=[bost[d][j]])
                        ps4, pb4 = next_ps()
                        S_.pe_group([lambda e: e.matmul(ps4[:, 0:128], khv[d][i2][:, 0:128], khv[d][i2][:, 128:256], start=True, stop=True)],
                                    rd=[bkhv[d][i2]], wr=[pb4])
                        S_.op("dve", lambda e: e.scalar_tensor_tensor(out=st[d][:], in0=st[d][:], scalar=dec[d][:, c:c + 1], in1=ps4[:, 0:128],
                                                                       op0=ALU.mult, op1=ALU.add), rd=[pb4, bdec[d], bst[d]], wr=[bst[d]])
                        S_.op("pool", lambda e: e.tensor_copy(out=stb[d][:], in_=st[d][:]), rd=[bst[d]], wr=[bstb[d]])
                for d in range(2):
                    g = gi if d == 0 else NG - 1 - gi
                    j = gi % 2
                    S_.dma_st(OA[d, h, g * GT:(g + 1) * GT, :].rearrange("(c t) v -> t c v", t=64), ost[d][j][:], rd=[bost[d][j]])
        S_.barrier()
        es.close()

    def stage_a_norm(l):
        es, f = stage_sb()
        NB = TT // 128
        of = [f(f"nof{i}", [128, NB, 128]) for i in range(2)]; ob_ = [f(f"nob{i}", [128, NB, 128]) for i in range(2)]
        sq = f("nsq", [128, NB, 128]); ss = f("nss", [128, NB]); on = [f(f"non{i}", [128, NB, 128]) for i in range(2)]
        sg_ = [f(f"nsg{i}", [128, TT]) for i in range(2)]; yo = [f(f"nyo{i}", [128, TT], BF16) for i in range(2)]
        hg = f("nhg", [128, H])
        bof = [Buf(), Buf()]; bsq, bss = Buf(), Buf(); bon = [Buf(), Buf()]; bsg = [Buf(), Buf()]; byo = [Buf(), Buf()]; bhg = Buf()
        S_.dma(hg[:], hgg_in[l], wr=[bhg])
        k = 0
        for h in range(H):
            for tt in range(NT):
                j = k % 2; k += 1
                tsl = slice(tt * TT, (tt + 1) * TT)
                S_.dma(of[j][:], OA[0, h, tsl, :].rearrange("(b p) v -> p b v", p=128), wr=[bof[j]])
                S_.dma(ob_[j][:], OA[1, h, tsl, :].rearrange("(b p) v -> p b v", p=128), wr=[bof[j]])
                S_.dma(sg_[j][:], PROJ[o_ag + h, :, tsl], wr=[bsg[j]])
                S_.op("act", lambda e: e.activation(out=sg_[j][:], in_=sg_[j][:], func=AF.Silu), rd=[bsg[j]], wr=[bsg[j]])
                S_.op("pool", lambda e: e.tensor_tensor(out=of[j][:], in0=of[j][:], in1=ob_[j][:], op=ALU.add), rd=[bof[j]], wr=[bof[j]])
                S_.op("pool", lambda e: e.tensor_tensor(out=sq[:], in0=of[j][:], in1=of[j][:], op=ALU.mult), rd=[bof[j]], wr=[bsq])
                S_.op("dve", lambda e: e.tensor_reduce(out=ss[:], in_=sq[:], axis=mybir.AxisListType.X, op=ALU.add), rd=[bsq], wr=[bss])
                S_.op("dve", lambda e: e.tensor_scalar(out=ss[:], in0=ss[:], scalar1=1.0 / 128, scalar2=EPS, op0=ALU.mult, op1=ALU.add), rd=[bss], wr=[bss])
                rsqrt(ss[:], bss)
                S_.op("dve", lambda e: e.tensor_tensor(out=on[j][:], in0=of[j][:], in1=ss[:].unsqueeze(2).broadcast_to([128, NB, 128]), op=ALU.mult),
                      rd=[bof[j], bss], wr=[bon[j]])
                ps, pb = next_ps()
                S_.pe_group([(lambda e, b=b: e.matmul(ps[:, b * 128:(b + 1) * 128], on[j][:, b, :], ident_f[:], start=True, stop=True)) for b in range(NB)],
                            rd=[bon[j], B_const], wr=[pb])
                S_.op("dve", lambda e: e.scalar_tensor_tensor(out=yo[j][:], in0=ps[:, 0:TT], scalar=hg[:, h:h + 1], in1=sg_[j][:], op0=ALU.mult, op1=ALU.mult),
                      rd=[pb, bhg, bsg[j]], wr=[byo[j]])
                S_.dma_st(YT[h, :, tsl], yo[j][:], rd=[byo[j]])
        S_.barrier()
        es.close()

    def next_ps_lo():
        j = psk[0] % 4
        psk[0] += 1
        return psum[j], PB[j]

    def stage_b(l):
        es, f = stage_sb()
        lam_init = 0.8 - 0.6 * math.exp(-0.3 * l)
        scale = 128.0 ** -0.5
        NB = TT // 128
        dal = f("bdal", [1, 512]); pr = f("bpr", [1, 256]); s2 = f("bs2", [1, 2]); lam1 = f("blam1", [1, 1])
        lamb = f("blamb", [128, 1]); dag = f("bdag", [128, 2])
        bl = Buf()
        S_.dma(dal[:], dal_in[l], wr=[bl])
        S_.dma(dag[:], dag_in[l], wr=[bl])
        d4 = dal[:].rearrange("p (a d) -> p a d", d=128)
        S_.op("dve", lambda e: e.tensor_tensor(out=pr[:].rearrange("p (a d) -> p a d", d=128), in0=d4[:, 0::2, :], in1=d4[:, 1::2, :], op=ALU.mult), rd=[bl], wr=[bl])
        S_.op("dve", lambda e: e.tensor_reduce(out=s2[:], in_=pr[:].rearrange("p (a d) -> p a d", d=128), axis=mybir.AxisListType.X, op=ALU.add), rd=[bl], wr=[bl])
        S_.op("act", lambda e: e.activation(out=s2[:], in_=s2[:], func=AF.Exp), rd=[bl], wr=[bl])
        S_.op("dve", lambda e: e.tensor_tensor(out=lam1[:], in0=s2[:, 0:1], in1=s2[:, 1:2], op=ALU.subtract), rd=[bl], wr=[bl])
        S_.op("dve", lambda e: e.tensor_scalar(out=lam1[:], in0=lam1[:], scalar1=lam_init, scalar2=None, op0=ALU.add), rd=[bl], wr=[bl])
        ps, pb = next_ps()
        S_.pe_group([lambda e: e.matmul(ps[:, 0:1], ones_f[0:1, :], lam1[:], start=True, stop=True)], rd=[bl, B_const], wr=[pb])
        S_.op("dve", lambda e: e.tensor_copy(out=lamb[:], in_=ps[:, 0:1]), rd=[pb], wr=[bl])
        S_.op("dve", lambda e: e.tensor_scalar(out=dag[:], in0=dag[:], scalar1=(1.0 - lam_init), scalar2=None, op0=ALU.mult), rd=[bl], wr=[bl])

        xin = [f(f"bx{i}", [128, TT]) for i in range(2)]; bx = [Buf(), Buf()]
        cs_ = [f(f"bcs{i}", [32, 2, TT]) for i in range(2)]; bcs = [Buf(), Buf()]
        t1 = f("bt1", [32, TT]); t2 = f("bt2", [32, TT]); bt = Buf()
        xb = [f(f"bxb{i}", [128, TT], BF16) for i in range(2)]; bxb = [Buf(), Buf()]
        v0 = [f(f"bv0{i}", [128, TT]) for i in range(2)]; v1_ = [f(f"bv1{i}", [128, TT]) for i in range(2)]; bv = [Buf(), Buf()]
        V1 = f("bV1", [128, NB128, 257], BF16); bV1 = Buf()
        KT = [f(f"bKT{c}", [128, T], BF16) for c in range(2)]; bKT = Buf()
        QT = [f(f"bQT{i}", [128, TT], BF16) for i in range(2)]; bQT = [Buf(), Buf()]
        pT = [f(f"bpT{i}", [128, TT], BF16) for i in range(3)]; bpT = [Buf() for _ in range(3)]
        Oc = [f(f"bOc{c}", [128, NB, 257]) for c in range(2)]; bOc = [Buf(), Buf()]
        rz = f("brz", [128, NB, 2]); on = f("bon", [128, NB, 256]); sq = f("bsq", [128, NB, 256]); ss = f("bss", [128, NB])
        bo = Buf()
        yo = [f(f"byo{i}", [128, TT], BF16) for i in range(2)]; byo = [Buf(), Buf()]
        PBacc = Buf()
        S_.op("pool", lambda e: e.memset(V1[:], 1.0), wr=[bV1])
        k = 0
        for h in range(H):
            for idx in range(4):
                chunk = (o_bq if idx < 2 else o_bk) + 2 * h + (idx % 2)
                for tt in range(NT):
                    j = k % 2; k += 1
                    tsl = slice(tt * TT, (tt + 1) * TT)
                    S_.dma(xin[j][:], PROJ[chunk, :, tsl], wr=[bx[j]])
                    S_.dma(cs_[j][:, 0, :], cos_in[:, tsl], wr=[bcs[j]])
                    S_.dma(cs_[j][:, 1, :], sin_in[:, tsl], wr=[bcs[j]])
                    ps, pb = next_ps()
                    S_.pe_group([lambda e: e.matmul(ps[0:32, 0:TT], rotm[:], xin[j][0:32, :], start=True, stop=True)], rd=[bx[j], B_const], wr=[pb])
                    S_.op("dve", lambda e: e.tensor_tensor(out=t1[:], in0=xin[j][0:32, :], in1=cs_[j][:, 0, :], op=ALU.mult), rd=[bx[j], bcs[j]], wr=[bt])
                    S_.op("dve", lambda e: e.tensor_tensor(out=t2[:], in0=ps[0:32, 0:TT], in1=cs_[j][:, 1, :], op=ALU.mult), rd=[pb, bcs[j]], wr=[bt])
                    S_.op("pool", lambda e: e.tensor_tensor(out=xin[j][0:32, :], in0=t1[:], in1=t2[:], op=ALU.add), rd=[bt], wr=[bx[j]])
                    S_.op("act", lambda e: e.activation(out=xb[j][:], in_=xin[j][:], func=AF.Copy), rd=[bx[j]], wr=[bxb[j]])
                    S_.dma_st(BQK[idx, :, tsl], xb[j][:], rd=[bxb[j]])
            for tt in range(NT):
                j = k % 2; k += 1
                tsl = slice(tt * TT, (tt + 1) * TT)
                S_.dma(v0[j][:], PROJ[o_bv + 2 * h, :, tsl], wr=[bv[j]])
                S_.dma(v1_[j][:], PROJ[o_bv + 2 * h + 1, :, tsl], wr=[bv[j]])
                for b in range(NB):
                    ps, pb = next_ps()
                    S_.pe_group([lambda e: e.matmul(ps[:, 0:128], v0[j][:, b * 128:(b + 1) * 128], ident_f[:], start=True, stop=True),
                                 lambda e: e.matmul(ps[:, 128:256], v1_[j][:, b * 128:(b + 1) * 128], ident_f[:], start=True, stop=True)],
                                rd=[bv[j], B_const], wr=[pb])
                    S_.op(("act", "dve")[b % 2] if False else "dve", lambda e: e.tensor_copy(out=V1[:, tt * NB + b, 0:256], in_=ps[:, 0:256]), rd=[pb], wr=[bV1])
            S_.barrier()
            S_.dma(KT[0][:], BQK[2], wr=[bKT])
            S_.dma(KT[1][:], BQK[3], wr=[bKT])
            for qb in range(NT):
                tsl = slice(qb * TT, (qb + 1) * TT)
                for c in range(2):
                    jq = k % 2; k += 1
                    S_.dma(QT[jq][:], BQK[c, :, tsl], wr=[bQT[jq]])
                    for kb in range(NB128):
                        ps, pb = next_ps_lo()
                        S_.pe_group([lambda e: e.matmul(ps[:, 0:TT], KT[c][:, kb * 128:(kb + 1) * 128], QT[jq][:], start=True, stop=True)],
                                    rd=[bKT, bQT[jq]], wr=[pb])
                        ip = kb % 3
                        S_.op("act", lambda e: e.activation(out=pT[ip][:], in_=ps[:, 0:TT], func=AF.Exp, scale=scale), rd=[pb], wr=[bpT[ip]])
                        S_.pe_group([(lambda e, qs=qs: e.matmul(psum[4 + qs][:, 0:257], pT[ip][:, qs * 128:(qs + 1) * 128], V1[:, kb, :],
                                                                start=(kb == 0), stop=(kb == NB128 - 1))) for qs in range(NB)],
                                    rd=[bpT[ip], bV1], wr=[PBacc, PB[4], PB[5], PB[6], PB[7]])
                    for qs in range(NB):
                        S_.op(("dve", "act")[qs % 2], (lambda e, qs=qs: e.tensor_copy(out=Oc[c][:, qs, :], in_=psum[4 + qs][:, 0:257])) if qs % 2 == 0 else
                              (lambda e, qs=qs: e.activation(out=Oc[c][:, qs, :], in_=psum[4 + qs][:, 0:257], func=AF.Copy)), rd=[PBacc], wr=[bOc[c]])
                S_.op("dve", lambda e: e.reciprocal(out=rz[:, :, 0:1], in_=Oc[0][:, :, 256:257]), rd=[bOc[0]], wr=[bo])
                S_.op("dve", lambda e: e.reciprocal(out=rz[:, :, 1:2], in_=Oc[1][:, :, 256:257]), rd=[bOc[1]], wr=[bo])
                S_.op("dve", lambda e: e.tensor_scalar(out=rz[:, :, 1:2], in0=rz[:, :, 1:2], scalar1=lamb[:, 0:1], scalar2=None, op0=ALU.mult), rd=[bo, bl], wr=[bo])
                S_.op("dve", lambda e: e.tensor_tensor(out=on[:], in0=Oc[0][:, :, 0:256], in1=rz[:, :, 0:1].broadcast_to([128, NB, 256]), op=ALU.mult), rd=[bOc[0], bo], wr=[bo])
                S_.op("pool", lambda e: e.tensor_tensor(out=sq[:], in0=Oc[1][:, :, 0:256], in1=rz[:, :, 1:2].broadcast_to([128, NB, 256]), op=ALU.mult), rd=[bOc[1], bo], wr=[bo])
                S_.op("dve", lambda e: e.tensor_tensor(out=on[:], in0=on[:], in1=sq[:], op=ALU.subtract), rd=[bo], wr=[bo])
                S_.op("pool", lambda e: e.tensor_tensor(out=sq[:], in0=on[:], in1=on[:], op=ALU.mult), rd=[bo], wr=[bo])
                S_.op("dve", lambda e: e.tensor_reduce(out=ss[:], in_=sq[:], axis=mybir.AxisListType.X, op=ALU.add), rd=[bo], wr=[bo])
                S_.op("dve", lambda e: e.tensor_scalar(out=ss[:], in0=ss[:], scalar1=1.0 / 256, scalar2=EPS, op0=ALU.mult, op1=ALU.add), rd=[bo], wr=[bo])
                rsqrt(ss[:], bo)
                S_.op("dve", lambda e: e.tensor_tensor(out=on[:], in0=on[:], in1=ss[:].unsqueeze(2).broadcast_to([128, NB, 256]), op=ALU.mult), rd=[bo], wr=[bo])
                for half in range(2):
                    ps, pb = next_ps_lo()
                    S_.pe_group([(lambda e, qs=qs: e.matmul(ps[:, qs * 128:(qs + 1) * 128], on[:, qs, half * 128:(half + 1) * 128], ident_f[:], start=True, stop=True))
                                 for qs in range(NB)], rd=[bo, B_const], wr=[pb])
                    jy = k % 2; k += 1
                    S_.op("dve", lambda e: e.tensor_scalar(out=yo[jy][:], in0=ps[:, 0:TT], scalar1=dag[:, half:half + 1], scalar2=None, op0=ALU.mult), rd=[pb, bl], wr=[byo[jy]])
                    S_.dma_st(YT[H + 2 * h + half, :, tsl], yo[jy][:], rd=[byo[jy]])
            S_.barrier()
        es.close()

    def stage_c(l):
        es, f = stage_sb()
        G = H
        sgw_f = f("csgwf", [128, G * 128]); sgw = f("csgw", [128, G * 128], BF16); sgbs = f("csgbs", [128, G * 128])
        sgg = f("csgg", [128, HGW]); sgb = f("csgb", [128, HGW])
        bc_ = Buf()
        S_.dma(sgw_f[:], sgw_in[l], wr=[bc_]); S_.dma(sgbs[:], sgbs_in[l], wr=[bc_])
        S_.dma(sgg[:], sgg_in[l], wr=[bc_]); S_.dma(sgb[:], sgb_in[l], wr=[bc_])
        S_.op("dve", lambda e: e.tensor_copy(out=sgw[:], in_=sgw_f[:]), rd=[bc_], wr=[bc_])
        cu = [f(f"ccu{i}", [128, G, 128]) for i in range(2)]; cv = [f(f"ccv{i}", [128, G, 128]) for i in range(2)]
        bcu = [Buf(), Buf()]; bcv = [Buf(), Buf()]
        gv = f("cgv", [128, HGW]); xc = f("cxc", [128, HGW]); st = f("cst", [128, 2]); vt = f("cvt", [128, HGW], BF16)
        bg_ = Buf(); bvt = Buf()
        tmp = f("ctmp", [128, G, 128]); yo = [f(f"cyo{i}", [128, G, 128], BF16) for i in range(2)]; btmp = Buf(); byo = [Buf(), Buf()]
        for blk in range(NB128):
            j = blk % 2
            tsl = slice(blk * 128, (blk + 1) * 128)
            S_.dma(cu[j][:], PROJ[o_cu:o_cu + G, :, tsl].rearrange("g p t -> p g t"), wr=[bcu[j]])
            S_.dma(cv[j][:], PROJ[o_cv:o_cv + G, :, tsl].rearrange("g p t -> p g t"), wr=[bcv[j]])
            S_.op("act", lambda e: e.activation(out=cu[j][:], in_=cu[j][:], func=AF.Gelu_apprx_tanh), rd=[bcu[j]], wr=[bcu[j]])
            nbk = (G * 128 + 511) // 512
            pss = [next_ps() for _ in range(nbk)]
            for bk in range(nbk):
                gs = range(bk * 4, min(G, bk * 4 + 4))
                S_.pe_group([(lambda e, g=g: e.matmul(pss[bk][0][:, (g % 4) * 128:(g % 4 + 1) * 128], cv[j][:, g, :], ident_f[:], start=True, stop=True)) for g in gs],
                            rd=[bcv[j], B_const], wr=[pss[bk][1]])
                w = len(gs) * 128
                S_.op("act", lambda e: e.activation(out=gv[:, bk * 512:bk * 512 + w], in_=pss[bk][0][:, 0:w], func=AF.Gelu_apprx_tanh), rd=[pss[bk][1]], wr=[bg_])
            S_.op("dve", lambda e: e.tensor_reduce(out=st[:, 0:1], in_=gv[:], axis=mybir.AxisListType.X, op=ALU.add), rd=[bg_], wr=[bg_])
            S_.op("dve", lambda e: e.tensor_scalar(out=st[:, 0:1], in0=st[:, 0:1], scalar1=1.0 / HGW, scalar2=None, op0=ALU.mult), rd=[bg_], wr=[bg_])
            S_.op("dve", lambda e: e.tensor_scalar(out=xc[:], in0=gv[:], scalar1=st[:, 0:1], scalar2=None, op0=ALU.subtract), rd=[bg_], wr=[bg_])
            S_.op("pool", lambda e: e.tensor_tensor(out=gv[:], in0=xc[:], in1=xc[:], op=ALU.mult), rd=[bg_], wr=[bg_])
            S_.op("dve", lambda e: e.tensor_reduce(out=st[:, 1:2], in_=gv[:], axis=mybir.AxisListType.X, op=ALU.add), rd=[bg_], wr=[bg_])
            S_.op("dve", lambda e: e.tensor_scalar(out=st[:, 1:2], in0=st[:, 1:2], scalar1=1.0 / HGW, scalar2=EPS, op0=ALU.mult, op1=ALU.add), rd=[bg_], wr=[bg_])
            rsqrt(st[:, 1:2], bg_)
            S_.op("dve", lambda e: e.scalar_tensor_tensor(out=xc[:], in0=xc[:], scalar=st[:, 1:2], in1=sgg[:], op0=ALU.mult, op1=ALU.mult), rd=[bg_, bc_], wr=[bg_])
            S_.op("pool", lambda e: e.tensor_tensor(out=vt[:], in0=xc[:], in1=sgb[:], op=ALU.add), rd=[bg_, bc_], wr=[bvt])
            pss = [next_ps() for _ in range(nbk)]
            for bk in range(nbk):
                gs = range(bk * 4, min(G, bk * 4 + 4))
                S_.pe_group([(lambda e, g=g: e.matmul(pss[bk][0][:, (g % 4) * 128:(g % 4 + 1) * 128], vt[:, g * 128:(g + 1) * 128], sgw[:, g * 128:(g + 1) * 128], start=True, stop=True)) for g in gs],
                            rd=[bvt, bc_], wr=[pss[bk][1]])
                w = len(gs) * 128
                g0 = bk * 4
                S_.op("dve", lambda e: e.tensor_tensor(out=tmp[:, g0:g0 + len(gs), :], in0=pss[bk][0][:, 0:w].rearrange("p (g t) -> p g t", t=128),
                                                       in1=sgbs[:, g0 * 128:g0 * 128 + w].rearrange("p (g t) -> p g t", t=128), op=ALU.add), rd=[pss[bk][1], bc_], wr=[btmp])
            S_.op("pool", lambda e: e.tensor_tensor(out=yo[j][:], in0=tmp[:], in1=cu[j][:], op=ALU.mult), rd=[btmp, bcu[j]], wr=[byo[j]])
            S_.dma_st(YT[3 * H:4 * H, :, tsl].rearrange("g p t -> p g t"), yo[j][:], rd=[byo[j]])
        S_.barrier()
        es.close()

    RS = dscr("RS", [KC, 128, T])
    RSB = [Buf(f"rs{m}") for m in range(KC)]

    class LNStream:
        def __init__(self, f, g_in, b_in):
            self.xin = [f(f"lx{i}", [128, TM]) for i in range(3)]; self.bx = [Buf() for _ in range(3)]
            self.sq = [f(f"lq{i}", [128, TM]) for i in range(2)]; self.bq = [Buf(), Buf()]
            self.ob = [f(f"lo{i}", [128, TM], BF16) for i in range(2)]; self.bo = [Buf(), Buf()]
            self.mean = f("lmean", [128, TM]); self.rstd = f("lrstd", [128, TM]); self.msq = f("lmsq", [128, TM]); self.bstat = Buf()
            self.g = f("lg", [128, KC]); self.b = f("lb", [128, KC]); self.bgb = Buf()
            S_.dma(self.g[:], g_in, wr=[self.bgb]); S_.dma(self.b[:], b_in, wr=[self.bgb])
            self.k = 0

        def add(self, m, ps, pb, tsl, first, last, from_rs):
            i = self.k % 3; self.k += 1
            if from_rs:
                S_.dma(self.xin[i][:], RS[m, :, tsl], rd=[RSB[m]], wr=[self.bx[i]])
                S_.op("dve", lambda e: e.tensor_tensor(out=self.xin[i][:], in0=self.xin[i][:], in1=ps[:, 0:TM], op=ALU.add), rd=[pb, self.bx[i]], wr=[self.bx[i]])
            else:
                S_.dma(self.xin[i][:], XT[m, :, tsl], wr=[self.bx[i]])
                S_.op("dve", lambda e: e.scalar_tensor_tensor(out=self.xin[i][:], in0=self.xin[i][:], scalar=ALPHA, in1=ps[:, 0:TM], op0=ALU.mult, op1=ALU.add),
                      rd=[pb, self.bx[i]], wr=[self.bx[i]])
            if last:
                q = m % 2
                S_.op("pool", lambda e: e.tensor_tensor(out=self.sq[q][:], in0=self.xin[i][:], in1=self.xin[i][:], op=ALU.mult), rd=[self.bx[i]], wr=[self.bq[q]])
                S_.pe_group([lambda e: e.matmul(psum[6][:, 0:TM], ones_f[:], self.xin[i][:], start=(m == 0), stop=(m == KC - 1))], rd=[self.bx[i], B_const], wr=[PB[6]])
                S_.pe_group([lambda e: e.matmul(psum[7][:, 0:TM], ones_f[:], self.sq[q][:], start=(m == 0), stop=(m == KC - 1))], rd=[self.bq[q], B_const], wr=[PB[7]])
            S_.dma_st(RS[m, :, tsl], self.xin[i][:], rd=[self.bx[i]], wr=[RSB[m]])

        def finish(self, tsl, final):
            mean, rstd, msq, bstat = self.mean, self.rstd, self.msq, self.bstat
            S_.op("dve", lambda e: e.tensor_scalar(out=mean[:], in0=psum[6][:, 0:TM], scalar1=1.0 / D, scalar2=None, op0=ALU.mult), rd=[PB[6]], wr=[bstat])
            S_.op("dve", lambda e: e.tensor_scalar(out=rstd[:], in0=psum[7][:, 0:TM], scalar1=1.0 / D, scalar2=EPS, op0=ALU.mult, op1=ALU.add), rd=[PB[7]], wr=[bstat])
            S_.op("pool", lambda e: e.tensor_tensor(out=msq[:], in0=mean[:], in1=mean[:], op=ALU.mult), rd=[bstat], wr=[bstat])
            S_.op("dve", lambda e: e.tensor_tensor(out=rstd[:], in0=rstd[:], in1=msq[:], op=ALU.subtract), rd=[bstat], wr=[bstat])
            rsqrt(rstd[:], bstat)
            for m in range(KC):
                i = self.k % 3; self.k += 1
                S_.dma(self.xin[i][:], RS[m, :, tsl], rd=[RSB[m]], wr=[self.bx[i]])
                S_.op("dve", lambda e: e.tensor_tensor(out=self.xin[i][:], in0=self.xin[i][:], in1=mean[:], op=ALU.subtract), rd=[self.bx[i], bstat], wr=[self.bx[i]])
                S_.op("pool", lambda e: e.tensor_tensor(out=self.xin[i][:], in0=self.xin[i][:], in1=rstd[:], op=ALU.mult), rd=[self.bx[i], bstat], wr=[self.bx[i]])
                S_.op("dve", lambda e: e.tensor_scalar(out=self.xin[i][:], in0=self.xin[i][:], scalar1=self.g[:, m:m + 1], scalar2=self.b[:, m:m + 1], op0=ALU.mult, op1=ALU.add),
                      rd=[self.bx[i], self.bgb], wr=[self.bx[i]])
                if final:
                    S_.dma_st(yT_out[m, :, tsl], self.xin[i][:], rd=[self.bx[i]])
                else:
                    S_.dma_st(XT[m, :, tsl], self.xin[i][:], rd=[self.bx[i]])
                    q = m % 2
                    S_.op("act", lambda e: e.activation(out=self.ob[q][:], in_=self.xin[i][:], func=AF.Copy), rd=[self.bx[i]], wr=[self.bo[q]])
                    S_.dma_st(HT[m, :, tsl], self.ob[q][:], rd=[self.bo[q]])

    def stage_merge(l):
        es, f = stage_sb()
        ps_ring[0] = 6
        y = f("my", [128, KC, TM], BF16); by = Buf()
        mg = f("mmg", [128, KC, TM], BF16); bmg = Buf()
        gt = [f(f"mgt{i}", [128, 3, TM], BF16) for i in range(2)]; bgt = [Buf(), Buf()]
        t3 = [f(f"mt{i}", [128, TM]) for i in range(3)]; bt3 = [Buf() for _ in range(3)]
        ln = LNStream(f, ln1g_in[l], ln1b_in[l])
        wr_ = WRing(f, max(KC, 2 * H), 4)
        for tt in range(NTM):
            tsl = slice(tt * TM, (tt + 1) * TM)
            S_.dma(y[:], YT[:, :, tsl].rearrange("k p t -> p k t"), wr=[by])
            for i in range(KC):
                jg = i % 2
                S_.dma(gt[jg][:], GATE[:, :, tsl].rearrange("(c k) p t -> k p c t", k=KC)[i], wr=[bgt[jg]])
                for bi, (nm, koff, kn) in enumerate((("ba", 0, H), ("bb", H, 2 * H), ("bc", 3 * H, H))):
                    wt, wb = wr_.load(Wb[nm][l][i])
                    ps, pb = next_ps()
                    S_.pe_group([(lambda e, k=k: e.matmul(ps[:, 0:TM], wt[:, k * 128:(k + 1) * 128], y[:, koff + k, :], start=(k == 0), stop=(k == kn - 1))) for k in range(kn)],
                                rd=[wb, by], wr=[pb])
                    S_.op("dve", lambda e: e.tensor_tensor(out=t3[bi][:], in0=ps[:, 0:TM], in1=gt[jg][:, bi, :], op=ALU.mult), rd=[pb, bgt[jg]], wr=[bt3[bi]])
                S_.op("pool", lambda e: e.tensor_tensor(out=t3[0][:], in0=t3[0][:], in1=t3[1][:], op=ALU.add), rd=[bt3[0], bt3[1]], wr=[bt3[0]])
                S_.op("pool", lambda e: e.tensor_tensor(out=mg[:, i, :], in0=t3[0][:], in1=t3[2][:], op=ALU.add), rd=[bt3[0], bt3[2]], wr=[bmg])

            def epi(m, ps, pb):
                ln.add(m, ps, pb, tsl, True, True, False)
            gemm_tile(wr_, [Wb["out"][l][m] for m in range(KC)], lambda k: mg[:, k, :], [bmg], KC, TM, epi)
            ln.finish(tsl, False)
        S_.barrier()
        ps_ring[0] = 8
        es.close()

    def stage_ffn(l, final):
        es, f = stage_sb()
        ps_ring[0] = 6
        NH = 2 if FC >= 64 else 1
        FH = FC // NH
        x1 = f("fx1", [128, KC, TM], BF16); bx1 = Buf()
        hid = f("fhid", [128, FH, TM], BF16); bh = Buf()
        rl = [f(f"frl{i}", [128, TM]) for i in range(2)]; brl = [Buf(), Buf()]
        ln = LNStream(f, ln2g_in[l], ln2b_in[l])
        KU = min(FH, 32)
        wr_ = WRing(f, max(KC, KU), 4)
        cnt = [0]
        for tt in range(NTM):
            tsl = slice(tt * TM, (tt + 1) * TM)
            S_.dma(x1[:], HT[:, :, tsl].rearrange("k p t -> p k t"), wr=[bx1])
            for hf in range(NH):
                def epi_up(m, ps, pb):
                    i = cnt[0] % 2; cnt[0] += 1
                    S_.op("act", lambda e: e.activation(out=rl[i][:], in_=ps[:, 0:TM], func=AF.Relu), rd=[pb], wr=[brl[i]])
                    S_.op("pool", lambda e: e.tensor_tensor(out=hid[:, m, :], in0=rl[i][:], in1=rl[i][:], op=ALU.mult), rd=[brl[i]], wr=[bh])
                gemm_tile(wr_, [Wb["up"][l][hf * FH + m] for m in range(FH)], lambda k: x1[:, k, :], [bx1], KC, TM, epi_up)
                for m in range(KC):
                    ps, pb = next_ps()
                    nsu = FH // KU
                    for su in range(nsu):
                        c0 = (hf * FH + su * KU) * 128
                        wt, wb = wr_.load(Wb["down"][l][m][:, c0:c0 + KU * 128])
                        S_.pe_group([(lambda e, k=k: e.matmul(ps[:, 0:TM], wt[:, k * 128:(k + 1) * 128], hid[:, su * KU + k, :],
                                                              start=(su == 0 and k == 0), stop=(su == nsu - 1 and k == KU - 1))) for k in range(KU)],
                                    rd=[wb, bh], wr=[pb])
                    ln.add(m, ps, pb, tsl, hf == 0, hf == NH - 1, hf > 0)
            ln.finish(tsl, final)
        S_.barrier()
        ps_ring[0] = 8
        es.close()

    stop_after = cfg.get("stop_after")
    stage_x0()
    for nm, src in (("in", w_in_f), ("gate", w_gate_f), ("ba", w_ba_f), ("bb", w_bb_f), ("bc", w_bc_f),
                    ("out", w_out_f), ("up", w_up_f), ("down", w_down_f)):
        for l_ in range(DEPTH):
            stage_cast(src[l_], Wb[nm][l_])
    dbg = {}
    for l in range(DEPTH):
        stage_proj(l)
        stage_a_prep(l); stage_a_rec(l); stage_a_norm(l)
        stage_b(l)
        stage_c(l)
        stage_merge(l)
        stage_ffn(l, final=(l == DEPTH - 1))
    for nm in cfg.get("dump", ()):
        src = {"PROJ": PJA, "YT": YT, "XT": XT, "GATE": GATE, "OA": OA, "HT": HT}[nm]
        o = nc.dram_tensor("dump_" + nm, list(src.shape), src.dtype, kind="ExternalOutput").ap()
        S_.dma(o, src)
    S_.barrier()
    ES.close()
    return nc


def tile_w(W):
    K, M = W.shape
    return np.ascontiguousarray(W.reshape(K // 128, 128, M // 128, 128).transpose(2, 1, 0, 3)).reshape(M // 128, 128, (K // 128) * 128)


def host_inputs(cfg, inp, b):
    D, S, DEPTH, KC, H = cfg["D"], cfg["S"], cfg["DEPTH"], cfg["KC"], cfg["H"]
    HGW = cfg["HGW"]
    f32 = np.float32
    m = {}
    m["xT"] = np.ascontiguousarray(inp["x"][b].T).reshape(KC, 128, S)
    for nm, key in (("w_in_t", "w_in"), ("w_gate_t", "w_gate"), ("w_ba_t", "w_branch_a"), ("w_bb_t", "w_branch_b"),
                    ("w_bc_t", "w_branch_c"), ("w_out_t", "w_out"), ("w_up_t", "w_up"), ("w_down_t", "w_down")):
        m[nm] = np.stack([tile_w(np.asarray(inp[key][l])) for l in range(DEPTH)])
    pk = lambda v, n: np.ascontiguousarray(np.asarray(v).reshape(DEPTH, n, 128).transpose(0, 2, 1))
    m["b_gate_p"] = pk(inp["b_gate"], 3 * KC)
    m["ln1_g_p"] = pk(inp["ln1_g"], KC); m["ln1_b_p"] = pk(inp["ln1_b"], KC)
    m["ln2_g_p"] = pk(inp["ln2_g"], KC); m["ln2_b_p"] = pk(inp["ln2_b"], KC)
    m["lbraw_p"] = np.ascontiguousarray(np.asarray(inp["hg_lb_raw"]).reshape(DEPTH, 2, H, 128).transpose(3, 0, 1, 2)).reshape(128, DEPTH * 2 * H)
    m["hgg_p"] = pk(inp["hg_norm_g"], H)
    m["dal_p"] = np.ascontiguousarray(np.asarray(inp["da_lambda"]).reshape(DEPTH, 1, 512))
    m["dag_p"] = pk(inp["da_norm_g"], 2)
    m["sgg_b"] = np.ascontiguousarray(np.broadcast_to(np.asarray(inp["sg_norm_g"])[:, None, :], (DEPTH, 128, HGW)))
    m["sgb_b"] = np.ascontiguousarray(np.broadcast_to(np.asarray(inp["sg_norm_b"])[:, None, :], (DEPTH, 128, HGW)))
    m["sgw_t"] = np.ascontiguousarray(np.asarray(inp["sg_w_s"]).transpose(0, 3, 1, 2)).reshape(DEPTH, 128, H * 128)
    m["sgbs_b"] = np.ascontiguousarray(np.broadcast_to(np.asarray(inp["sg_b_s"]).reshape(DEPTH, 1, H * 128), (DEPTH, 128, H * 128)))
    pos = np.arange(S, dtype=f32)
    freqs = (f32(ROPE_THETA) ** (-np.arange(0, 32, 2, dtype=f32) / f32(32))).astype(f32)
    ang = (pos[:, None] * freqs[None, :]).astype(f32)
    cos = np.cos(ang).astype(f32).T; sin = np.sin(ang).astype(f32).T
    m["cosF"] = np.ascontiguousarray(np.concatenate([cos, cos], 0)); m["sinF"] = np.ascontiguousarray(np.concatenate([sin, sin], 0))
    R = np.zeros((32, 32), f32)
    for i in range(16):
        R[16 + i, i] = -1.0; R[i, 16 + i] = 1.0
    m["rotm"] = R
    m["ident"] = np.eye(128, dtype=f32)
    st = np.arange(64)
    m["trif"] = (st[:, None] <= st[None, :]).astype(f32); m["trib"] = (st[:, None] >= st[None, :]).astype(f32)
    return {k: np.ascontiguousarray(v, dtype=f32) for k, v in m.items()}


def kernel(**inputs):
    cfg = make_cfg()
    nc = build_program(cfg)
    B = cfg["B"]
    base = host_inputs(cfg, inputs, 0)
    in_maps = [base]
    for b in range(1, B):
        mb = dict(base)
        mb["xT"] = np.ascontiguousarray(np.asarray(inputs["x"][b]).T, dtype=np.float32).reshape(cfg["KC"], 128, cfg["S"])
        in_maps.append(mb)
    res = run_bass_kernel_spmd(nc, in_maps, core_ids=list(range(B)))
    out = np.stack([np.ascontiguousarray(res.results[b]["yT"].reshape(cfg["D"], cfg["S"]).T) for b in range(B)])
    return out.astype(np.float32)
```

```python
import math
import numpy as np
import concourse.bass as bass
import concourse.mybir as mybir
from concourse.bass_utils import run_bass_kernel_spmd

F32 = mybir.dt.float32
BF16 = mybir.dt.bfloat16
AF = mybir.ActivationFunctionType
ALU = mybir.AluOpType

EPS = 1e-5
ROPE_THETA = 500000.0


def make_cfg(D=4096, S=8192, DEPTH=2, B=2):
    c = dict(D=D, S=S, DEPTH=DEPTH, B=B)
    c["KC"] = D // 128
    c["HGW"] = D // 4
    c["H"] = c["HGW"] // 128
    c["DFF"] = 4 * D
    c["DIN"] = 5 * c["HGW"] + 3 * (2 * c["H"] * 128) + 2 * c["HGW"]
    c["ALPHA"] = (2.0 * DEPTH) ** 0.25
    return c


class Buf:
    __slots__ = ("w", "r", "name")

    def __init__(self, name=""):
        self.w = {}
        self.r = {}
        self.name = name


class Sch:
    SEM_CAP = 30000

    def __init__(self, nc):
        self.nc = nc
        self.eng = {"pe": nc.tensor, "act": nc.scalar, "dve": nc.vector, "pool": nc.gpsimd, "sp": nc.sync}
        self.cur = {}
        self.seen = {e: {} for e in self.eng}
        self.nsem = 0
        self.all_tokens = {}
        self.snap = {}
        self.dma_ring = []
        self.dma_next = 0
        self.NDMA = 12
        for e in ("pe", "act", "dve", "pool"):
            self.cur[e] = [self._newsem(), 0]
        for i in range(self.NDMA):
            self.dma_ring.append([self._newsem(), 0])

    def _newsem(self):
        self.nsem += 1
        return self.nc.alloc_semaphore(f"sm{self.nsem}")

    def _wait(self, e, tok):
        sem, val = tok
        k = id(sem)
        se = self.seen[e]
        if se.get(k, 0) >= val:
            return
        self.eng[e].wait_ge(sem, val)
        se[k] = val
        sn = self.snap.get((k, val))
        if sn:
            for k2, v2 in sn.items():
                if se.get(k2, 0) < v2:
                    se[k2] = v2

    def _deps(self, e, rd, wr):
        toks = {}
        for b in rd:
            for k, t in b.w.items():
                if toks.get(k, (None, 0))[1] < t[1]:
                    toks[k] = t
        for b in wr:
            for d in (b.w, b.r):
                for k, t in d.items():
                    if toks.get(k, (None, 0))[1] < t[1]:
                        toks[k] = t
        for t in toks.values():
            self._wait(e, t)

    def _mark(self, tok, rd, wr, e=None):
        k = id(tok[0])
        if e is not None:
            self.snap[(k, tok[1])] = dict(self.seen[e])
        self.all_tokens[k] = tok
        for b in wr:
            b.w[k] = tok
            b.r = {}
        for b in rd:
            b.r[k] = tok

    def op(self, e, fn, rd=(), wr=()):
        self._deps(e, rd, wr)
        cur = self.cur[e]
        if cur[1] >= self.SEM_CAP:
            cur[0] = self._newsem()
            cur[1] = 0
        cur[1] += 1
        tok = (cur[0], cur[1])
        fn(self.eng[e]).then_inc(cur[0], 1)
        self._mark(tok, rd, wr, e)
        return tok

    def pe_group(self, fns, rd=(), wr=()):
        self._deps("pe", rd, wr)
        cur = self.cur["pe"]
        if cur[1] >= self.SEM_CAP:
            cur[0] = self._newsem()
            cur[1] = 0
        ins = None
        for fn in fns:
            ins = fn(self.nc.tensor)
        cur[1] += 1
        tok = (cur[0], cur[1])
        ins.then_inc(cur[0], 1)
        self._mark(tok, rd, wr, "pe")
        return tok

    def dma(self, out, in_, rd=(), wr=(), e="sp"):
        self._deps(e, rd, wr)
        slot = self.dma_ring[self.dma_next]
        self.dma_next = (self.dma_next + 1) % self.NDMA
        if slot[1] + 16 > self.SEM_CAP:
            slot[0] = self._newsem()
            slot[1] = 0
        if slot[1] > 0:
            self._wait(e, (slot[0], slot[1]))
        slot[1] += 16
        tok = (slot[0], slot[1])
        self.eng[e].dma_start(out=out, in_=in_).then_inc(slot[0], 16)
        self._mark(tok, rd, wr, e)
        return tok

    def dma_st(self, out, in_, rd=(), wr=()):
        return self.dma(out, in_, rd=rd, wr=wr, e="act")

    def barrier(self):
        toks = list(self.all_tokens.values())
        for e in self.eng:
            for t in toks:
                self._wait(e, t)


def build_program(cfg, debug_outputs=()):
    D, S, DEPTH, KC, H, DFF, DIN = (cfg[k] for k in ("D", "S", "DEPTH", "KC", "H", "DFF", "DIN"))
    T = S
    ALPHA = cfg["ALPHA"]
    HGW = cfg["HGW"]
    NPROJ = DIN // 128
    NGATE = 3 * KC
    FC = DFF // 128
    TT = 512 if T >= 512 else T
    NT = T // TT
    TM = 512 if T >= 512 else T
    NTM = T // TM
    NCH = T // 64
    NB128 = T // 128

    nc = bass.Bass("TRN2", target_bir_lowering=False)
    S_ = Sch(nc)

    def din(name, shape, dt=F32):
        return nc.dram_tensor(name, list(shape), dt, kind="ExternalInput").ap()

    def dscr(name, shape, dt=F32):
        return nc.dram_tensor(name, list(shape), dt).ap()

    xT_in = din("xT", [KC, 128, T])
    w_in_f = din("w_in_t", [DEPTH, NPROJ, 128, KC * 128])
    w_gate_f = din("w_gate_t", [DEPTH, NGATE, 128, KC * 128])
    w_ba_f = din("w_ba_t", [DEPTH, KC, 128, H * 128])
    w_bb_f = din("w_bb_t", [DEPTH, KC, 128, 2 * H * 128])
    w_bc_f = din("w_bc_t", [DEPTH, KC, 128, H * 128])
    w_out_f = din("w_out_t", [DEPTH, KC, 128, KC * 128])
    w_up_f = din("w_up_t", [DEPTH, FC, 128, KC * 128])
    w_down_f = din("w_down_t", [DEPTH, KC, 128, FC * 128])
    b_gate_in = din("b_gate_p", [DEPTH, 128, NGATE])
    ln1g_in = din("ln1_g_p", [DEPTH, 128, KC]); ln1b_in = din("ln1_b_p", [DEPTH, 128, KC])
    ln2g_in = din("ln2_g_p", [DEPTH, 128, KC]); ln2b_in = din("ln2_b_p", [DEPTH, 128, KC])
    lbraw_in = din("lbraw_p", [128, DEPTH * 2 * H])
    hgg_in = din("hgg_p", [DEPTH, 128, H])
    dal_in = din("dal_p", [DEPTH, 1, 512])
    dag_in = din("dag_p", [DEPTH, 128, 2])
    sgg_in = din("sgg_b", [DEPTH, 128, HGW]); sgb_in = din("sgb_b", [DEPTH, 128, HGW])
    sgw_in = din("sgw_t", [DEPTH, 128, H * 128])
    sgbs_in = din("sgbs_b", [DEPTH, 128, H * 128])
    cos_in = din("cosF", [32, T]); sin_in = din("sinF", [32, T])
    rot_in = din("rotm", [32, 32])
    ident_in = din("ident", [128, 128])
    trif_in = din("trif", [64, 64]); trib_in = din("trib", [64, 64])

    yT_out = nc.dram_tensor("yT", [KC, 128, T], F32, kind="ExternalOutput").ap()

    Wb = {}
    for nm, src in (("in", w_in_f), ("gate", w_gate_f), ("ba", w_ba_f), ("bb", w_bb_f), ("bc", w_bc_f),
                    ("out", w_out_f), ("up", w_up_f), ("down", w_down_f)):
        Wb[nm] = [dscr(f"wb_{nm}{l_}", src.shape[1:], BF16) for l_ in range(DEPTH)]
    XT = dscr("XT", [KC, 128, T])
    HT = dscr("HT", [KC, 128, T], BF16)
    PJA = dscr("PJA", [5 * H, 128, T]); PJB = dscr("PJB", [6 * H, 128, T]); PJC = dscr("PJC", [2 * H, 128, T])

    class _Proj:
        def __getitem__(self, idx):
            c = idx[0]
            c0 = c.start if isinstance(c, slice) else c
            if c0 < 5 * H:
                t_, off = PJA, 0
            elif c0 < 11 * H:
                t_, off = PJB, 5 * H
            else:
                t_, off = PJC, 11 * H
            if isinstance(c, slice):
                return t_[(slice(c.start - off, c.stop - off),) + tuple(idx[1:])]
            return t_[(c - off,) + tuple(idx[1:])]
    PROJ = _Proj()
    GATE = dscr("GATE", [NGATE, 128, T], BF16)
    YT = dscr("YT", [KC, 128, T], BF16)
    AQM = dscr("AQM", [2, H, 128, T], BF16); AKM = dscr("AKM", [2, H, 128, T], BF16)
    AQH = dscr("AQH", [2, H, 128, T], BF16); AKH = dscr("AKH", [2, H, 128, T], BF16)
    AV = dscr("AV", [H, 128, T], BF16)
    ADEC = dscr("ADEC", [2, H, 128, NCH])
    OA = dscr("OA", [2, H, T, 128])
    BQK = dscr("BQK", [4, 128, T], BF16)

    o_aq, o_aff, o_afb, o_ai, o_ag = 0, H, 2 * H, 3 * H, 4 * H
    o_bq, o_bk, o_bv = 5 * H, 7 * H, 9 * H
    o_cu, o_cv = 11 * H, 12 * H

    from contextlib import ExitStack
    ES = ExitStack()

    uid = [0]

    def uname(name):
        uid[0] += 1
        return f"sb{uid[0]}_{name}"

    def sb(name, shape, dt=F32):
        return ES.enter_context(nc.sbuf_tensor(uname(name), list(shape), dt))

    ident_f = sb("ident_f", [128, 128]); ident_b = sb("ident_b", [128, 128], BF16)
    trif = sb("trif", [64, 64]); trib = sb("trib", [64, 64])
    rotm = sb("rotm", [32, 32])
    ones_f = sb("ones_f", [128, 128])
    ones_b = sb("ones_b", [128, 128], BF16)
    lbt = sb("lbt", [128, DEPTH * 2 * H]); omlt = sb("omlt", [128, DEPTH * 2 * H])
    B_const = Buf("const")

    psum = [ES.enter_context(nc.psum_tensor(f"ps{i}", [128, 512], F32)) for i in range(8)]
    PB = [Buf(f"ps{i}") for i in range(8)]

    S_.dma(ident_f[:], ident_in, wr=[B_const])
    S_.dma(trif[:], trif_in, wr=[B_const])
    S_.dma(trib[:], trib_in, wr=[B_const])
    S_.dma(rotm[:], rot_in, wr=[B_const])
    S_.dma(lbt[:], lbraw_in, wr=[B_const])
    S_.barrier()
    S_.op("dve", lambda e: e.tensor_copy(out=ident_b[:], in_=ident_f[:]), rd=[B_const], wr=[B_const])
    S_.op("dve", lambda e: e.memset(ones_f[:], 1.0), wr=[B_const])
    S_.op("dve", lambda e: e.memset(ones_b[:], 1.0), wr=[B_const])
    NL = 2 * H
    S_.op("act", lambda e: e.activation(out=lbt[:], in_=lbt[:], func=AF.Exp), rd=[B_const], wr=[B_const])
    zsum = sb("zsum", [128, NL]); zrec = sb("zrec", [128, NL])
    S_.op("dve", lambda e: e.tensor_copy(out=zsum[:], in_=lbt[:, 0:NL]), rd=[B_const], wr=[B_const])
    for l in range(1, DEPTH):
        S_.op("dve", lambda e, l=l: e.tensor_tensor(out=zsum[:], in0=zsum[:], in1=lbt[:, l * NL:(l + 1) * NL], op=ALU.add),
              rd=[B_const], wr=[B_const])
    S_.op("dve", lambda e: e.reciprocal(out=zrec[:], in_=zsum[:]), rd=[B_const], wr=[B_const])
    S_.op("dve", lambda e: e.memset(lbt[:, 0:NL], 0.0), rd=[B_const], wr=[B_const])
    for l in range(2, DEPTH):
        S_.op("dve", lambda e, l=l: e.tensor_tensor(out=lbt[:, l * NL:(l + 1) * NL], in0=lbt[:, l * NL:(l + 1) * NL],
                                                    in1=lbt[:, (l - 1) * NL:l * NL], op=ALU.add), rd=[B_const], wr=[B_const])
    for l in range(DEPTH):
        S_.op("dve", lambda e, l=l: e.tensor_tensor(out=lbt[:, l * NL:(l + 1) * NL], in0=lbt[:, l * NL:(l + 1) * NL],
                                                    in1=zrec[:], op=ALU.mult), rd=[B_const], wr=[B_const])
    S_.op("dve", lambda e: e.tensor_scalar(out=omlt[:], in0=lbt[:], scalar1=-1.0, scalar2=1.0, op0=ALU.mult, op1=ALU.add),
          rd=[B_const], wr=[B_const])
    S_.barrier()

    def stage_sb():
        es = ExitStack()
        def f(name, shape, dt=F32):
            return es.enter_context(nc.sbuf_tensor(uname(name), list(shape), dt))
        return es, f

    ENG3 = ("dve", "pool", "act")

    def rsqrt(ap, b):
        S_.op("act", lambda e: e.activation(out=ap, in_=ap, func=AF.Sqrt), rd=[b], wr=[b])
        S_.op("dve", lambda e: e.reciprocal(out=ap, in_=ap), rd=[b], wr=[b])

    def copy_op(en, out, in_, rd, wr):
        if en == "act":
            return S_.op("act", lambda e: e.activation(out=out, in_=in_, func=AF.Copy), rd=rd, wr=wr)
        return S_.op(en, lambda e: e.tensor_copy(out=out, in_=in_), rd=rd, wr=wr)

    def stage_cast(src, dst):
        es, f = stage_sb()
        s2 = src
        d2 = dst
        n, _, fsz = s2.shape
        CH = min(4096, fsz)
        NBUF = 3
        tin = [f(f"ci{i}", [128, CH]) for i in range(NBUF)]
        tout = [f(f"co{i}", [128, CH], BF16) for i in range(NBUF)]
        bi = [Buf() for _ in range(NBUF)]; bo = [Buf() for _ in range(NBUF)]
        k = 0
        for i in range(n):
            for c in range(fsz // CH):
                j = k % NBUF
                S_.dma(tin[j][:], s2[i, :, c * CH:(c + 1) * CH], wr=[bi[j]])
                copy_op(ENG3[k % 3], tout[j][:], tin[j][:], [bi[j]], [bo[j]])
                S_.dma_st(d2[i, :, c * CH:(c + 1) * CH], tout[j][:], rd=[bo[j]])
                k += 1
        S_.barrier()
        es.close()

    def stage_x0():
        es, f = stage_sb()
        CH = min(2048, T)
        tin = [f(f"xi{i}", [128, CH]) for i in range(2)]
        tout = [f(f"xo{i}", [128, CH], BF16) for i in range(2)]
        bi = [Buf(), Buf()]; bo = [Buf(), Buf()]
        k = 0
        for kc in range(KC):
            for c in range(T // CH):
                j = k % 2
                S_.dma(tin[j][:], xT_in[kc, :, c * CH:(c + 1) * CH], wr=[bi[j]])
                copy_op(ENG3[k % 2], tout[j][:], tin[j][:], [bi[j]], [bo[j]])
                S_.dma_st(XT[kc, :, c * CH:(c + 1) * CH], tin[j][:], rd=[bi[j]])
                S_.dma_st(HT[kc, :, c * CH:(c + 1) * CH], tout[j][:], rd=[bo[j]])
                k += 1
        S_.barrier()
        es.close()

    class WRing:
        def __init__(self, f, kcw, nbuf=4, tag="w"):
            self.t = [f(f"{tag}{i}", [128, kcw * 128], BF16) for i in range(nbuf)]
            self.b = [Buf() for _ in range(nbuf)]
            self.k = 0
            self.n = nbuf

        def load(self, src):
            j = self.k % self.n
            self.k += 1
            S_.dma(self.t[j][:, 0:src.shape[-1]], src, wr=[self.b[j]])
            return self.t[j], self.b[j]

    psk = [0]

    ps_ring = [8]

    def next_ps(n=1):
        j = psk[0] % ps_ring[0]
        psk[0] += 1
        return psum[j], PB[j]

    def gemm_tile(wring, wsrcs, act_fn, act_bufs, kcs, N, epi):
        for m, src in enumerate(wsrcs):
            wt, wb = wring.load(src)
            ps, pb = next_ps()
            fns = []
            for k in range(kcs):
                fns.append(lambda e, k=k, wt=wt, ps=ps: e.matmul(ps[:, 0:N], wt[:, k * 128:(k + 1) * 128], act_fn(k),
                                                                 start=(k == 0), stop=(k == kcs - 1)))
            S_.pe_group(fns, rd=[wb] + list(act_bufs), wr=[pb])
            epi(m, ps, pb)

    def stage_proj(l):
        es, f = stage_sb()
        ht = [f(f"ht{i}", [128, KC, TT], BF16) for i in range(2)]
        hb = [Buf(), Buf()]
        bg = f("bg", [128, NGATE])
        bgb = Buf()
        S_.dma(bg[:], b_gate_in[l], wr=[bgb])
        wr_ = WRing(f, KC, 4)
        ot = [f(f"ot{i}", [128, TT]) for i in range(4)]
        otb = [f(f"otb{i}", [128, TT], BF16) for i in range(4)]
        ob = [Buf() for _ in range(4)]
        cnt = [0]
        for tt in range(NT):
            j = tt % 2
            tsl = slice(tt * TT, (tt + 1) * TT)
            S_.dma(ht[j][:], HT[:, :, tsl].rearrange("k p t -> p k t"), wr=[hb[j]])

            def epi_proj(m, ps, pb, tsl=tsl):
                i = cnt[0] % 4
                cnt[0] += 1
                copy_op(("dve", "act")[cnt[0] % 2], ot[i][:], ps[:, 0:TT], [pb], [ob[i]])
                S_.dma_st(PROJ[m, :, tsl], ot[i][:], rd=[ob[i]])

            def epi_gate(m, ps, pb, tsl=tsl):
                i = cnt[0] % 4
                cnt[0] += 1
                S_.op("act", lambda e: e.activation(out=otb[i][:], in_=ps[:, 0:TT], func=AF.Sigmoid, bias=bg[:, m:m + 1], scale=1.0),
                      rd=[pb, bgb], wr=[ob[i]])
                S_.dma_st(GATE[m, :, tsl], otb[i][:], rd=[ob[i]])

            act = lambda k, j=j: ht[j][:, k, :]
            gemm_tile(wr_, [Wb["in"][l][m] for m in range(NPROJ)], act, [hb[j]], KC, TT, epi_proj)
            gemm_tile(wr_, [Wb["gate"][l][m] for m in range(NGATE)], act, [hb[j]], KC, TT, epi_gate)
        S_.barrier()
        es.close()

    def stage_a_prep(l):
        es, f = stage_sb()
        TS = min(2048, T)
        NCS = TS // 64
        q = f("aq", [128, TS]); a = f("aa", [128, TS]); kk = f("akk", [128, TS])
        pe_ = f("ape", [128, TS + 64]); onesr = f("aones", [128, TS])
        arg = [f(f"aarg{i}", [128, TS]) for i in range(2)]
        ex = [f(f"aex{i}", [128, TS]) for i in range(2)]
        outb = [f(f"aob{i}", [128, TS], BF16) for i in range(2)]
        dec = f("adec", [128, NCS]); dec2 = f("adec2", [128, NCS])
        bq, ba, bk, bpe, bones, bdec = Buf(), Buf(), Buf(), Buf(), Buf(), Buf()
        barg = [Buf(), Buf()]; bex = [Buf(), Buf()]; bob = [Buf(), Buf()]
        S_.op("pool", lambda e: e.memset(onesr[:], 1.0), wr=[bones])
        S_.op("pool", lambda e: e.memset(pe_[:], 0.0), wr=[bpe])
        v3 = lambda ap: ap.rearrange("p (c t) -> p c t", t=64)
        cnt = [0]
        for h in range(H):
            for sg in range(T // TS):
                tsl = slice(sg * TS, (sg + 1) * TS)
                S_.dma(q[:], PROJ[o_aq + h, :, tsl], wr=[bq])
                S_.dma(a[:], PROJ[o_ai + h, :, tsl], wr=[ba])
                i0 = cnt[0] % 2; cnt[0] += 1
                S_.op("pool", lambda e, i0=i0: e.tensor_copy(out=outb[i0][:], in_=a[:]), rd=[ba], wr=[bob[i0]])
                S_.dma_st(AV[h, :, tsl], outb[i0][:], rd=[bob[i0]])
                for dr in range(2):
                    col = l * 2 * H + dr * H + h
                    S_.dma(a[:], PROJ[(o_aff if dr == 0 else o_afb) + h, :, tsl], wr=[ba])
                    S_.op("act", lambda e: e.activation(out=a[:], in_=a[:], func=AF.Sigmoid), rd=[ba], wr=[ba])
                    S_.op("dve", lambda e, col=col: e.tensor_scalar(out=a[:], in0=a[:], scalar1=omlt[:, col:col + 1], scalar2=lbt[:, col:col + 1],
                                                                    op0=ALU.mult, op1=ALU.add), rd=[ba, B_const], wr=[ba])
                    S_.op("pool", lambda e: e.tensor_scalar(out=kk[:], in0=a[:], scalar1=-1.0, scalar2=1.0, op0=ALU.mult, op1=ALU.add),
                          rd=[ba], wr=[bk])
                    S_.op("act", lambda e: e.activation(out=a[:], in_=a[:], func=AF.Ln), rd=[ba], wr=[ba])
                    S_.op("dve", lambda e: e.tensor_tensor_scan(out=pe_[:, 1:TS + 1], data0=onesr[:], data1=a[:], initial=0.0,
                                                                op0=ALU.mult, op1=ALU.add), rd=[ba, bones], wr=[bpe])
                    PF = v3(pe_[:, 1:TS + 1]) if dr == 0 else v3(pe_[:, 0:TS])
                    base = v3(pe_[:, 0:TS])
                    Rmid = base[:, :, 32:33]; Rcs = base[:, :, 0:1]; Rce = v3(pe_[:, 64:TS + 64])[:, :, 0:1]
                    bc = lambda r: r.broadcast_to([128, NCS, 64])
                    if dr == 0:
                        specs = [(PF, bc(Rmid), q, AQM), (bc(Rmid), PF, kk, AKM), (PF, bc(Rcs), q, AQH), (bc(Rce), PF, kk, AKH)]
                    else:
                        specs = [(bc(Rmid), PF, q, AQM), (PF, bc(Rmid), kk, AKM), (bc(Rce), PF, q, AQH), (PF, bc(Rcs), kk, AKH)]
                    for (x0, x1, mul, dst) in specs:
                        i = cnt[0] % 2; cnt[0] += 1
                        S_.op("dve", lambda e, i=i, x0=x0, x1=x1: e.tensor_tensor(out=v3(arg[i][:]), in0=x0, in1=x1, op=ALU.subtract),
                              rd=[bpe], wr=[barg[i]])
                        S_.op("act", lambda e, i=i: e.activation(out=ex[i][:], in_=arg[i][:], func=AF.Exp), rd=[barg[i]], wr=[bex[i]])
                        mb = bq if mul is q else bk
                        S_.op("pool", lambda e, i=i, mul=mul: e.tensor_tensor(out=outb[i][:], in0=ex[i][:], in1=mul[:], op=ALU.mult),
                              rd=[bex[i], mb], wr=[bob[i]])
                        S_.dma_st(dst[dr, h, :, tsl], outb[i][:], rd=[bob[i]])
                    S_.op("dve", lambda e: e.tensor_tensor(out=dec[:].unsqueeze(2), in0=Rce, in1=Rcs, op=ALU.subtract), rd=[bpe], wr=[bdec])
                    S_.op("act", lambda e: e.activation(out=dec2[:], in_=dec[:], func=AF.Exp), rd=[bdec], wr=[bdec])
                    S_.dma_st(ADEC[dr, h, :, sg * NCS:(sg + 1) * NCS], dec2[:], rd=[bdec])
        S_.barrier()
        es.close()

    def stage_a_rec(l):
        es, f = stage_sb()
        GT = min(512, T)
        GC = GT // 64
        NG = T // GT
        names = ("qm", "km", "qh", "kh", "v")
        tl = [[[f(f"r{n}{d}{i}", [128, GT], BF16) for n in names] for i in range(2)] for d in range(2)]
        tb = [[Buf() for i in range(2)] for d in range(2)]
        dec = [f(f"rdec{d}", [128, NCH]) for d in range(2)]; bdec = [Buf(), Buf()]
        st = [f(f"rst{d}", [128, 128]) for d in range(2)]; stb = [f(f"rstb{d}", [128, 128], BF16) for d in range(2)]
        bst = [Buf(), Buf()]; bstb = [Buf(), Buf()]
        khv = [[f(f"rkhv{d}{i}", [64, 256], BF16) for i in range(2)] for d in range(2)]
        bkhv = [[Buf(), Buf()] for d in range(2)]
        pT = [[f(f"rpT{d}{i}", [64, 64], BF16) for i in range(2)] for d in range(2)]
        bpT = [[Buf(), Buf()] for d in range(2)]
        ost = [[f(f"rost{d}{i}", [64, GC, 128]) for i in range(2)] for d in range(2)]
        bost = [[Buf(), Buf()] for d in range(2)]
        srcs = (AQM, AKM, AQH, AKH)
        for h in range(H):
            for d in range(2):
                S_.dma(dec[d][:], ADEC[d, h], wr=[bdec[d]])
                S_.op("dve", lambda e, d=d: e.memset(st[d][:], 0.0), wr=[bst[d]])
                S_.op("pool", lambda e, d=d: e.memset(stb[d][:], 0.0), wr=[bstb[d]])
            for gi in range(NG):
                for d in range(2):
                    g = gi if d == 0 else NG - 1 - gi
                    j = gi % 2
                    tsl = slice(g * GT, (g + 1) * GT)
                    for n in range(4):
                        S_.dma(tl[d][j][n][:], srcs[n][d, h, :, tsl], wr=[tb[d][j]])
                    S_.dma(tl[d][j][4][:], AV[h, :, tsl], wr=[tb[d][j]])
                for ci in range(GC):
                    for d in range(2):
                        g = gi if d == 0 else NG - 1 - gi
                        j = gi % 2
                        cl = ci if d == 0 else GC - 1 - ci
                        c = g * GC + cl
                        lo = cl * 64
                        qm, km, qh, kh, vv = tl[d][j]
                        i2 = (gi * GC + ci) % 2
                        ps1, pb1 = next_ps()
                        S_.pe_group([lambda e: e.matmul(ps1[0:64, 0:128], kh[:, lo:lo + 64], ident_b[:], start=True, stop=True),
                                     lambda e: e.matmul(ps1[0:64, 128:256], vv[:, lo:lo + 64], ident_b[:], start=True, stop=True)],
                                    rd=[tb[d][j], B_const], wr=[pb1])
                        S_.op("act", lambda e: e.activation(out=khv[d][i2][:], in_=ps1[0:64, 0:256], func=AF.Copy), rd=[pb1], wr=[bkhv[d][i2]])
                        ps2, pb2 = next_ps()
                        S_.pe_group([lambda e: e.matmul(ps2[0:64, 0:64], km[:, lo:lo + 64], qm[:, lo:lo + 64], start=True, stop=True)],
                                    rd=[tb[d][j]], wr=[pb2])
                        msk = trif if d == 0 else trib
                        S_.op("dve", lambda e: e.tensor_tensor(out=pT[d][i2][:], in0=ps2[0:64, 0:64], in1=msk[:], op=ALU.mult),
                              rd=[pb2, B_const], wr=[bpT[d][i2]])
                        ps3, pb3 = next_ps()
                        S_.pe_group([lambda e: e.matmul(ps3[0:64, 0:128], pT[d][i2][:], khv[d][i2][:, 128:256], start=True, stop=False),
                                     lambda e: e.matmul(ps3[0:64, 0:128], qh[:, lo:lo + 64], stb[d][:], start=False, stop=True)],
                                    rd=[bpT[d][i2], bkhv[d][i2], tb[d][j], bstb[d]], wr=[pb3])
                        S_.op("act", lambda e: e.activation(out=ost[d][j][:, cl, :], in_=ps3[0:64, 0:128], func=AF.Copy), rd=[pb3], wr=[bost[d][j]])
                        ps4, pb4 = next_ps()
                        S_.pe_group([lambda e: e.matmul(ps4[:, 0:128], khv[d][i2][:, 0:128], khv[d][i2][:, 128:256], start=True, stop=True)],
                                    rd=[bkhv[d][i2]], wr=[pb4])
                        S_.op("dve", lambda e: e.scalar_tensor_tensor(out=st[d][:], in0=st[d][:], scalar=dec[d][:, c:c + 1], in1=ps4[:, 0:128],
                                                                       op0=ALU.mult, op1=ALU.add), rd=[pb4, bdec[d], bst[d]], wr=[bst[d]])
                        S_.op("pool", lambda e: e.tensor_copy(out=stb[d][:], in_=st[d][:]), rd=[bst[d]], wr=[bstb[d]])
                for d in range(2):
                    g = gi if d == 0 else NG - 1 - gi
                    j = gi % 2
                    S_.dma_st(OA[d, h, g * GT:(g + 1) * GT, :].rearrange("(c t) v -> t c v", t=64), ost[d][j][:], rd=[bost[d][j]])
        S_.barrier()
        es.close()

    def stage_a_norm(l):
        es, f = stage_sb()
        NB = TT // 128
        of = [f(f"nof{i}", [128, NB, 128]) for i in range(2)]; ob_ = [f(f"nob{i}", [128, NB, 128]) for i in range(2)]
        sq = f("nsq", [128, NB, 128]); ss = f("nss", [128, NB]); on = [f(f"non{i}", [128, NB, 128]) for i in range(2)]
        sg_ = [f(f"nsg{i}", [128, TT]) for i in range(2)]; yo = [f(f"nyo{i}", [128, TT], BF16) for i in range(2)]
        hg = f("nhg", [128, H])
        bof = [Buf(), Buf()]; bsq, bss = Buf(), Buf(); bon = [Buf(), Buf()]; bsg = [Buf(), Buf()]; byo = [Buf(), Buf()]; bhg = Buf()
        S_.dma(hg[:], hgg_in[l], wr=[bhg])
        k = 0
        for h in range(H):
            for tt in range(NT):
                j = k % 2; k += 1
                tsl = slice(tt * TT, (tt + 1) * TT)
                S_.dma(of[j][:], OA[0, h, tsl, :].rearrange("(b p) v -> p b v", p=128), wr=[bof[j]])
                S_.dma(ob_[j][:], OA[1, h, tsl, :].rearrange("(b p) v -> p b v", p=128), wr=[bof[j]])
                S_.dma(sg_[j][:], PROJ[o_ag + h, :, tsl], wr=[bsg[j]])
                S_.op("act", lambda e: e.activation(out=sg_[j][:], in_=sg_[j][:], func=AF.Silu), rd=[bsg[j]], wr=[bsg[j]])
                S_.op("pool", lambda e: e.tensor_tensor(out=of[j][:], in0=of[j][:], in1=ob_[j][:], op=ALU.add), rd=[bof[j]], wr=[bof[j]])
                S_.op("pool", lambda e: e.tensor_tensor(out=sq[:], in0=of[j][:], in1=of[j][:], op=ALU.mult), rd=[bof[j]], wr=[bsq])
                S_.op("dve", lambda e: e.tensor_reduce(out=ss[:], in_=sq[:], axis=mybir.AxisListType.X, op=ALU.add), rd=[bsq], wr=[bss])
                S_.op("dve", lambda e: e.tensor_scalar(out=ss[:], in0=ss[:], scalar1=1.0 / 128, scalar2=EPS, op0=ALU.mult, op1=ALU.add), rd=[bss], wr=[bss])
                rsqrt(ss[:], bss)
                S_.op("dve", lambda e: e.tensor_tensor(out=on[j][:], in0=of[j][:], in1=ss[:].unsqueeze(2).broadcast_to([128, NB, 128]), op=ALU.mult),
                      rd=[bof[j], bss], wr=[bon[j]])
                ps, pb = next_ps()
                S_.pe_group([(lambda e, b=b: e.matmul(ps[:, b * 128:(b + 1) * 128], on[j][:, b, :], ident_f[:], start=True, stop=True)) for b in range(NB)],
                            rd=[bon[j], B_const], wr=[pb])
                S_.op("dve", lambda e: e.scalar_tensor_tensor(out=yo[j][:], in0=ps[:, 0:TT], scalar=hg[:, h:h + 1], in1=sg_[j][:], op0=ALU.mult, op1=ALU.mult),
                      rd=[pb, bhg, bsg[j]], wr=[byo[j]])
                S_.dma_st(YT[h, :, tsl], yo[j][:], rd=[byo[j]])
        S_.barrier()
        es.close()

    def next_ps_lo():
        j = psk[0] % 4
        psk[0] += 1
        return psum[j], PB[j]

    def stage_b(l):
        es, f = stage_sb()
        lam_init = 0.8 - 0.6 * math.exp(-0.3 * l)
        scale = 128.0 ** -0.5
        NB = TT // 128
        dal = f("bdal", [1, 512]); pr = f("bpr", [1, 256]); s2 = f("bs2", [1, 2]); lam1 = f("blam1", [1, 1])
        lamb = f("blamb", [128, 1]); dag = f("bdag", [128, 2])
        bl = Buf()
        S_.dma(dal[:], dal_in[l], wr=[bl])
        S_.dma(dag[:], dag_in[l], wr=[bl])
        d4 = dal[:].rearrange("p (a d) -> p a d", d=128)
        S_.op("dve", lambda e: e.tensor_tensor(out=pr[:].rearrange("p (a d) -> p a d", d=128), in0=d4[:, 0::2, :], in1=d4[:, 1::2, :], op=ALU.mult), rd=[bl], wr=[bl])
        S_.op("dve", lambda e: e.tensor_reduce(out=s2[:], in_=pr[:].rearrange("p (a d) -> p a d", d=128), axis=mybir.AxisListType.X, op=ALU.add), rd=[bl], wr=[bl])
        S_.op("act", lambda e: e.activation(out=s2[:], in_=s2[:], func=AF.Exp), rd=[bl], wr=[bl])
        S_.op("dve", lambda e: e.tensor_tensor(out=lam1[:], in0=s2[:, 0:1], in1=s2[:, 1:2], op=ALU.subtract), rd=[bl], wr=[bl])
        S_.op("dve", lambda e: e.tensor_scalar(out=lam1[:], in0=lam1[:], scalar1=lam_init, scalar2=None, op0=ALU.add), rd=[bl], wr=[bl])
        ps, pb = next_ps()
        S_.pe_group([lambda e: e.matmul(ps[:, 0:1], ones_f[0:1, :], lam1[:], start=True, stop=True)], rd=[bl, B_const], wr=[pb])
        S_.op("dve", lambda e: e.tensor_copy(out=lamb[:], in_=ps[:, 0:1]), rd=[pb], wr=[bl])
        S_.op("dve", lambda e: e.tensor_scalar(out=dag[:], in0=dag[:], scalar1=(1.0 - lam_init), scalar2=None, op0=ALU.mult), rd=[bl], wr=[bl])

        xin = [f(f"bx{i}", [128, TT]) for i in range(2)]; bx = [Buf(), Buf()]
        cs_ = [f(f"bcs{i}", [32, 2, TT]) for i in range(2)]; bcs = [Buf(), Buf()]
        t1 = f("bt1", [32, TT]); t2 = f("bt2", [32, TT]); bt = Buf()
        xb = [f(f"bxb{i}", [128, TT], BF16) for i in range(2)]; bxb = [Buf(), Buf()]
        v0 = [f(f"bv0{i}", [128, TT]) for i in range(2)]; v1_ = [f(f"bv1{i}", [128, TT]) for i in range(2)]; bv = [Buf(), Buf()]
        V1 = f("bV1", [128, NB128, 257], BF16); bV1 = Buf()
        KT = [f(f"bKT{c}", [128, T], BF16) for c in range(2)]; bKT = Buf()
        QT = [f(f"bQT{i}", [128, TT], BF16) for i in range(2)]; bQT = [Buf(), Buf()]
        pT = [f(f"bpT{i}", [128, TT], BF16) for i in range(3)]; bpT = [Buf() for _ in range(3)]
        Oc = [f(f"bOc{c}", [128, NB, 257]) for c in range(2)]; bOc = [Buf(), Buf()]
        rz = f("brz", [128, NB, 2]); on = f("bon", [128, NB, 256]); sq = f("bsq", [128, NB, 256]); ss = f("bss", [128, NB])
        bo = Buf()
        yo = [f(f"byo{i}", [128, TT], BF16) for i in range(2)]; byo = [Buf(), Buf()]
        PBacc = Buf()
        S_.op("pool", lambda e: e.memset(V1[:], 1.0), wr=[bV1])
        k = 0
        for h in range(H):
            for idx in range(4):
                chunk = (o_bq if idx < 2 else o_bk) + 2 * h + (idx % 2)
                for tt in range(NT):
                    j = k % 2; k += 1
                    tsl = slice(tt * TT, (tt + 1) * TT)
                    S_.dma(xin[j][:], PROJ[chunk, :, tsl], wr=[bx[j]])
                    S_.dma(cs_[j][:, 0, :], cos_in[:, tsl], wr=[bcs[j]])
                    S_.dma(cs_[j][:, 1, :], sin_in[:, tsl], wr=[bcs[j]])
                    ps, pb = next_ps()
                    S_.pe_group([lambda e: e.matmul(ps[0:32, 0:TT], rotm[:], xin[j][0:32, :], start=True, stop=True)], rd=[bx[j], B_const], wr=[pb])
                    S_.op("dve", lambda e: e.tensor_tensor(out=t1[:], in0=xin[j][0:32, :], in1=cs_[j][:, 0, :], op=ALU.mult), rd=[bx[j], bcs[j]], wr=[bt])
                    S_.op("dve", lambda e: e.tensor_tensor(out=t2[:], in0=ps[0:32, 0:TT], in1=cs_[j][:, 1, :], op=ALU.mult), rd=[pb, bcs[j]], wr=[bt])
                    S_.op("pool", lambda e: e.tensor_tensor(out=xin[j][0:32, :], in0=t1[:], in1=t2[:], op=ALU.add), rd=[bt], wr=[bx[j]])
                    S_.op("act", lambda e: e.activation(out=xb[j][:], in_=xin[j][:], func=AF.Copy), rd=[bx[j]], wr=[bxb[j]])
                    S_.dma_st(BQK[idx, :, tsl], xb[j][:], rd=[bxb[j]])
            for tt in range(NT):
                j = k % 2; k += 1
                tsl = slice(tt * TT, (tt + 1) * TT)
                S_.dma(v0[j][:], PROJ[o_bv + 2 * h, :, tsl], wr=[bv[j]])
                S_.dma(v1_[j][:], PROJ[o_bv + 2 * h + 1, :, tsl], wr=[bv[j]])
                for b in range(NB):
                    ps, pb = next_ps()
                    S_.pe_group([lambda e: e.matmul(ps[:, 0:128], v0[j][:, b * 128:(b + 1) * 128], ident_f[:], start=True, stop=True),
                                 lambda e: e.matmul(ps[:, 128:256], v1_[j][:, b * 128:(b + 1) * 128], ident_f[:], start=True, stop=True)],
                                rd=[bv[j], B_const], wr=[pb])
                    S_.op(("act", "dve")[b % 2] if False else "dve", lambda e: e.tensor_copy(out=V1[:, tt * NB + b, 0:256], in_=ps[:, 0:256]), rd=[pb], wr=[bV1])
            S_.barrier()
            S_.dma(KT[0][:], BQK[2], wr=[bKT])
            S_.dma(KT[1][:], BQK[3], wr=[bKT])
            for qb in range(NT):
                tsl = slice(qb * TT, (qb + 1) * TT)
                for c in range(2):
                    jq = k % 2; k += 1
                    S_.dma(QT[jq][:], BQK[c, :, tsl], wr=[bQT[jq]])
                    sts = {}
                    LA = 2

                    def issue_st(kb2):
                        ps2, pb2 = next_ps_lo()
                        S_.pe_group([lambda e: e.matmul(ps2[:, 0:TT], KT[c][:, kb2 * 128:(kb2 + 1) * 128], QT[jq][:], start=True, stop=True)],
                                    rd=[bKT, bQT[jq]], wr=[pb2])
                        sts[kb2] = (ps2, pb2)
                    for kb in range(min(LA, NB128)):
                        issue_st(kb)
                    for kb in range(NB128):
                        if kb + LA < NB128:
                            issue_st(kb + LA)
                        ps, pb = sts.pop(kb)
                        ip = kb % 3
                        S_.op("act", lambda e: e.activation(out=pT[ip][:], in_=ps[:, 0:TT], func=AF.Exp, scale=scale), rd=[pb], wr=[bpT[ip]])
                        S_.pe_group([(lambda e, qs=qs: e.matmul(psum[4 + qs][:, 0:257], pT[ip][:, qs * 128:(qs + 1) * 128], V1[:, kb, :],
                                                                start=(kb == 0), stop=(kb == NB128 - 1))) for qs in range(NB)],
                                    rd=[bpT[ip], bV1], wr=[PBacc, PB[4], PB[5], PB[6], PB[7]])
                    for qs in range(NB):
                        S_.op(("dve", "act")[qs % 2], (lambda e, qs=qs: e.tensor_copy(out=Oc[c][:, qs, :], in_=psum[4 + qs][:, 0:257])) if qs % 2 == 0 else
                              (lambda e, qs=qs: e.activation(out=Oc[c][:, qs, :], in_=psum[4 + qs][:, 0:257], func=AF.Copy)), rd=[PBacc], wr=[bOc[c]])
                S_.op("dve", lambda e: e.reciprocal(out=rz[:, :, 0:1], in_=Oc[0][:, :, 256:257]), rd=[bOc[0]], wr=[bo])
                S_.op("dve", lambda e: e.reciprocal(out=rz[:, :, 1:2], in_=Oc[1][:, :, 256:257]), rd=[bOc[1]], wr=[bo])
                S_.op("dve", lambda e: e.tensor_scalar(out=rz[:, :, 1:2], in0=rz[:, :, 1:2], scalar1=lamb[:, 0:1], scalar2=None, op0=ALU.mult), rd=[bo, bl], wr=[bo])
                S_.op("dve", lambda e: e.tensor_tensor(out=on[:], in0=Oc[0][:, :, 0:256], in1=rz[:, :, 0:1].broadcast_to([128, NB, 256]), op=ALU.mult), rd=[bOc[0], bo], wr=[bo])
                S_.op("pool", lambda e: e.tensor_tensor(out=sq[:], in0=Oc[1][:, :, 0:256], in1=rz[:, :, 1:2].broadcast_to([128, NB, 256]), op=ALU.mult), rd=[bOc[1], bo], wr=[bo])
                S_.op("dve", lambda e: e.tensor_tensor(out=on[:], in0=on[:], in1=sq[:], op=ALU.subtract), rd=[bo], wr=[bo])
                S_.op("pool", lambda e: e.tensor_tensor(out=sq[:], in0=on[:], in1=on[:], op=ALU.mult), rd=[bo], wr=[bo])
                S_.op("dve", lambda e: e.tensor_reduce(out=ss[:], in_=sq[:], axis=mybir.AxisListType.X, op=ALU.add), rd=[bo], wr=[bo])
                S_.op("dve", lambda e: e.tensor_scalar(out=ss[:], in0=ss[:], scalar1=1.0 / 256, scalar2=EPS, op0=ALU.mult, op1=ALU.add), rd=[bo], wr=[bo])
                rsqrt(ss[:], bo)
                S_.op("dve", lambda e: e.tensor_tensor(out=on[:], in0=on[:], in1=ss[:].unsqueeze(2).broadcast_to([128, NB, 256]), op=ALU.mult), rd=[bo], wr=[bo])
                for half in range(2):
                    ps, pb = next_ps_lo()
                    S_.pe_group([(lambda e, qs=qs: e.matmul(ps[:, qs * 128:(qs + 1) * 128], on[:, qs, half * 128:(half + 1) * 128], ident_f[:], start=True, stop=True))
                                 for qs in range(NB)], rd=[bo, B_const], wr=[pb])
                    jy = k % 2; k += 1
                    S_.op("dve", lambda e: e.tensor_scalar(out=yo[jy][:], in0=ps[:, 0:TT], scalar1=dag[:, half:half + 1], scalar2=None, op0=ALU.mult), rd=[pb, bl], wr=[byo[jy]])
                    S_.dma_st(YT[H + 2 * h + half, :, tsl], yo[jy][:], rd=[byo[jy]])
            S_.barrier()
        es.close()

    def stage_c(l):
        es, f = stage_sb()
        G = H
        sgw_f = f("csgwf", [128, G * 128]); sgw = f("csgw", [128, G * 128], BF16); sgbs = f("csgbs", [128, G * 128])
        sgg = f("csgg", [128, HGW]); sgb = f("csgb", [128, HGW])
        bc_ = Buf()
        S_.dma(sgw_f[:], sgw_in[l], wr=[bc_]); S_.dma(sgbs[:], sgbs_in[l], wr=[bc_])
        S_.dma(sgg[:], sgg_in[l], wr=[bc_]); S_.dma(sgb[:], sgb_in[l], wr=[bc_])
        S_.op("dve", lambda e: e.tensor_copy(out=sgw[:], in_=sgw_f[:]), rd=[bc_], wr=[bc_])
        cu = [f(f"ccu{i}", [128, G, 128]) for i in range(2)]; cv = [f(f"ccv{i}", [128, G, 128]) for i in range(2)]
        bcu = [Buf(), Buf()]; bcv = [Buf(), Buf()]
        gv = f("cgv", [128, HGW]); xc = f("cxc", [128, HGW]); st = f("cst", [128, 2]); vt = f("cvt", [128, HGW], BF16)
        bg_ = Buf(); bvt = Buf()
        tmp = f("ctmp", [128, G, 128]); yo = [f(f"cyo{i}", [128, G, 128], BF16) for i in range(2)]; btmp = Buf(); byo = [Buf(), Buf()]
        for blk in range(NB128):
            j = blk % 2
            tsl = slice(blk * 128, (blk + 1) * 128)
            S_.dma(cu[j][:], PROJ[o_cu:o_cu + G, :, tsl].rearrange("g p t -> p g t"), wr=[bcu[j]])
            S_.dma(cv[j][:], PROJ[o_cv:o_cv + G, :, tsl].rearrange("g p t -> p g t"), wr=[bcv[j]])
            S_.op("act", lambda e: e.activation(out=cu[j][:], in_=cu[j][:], func=AF.Gelu_apprx_tanh), rd=[bcu[j]], wr=[bcu[j]])
            nbk = (G * 128 + 511) // 512
            pss = [next_ps() for _ in range(nbk)]
            for bk in range(nbk):
                gs = range(bk * 4, min(G, bk * 4 + 4))
                S_.pe_group([(lambda e, g=g: e.matmul(pss[bk][0][:, (g % 4) * 128:(g % 4 + 1) * 128], cv[j][:, g, :], ident_f[:], start=True, stop=True)) for g in gs],
                            rd=[bcv[j], B_const], wr=[pss[bk][1]])
                w = len(gs) * 128
                S_.op("act", lambda e: e.activation(out=gv[:, bk * 512:bk * 512 + w], in_=pss[bk][0][:, 0:w], func=AF.Gelu_apprx_tanh), rd=[pss[bk][1]], wr=[bg_])
            S_.op("dve", lambda e: e.tensor_reduce(out=st[:, 0:1], in_=gv[:], axis=mybir.AxisListType.X, op=ALU.add), rd=[bg_], wr=[bg_])
            S_.op("dve", lambda e: e.tensor_scalar(out=st[:, 0:1], in0=st[:, 0:1], scalar1=1.0 / HGW, scalar2=None, op0=ALU.mult), rd=[bg_], wr=[bg_])
            S_.op("dve", lambda e: e.tensor_scalar(out=xc[:], in0=gv[:], scalar1=st[:, 0:1], scalar2=None, op0=ALU.subtract), rd=[bg_], wr=[bg_])
            S_.op("pool", lambda e: e.tensor_tensor(out=gv[:], in0=xc[:], in1=xc[:], op=ALU.mult), rd=[bg_], wr=[bg_])
            S_.op("dve", lambda e: e.tensor_reduce(out=st[:, 1:2], in_=gv[:], axis=mybir.AxisListType.X, op=ALU.add), rd=[bg_], wr=[bg_])
            S_.op("dve", lambda e: e.tensor_scalar(out=st[:, 1:2], in0=st[:, 1:2], scalar1=1.0 / HGW, scalar2=EPS, op0=ALU.mult, op1=ALU.add), rd=[bg_], wr=[bg_])
            rsqrt(st[:, 1:2], bg_)
            S_.op("dve", lambda e: e.scalar_tensor_tensor(out=xc[:], in0=xc[:], scalar=st[:, 1:2], in1=sgg[:], op0=ALU.mult, op1=ALU.mult), rd=[bg_, bc_], wr=[bg_])
            S_.op("pool", lambda e: e.tensor_tensor(out=vt[:], in0=xc[:], in1=sgb[:], op=ALU.add), rd=[bg_, bc_], wr=[bvt])
            pss = [next_ps() for _ in range(nbk)]
            for bk in range(nbk):
                gs = range(bk * 4, min(G, bk * 4 + 4))
                S_.pe_group([(lambda e, g=g: e.matmul(pss[bk][0][:, (g % 4) * 128:(g % 4 + 1) * 128], vt[:, g * 128:(g + 1) * 128], sgw[:, g * 128:(g + 1) * 128], start=True, stop=True)) for g in gs],
                            rd=[bvt, bc_], wr=[pss[bk][1]])
                w = len(gs) * 128
                g0 = bk * 4
                S_.op("dve", lambda e: e.tensor_tensor(out=tmp[:, g0:g0 + len(gs), :], in0=pss[bk][0][:, 0:w].rearrange("p (g t) -> p g t", t=128),
                                                       in1=sgbs[:, g0 * 128:g0 * 128 + w].rearrange("p (g t) -> p g t", t=128), op=ALU.add), rd=[pss[bk][1], bc_], wr=[btmp])
            S_.op("pool", lambda e: e.tensor_tensor(out=yo[j][:], in0=tmp[:], in1=cu[j][:], op=ALU.mult), rd=[btmp, bcu[j]], wr=[byo[j]])
            S_.dma_st(YT[3 * H:4 * H, :, tsl].rearrange("g p t -> p g t"), yo[j][:], rd=[byo[j]])
        S_.barrier()
        es.close()

    RS = dscr("RS", [KC, 128, T])
    RSB = [Buf(f"rs{m}") for m in range(KC)]

    class LNStream:
        def __init__(self, f, g_in, b_in):
            self.xin = [f(f"lx{i}", [128, TM]) for i in range(3)]; self.bx = [Buf() for _ in range(3)]
            self.sq = [f(f"lq{i}", [128, TM]) for i in range(2)]; self.bq = [Buf(), Buf()]
            self.ob = [f(f"lo{i}", [128, TM], BF16) for i in range(2)]; self.bo = [Buf(), Buf()]
            self.mean = f("lmean", [128, TM]); self.rstd = f("lrstd", [128, TM]); self.msq = f("lmsq", [128, TM]); self.bstat = Buf()
            self.g = f("lg", [128, KC]); self.b = f("lb", [128, KC]); self.bgb = Buf()
            S_.dma(self.g[:], g_in, wr=[self.bgb]); S_.dma(self.b[:], b_in, wr=[self.bgb])
            self.k = 0

        def add(self, m, ps, pb, tsl, first, last, from_rs):
            i = self.k % 3; self.k += 1
            if from_rs:
                S_.dma(self.xin[i][:], RS[m, :, tsl], rd=[RSB[m]], wr=[self.bx[i]])
                S_.op("dve", lambda e: e.tensor_tensor(out=self.xin[i][:], in0=self.xin[i][:], in1=ps[:, 0:TM], op=ALU.add), rd=[pb, self.bx[i]], wr=[self.bx[i]])
            else:
                S_.dma(self.xin[i][:], XT[m, :, tsl], wr=[self.bx[i]])
                S_.op("dve", lambda e: e.scalar_tensor_tensor(out=self.xin[i][:], in0=self.xin[i][:], scalar=ALPHA, in1=ps[:, 0:TM], op0=ALU.mult, op1=ALU.add),
                      rd=[pb, self.bx[i]], wr=[self.bx[i]])
            if last:
                q = m % 2
                S_.op("pool", lambda e: e.tensor_tensor(out=self.sq[q][:], in0=self.xin[i][:], in1=self.xin[i][:], op=ALU.mult), rd=[self.bx[i]], wr=[self.bq[q]])
                S_.pe_group([lambda e: e.matmul(psum[6][:, 0:TM], ones_f[:], self.xin[i][:], start=(m == 0), stop=(m == KC - 1))], rd=[self.bx[i], B_const], wr=[PB[6]])
                S_.pe_group([lambda e: e.matmul(psum[7][:, 0:TM], ones_f[:], self.sq[q][:], start=(m == 0), stop=(m == KC - 1))], rd=[self.bq[q], B_const], wr=[PB[7]])
            S_.dma_st(RS[m, :, tsl], self.xin[i][:], rd=[self.bx[i]], wr=[RSB[m]])

        def finish(self, tsl, final):
            mean, rstd, msq, bstat = self.mean, self.rstd, self.msq, self.bstat
            S_.op("dve", lambda e: e.tensor_scalar(out=mean[:], in0=psum[6][:, 0:TM], scalar1=1.0 / D, scalar2=None, op0=ALU.mult), rd=[PB[6]], wr=[bstat])
            S_.op("dve", lambda e: e.tensor_scalar(out=rstd[:], in0=psum[7][:, 0:TM], scalar1=1.0 / D, scalar2=EPS, op0=ALU.mult, op1=ALU.add), rd=[PB[7]], wr=[bstat])
            S_.op("pool", lambda e: e.tensor_tensor(out=msq[:], in0=mean[:], in1=mean[:], op=ALU.mult), rd=[bstat], wr=[bstat])
            S_.op("dve", lambda e: e.tensor_tensor(out=rstd[:], in0=rstd[:], in1=msq[:], op=ALU.subtract), rd=[bstat], wr=[bstat])
            rsqrt(rstd[:], bstat)
            for m in range(KC):
                i = self.k % 3; self.k += 1
                S_.dma(self.xin[i][:], RS[m, :, tsl], rd=[RSB[m]], wr=[self.bx[i]])
                S_.op("dve", lambda e: e.tensor_tensor(out=self.xin[i][:], in0=self.xin[i][:], in1=mean[:], op=ALU.subtract), rd=[self.bx[i], bstat], wr=[self.bx[i]])
                S_.op("pool", lambda e: e.tensor_tensor(out=self.xin[i][:], in0=self.xin[i][:], in1=rstd[:], op=ALU.mult), rd=[self.bx[i], bstat], wr=[self.bx[i]])
                S_.op("dve", lambda e: e.tensor_scalar(out=self.xin[i][:], in0=self.xin[i][:], scalar1=self.g[:, m:m + 1], scalar2=self.b[:, m:m + 1], op0=ALU.mult, op1=ALU.add),
                      rd=[self.bx[i], self.bgb], wr=[self.bx[i]])
                if final:
                    S_.dma_st(yT_out[m, :, tsl], self.xin[i][:], rd=[self.bx[i]])
                else:
                    S_.dma_st(XT[m, :, tsl], self.xin[i][:], rd=[self.bx[i]])
                    q = m % 2
                    S_.op("act", lambda e: e.activation(out=self.ob[q][:], in_=self.xin[i][:], func=AF.Copy), rd=[self.bx[i]], wr=[self.bo[q]])
                    S_.dma_st(HT[m, :, tsl], self.ob[q][:], rd=[self.bo[q]])

    def stage_merge(l):
        es, f = stage_sb()
        ps_ring[0] = 6
        y = f("my", [128, KC, TM], BF16); by = Buf()
        mg = f("mmg", [128, KC, TM], BF16); bmg = Buf()
        gt = [f(f"mgt{i}", [128, 3, TM], BF16) for i in range(2)]; bgt = [Buf(), Buf()]
        t3 = [f(f"mt{i}", [128, TM]) for i in range(3)]; bt3 = [Buf() for _ in range(3)]
        ln = LNStream(f, ln1g_in[l], ln1b_in[l])
        wr_ = WRing(f, max(KC, 2 * H), 4)
        for tt in range(NTM):
            tsl = slice(tt * TM, (tt + 1) * TM)
            S_.dma(y[:], YT[:, :, tsl].rearrange("k p t -> p k t"), wr=[by])
            for i in range(KC):
                jg = i % 2
                S_.dma(gt[jg][:], GATE[:, :, tsl].rearrange("(c k) p t -> k p c t", k=KC)[i], wr=[bgt[jg]])
                for bi, (nm, koff, kn) in enumerate((("ba", 0, H), ("bb", H, 2 * H), ("bc", 3 * H, H))):
                    wt, wb = wr_.load(Wb[nm][l][i])
                    ps, pb = next_ps()
                    S_.pe_group([(lambda e, k=k: e.matmul(ps[:, 0:TM], wt[:, k * 128:(k + 1) * 128], y[:, koff + k, :], start=(k == 0), stop=(k == kn - 1))) for k in range(kn)],
                                rd=[wb, by], wr=[pb])
                    S_.op("dve", lambda e: e.tensor_tensor(out=t3[bi][:], in0=ps[:, 0:TM], in1=gt[jg][:, bi, :], op=ALU.mult), rd=[pb, bgt[jg]], wr=[bt3[bi]])
                S_.op("pool", lambda e: e.tensor_tensor(out=t3[0][:], in0=t3[0][:], in1=t3[1][:], op=ALU.add), rd=[bt3[0], bt3[1]], wr=[bt3[0]])
                S_.op("pool", lambda e: e.tensor_tensor(out=mg[:, i, :], in0=t3[0][:], in1=t3[2][:], op=ALU.add), rd=[bt3[0], bt3[2]], wr=[bmg])

            def epi(m, ps, pb):
                ln.add(m, ps, pb, tsl, True, True, False)
            gemm_tile(wr_, [Wb["out"][l][m] for m in range(KC)], lambda k: mg[:, k, :], [bmg], KC, TM, epi)
            ln.finish(tsl, False)
        S_.barrier()
        ps_ring[0] = 8
        es.close()

    def stage_ffn(l, final):
        es, f = stage_sb()
        ps_ring[0] = 6
        NH = 2 if FC >= 64 else 1
        FH = FC // NH
        x1 = f("fx1", [128, KC, TM], BF16); bx1 = Buf()
        hid = f("fhid", [128, FH, TM], BF16); bh = Buf()
        rl = [f(f"frl{i}", [128, TM]) for i in range(2)]; brl = [Buf(), Buf()]
        ln = LNStream(f, ln2g_in[l], ln2b_in[l])
        KU = min(FH, 32)
        wr_ = WRing(f, max(KC, KU), 4)
        cnt = [0]
        for tt in range(NTM):
            tsl = slice(tt * TM, (tt + 1) * TM)
            S_.dma(x1[:], HT[:, :, tsl].rearrange("k p t -> p k t"), wr=[bx1])
            for hf in range(NH):
                def epi_up(m, ps, pb):
                    i = cnt[0] % 2; cnt[0] += 1
                    S_.op("act", lambda e: e.activation(out=rl[i][:], in_=ps[:, 0:TM], func=AF.Relu), rd=[pb], wr=[brl[i]])
                    S_.op("pool", lambda e: e.tensor_tensor(out=hid[:, m, :], in0=rl[i][:], in1=rl[i][:], op=ALU.mult), rd=[brl[i]], wr=[bh])
                gemm_tile(wr_, [Wb["up"][l][hf * FH + m] for m in range(FH)], lambda k: x1[:, k, :], [bx1], KC, TM, epi_up)
                for m in range(KC):
                    ps, pb = next_ps()
                    nsu = FH // KU
                    for su in range(nsu):
                        c0 = (hf * FH + su * KU) * 128
                        wt, wb = wr_.load(Wb["down"][l][m][:, c0:c0 + KU * 128])
                        S_.pe_group([(lambda e, k=k: e.matmul(ps[:, 0:TM], wt[:, k * 128:(k + 1) * 128], hid[:, su * KU + k, :],
                                                              start=(su == 0 and k == 0), stop=(su == nsu - 1 and k == KU - 1))) for k in range(KU)],
                                    rd=[wb, bh], wr=[pb])
                    ln.add(m, ps, pb, tsl, hf == 0, hf == NH - 1, hf > 0)
            ln.finish(tsl, final)
        S_.barrier()
        ps_ring[0] = 8
        es.close()

    stop_after = cfg.get("stop_after")
    stage_x0()
    for nm, src in (("in", w_in_f), ("gate", w_gate_f), ("ba", w_ba_f), ("bb", w_bb_f), ("bc", w_bc_f),
                    ("out", w_out_f), ("up", w_up_f), ("down", w_down_f)):
        for l_ in range(DEPTH):
            if "cast" not in cfg.get("skip", ()): stage_cast(src[l_], Wb[nm][l_])
    dbg = {}
    skip = cfg.get("skip", ())
    for l in range(DEPTH):
        if "proj" not in skip: stage_proj(l)
        if "a_prep" not in skip: stage_a_prep(l)
        if "a_rec" not in skip: stage_a_rec(l)
        if "a_norm" not in skip: stage_a_norm(l)
        if "b" not in skip: stage_b(l)
        if "c" not in skip: stage_c(l)
        if "merge" not in skip: stage_merge(l)
        stage_ffn(l, final=(l == DEPTH - 1))
    for nm in cfg.get("dump", ()):
        src = {"PROJ": PJA, "YT": YT, "XT": XT, "GATE": GATE, "OA": OA, "HT": HT}[nm]
        o = nc.dram_tensor("dump_" + nm, list(src.shape), src.dtype, kind="ExternalOutput").ap()
        S_.dma(o, src)
    S_.barrier()
    ES.close()
    return nc


def tile_w(W):
    K, M = W.shape
    return np.ascontiguousarray(W.reshape(K // 128, 128, M // 128, 128).transpose(2, 1, 0, 3)).reshape(M // 128, 128, (K // 128) * 128)


def host_inputs(cfg, inp, b):
    D, S, DEPTH, KC, H = cfg["D"], cfg["S"], cfg["DEPTH"], cfg["KC"], cfg["H"]
    HGW = cfg["HGW"]
    f32 = np.float32
    m = {}
    m["xT"] = np.ascontiguousarray(inp["x"][b].T).reshape(KC, 128, S)
    for nm, key in (("w_in_t", "w_in"), ("w_gate_t", "w_gate"), ("w_ba_t", "w_branch_a"), ("w_bb_t", "w_branch_b"),
                    ("w_bc_t", "w_branch_c"), ("w_out_t", "w_out"), ("w_up_t", "w_up"), ("w_down_t", "w_down")):
        m[nm] = np.stack([tile_w(np.asarray(inp[key][l])) for l in range(DEPTH)])
    pk = lambda v, n: np.ascontiguousarray(np.asarray(v).reshape(DEPTH, n, 128).transpose(0, 2, 1))
    m["b_gate_p"] = pk(inp["b_gate"], 3 * KC)
    m["ln1_g_p"] = pk(inp["ln1_g"], KC); m["ln1_b_p"] = pk(inp["ln1_b"], KC)
    m["ln2_g_p"] = pk(inp["ln2_g"], KC); m["ln2_b_p"] = pk(inp["ln2_b"], KC)
    m["lbraw_p"] = np.ascontiguousarray(np.asarray(inp["hg_lb_raw"]).reshape(DEPTH, 2, H, 128).transpose(3, 0, 1, 2)).reshape(128, DEPTH * 2 * H)
    m["hgg_p"] = pk(inp["hg_norm_g"], H)
    m["dal_p"] = np.ascontiguousarray(np.asarray(inp["da_lambda"]).reshape(DEPTH, 1, 512))
    m["dag_p"] = pk(inp["da_norm_g"], 2)
    m["sgg_b"] = np.ascontiguousarray(np.broadcast_to(np.asarray(inp["sg_norm_g"])[:, None, :], (DEPTH, 128, HGW)))
    m["sgb_b"] = np.ascontiguousarray(np.broadcast_to(np.asarray(inp["sg_norm_b"])[:, None, :], (DEPTH, 128, HGW)))
    m["sgw_t"] = np.ascontiguousarray(np.asarray(inp["sg_w_s"]).transpose(0, 3, 1, 2)).reshape(DEPTH, 128, H * 128)
    m["sgbs_b"] = np.ascontiguousarray(np.broadcast_to(np.asarray(inp["sg_b_s"]).reshape(DEPTH, 1, H * 128), (DEPTH, 128, H * 128)))
    pos = np.arange(S, dtype=f32)
    freqs = (f32(ROPE_THETA) ** (-np.arange(0, 32, 2, dtype=f32) / f32(32))).astype(f32)
    ang = (pos[:, None] * freqs[None, :]).astype(f32)
    cos = np.cos(ang).astype(f32).T; sin = np.sin(ang).astype(f32).T
    m["cosF"] = np.ascontiguousarray(np.concatenate([cos, cos], 0)); m["sinF"] = np.ascontiguousarray(np.concatenate([sin, sin], 0))
    R = np.zeros((32, 32), f32)
    for i in range(16):
        R[16 + i, i] = -1.0; R[i, 16 + i] = 1.0
    m["rotm"] = R
    m["ident"] = np.eye(128, dtype=f32)
    st = np.arange(64)
    m["trif"] = (st[:, None] <= st[None, :]).astype(f32); m["trib"] = (st[:, None] >= st[None, :]).astype(f32)
    return {k: np.ascontiguousarray(v, dtype=f32) for k, v in m.items()}


def kernel(**inputs):
    cfg = make_cfg()
    nc = build_program(cfg)
    B = cfg["B"]
    base = host_inputs(cfg, inputs, 0)
    in_maps = [base]
    for b in range(1, B):
        mb = dict(base)
        mb["xT"] = np.ascontiguousarray(np.asarray(inputs["x"][b]).T, dtype=np.float32).reshape(cfg["KC"], 128, cfg["S"])
        in_maps.append(mb)
    res = run_bass_kernel_spmd(nc, in_maps, core_ids=list(range(B)))
    out = np.stack([np.ascontiguousarray(res.results[b]["yT"].reshape(cfg["D"], cfg["S"]).T) for b in range(B)])
    return out.astype(np.float32)
```
